# Optimizing a Trainium2 kernel written in Bass

```python
import math
import jax, jax.numpy as jnp
from jax import lax
import numpy as np

D_MODEL = 1024
BATCH = 8
SEQ = 8192
DEPTH = 2
DEC_BATCH = 16
DEC_SEQ = 32
PAST_LEN = 1024

CHUNK = 64
N_MIXERS = 2
N_SSD_LAYERS = (DEPTH + 1) // 2
N_MLA_LAYERS = DEPTH // 2
NORM_EPS = 1e-6

SSD_EXPAND = 2
SSD_D_INNER = SSD_EXPAND * D_MODEL
SSD_HEAD_DIM = 64
SSD_HEADS = SSD_D_INNER // SSD_HEAD_DIM
SSD_GROUPS = 8
SSD_HEADS_PER_GROUP = SSD_HEADS // SSD_GROUPS
SSD_STATE = 128
SSD_CONV = 4
SSD_GN = SSD_GROUPS * SSD_STATE
SSD_CONV_DIM = SSD_D_INNER + 2 * SSD_GN
SSD_IN_DIM = SSD_D_INNER + SSD_CONV_DIM + SSD_HEADS

MLA_HEADS = 16
MLA_Q_LORA = 512
MLA_KV_LORA = 256
MLA_NOPE = 64
MLA_ROPE = 32
MLA_V = 64
MLA_SCALE = 1.0 / math.sqrt(MLA_NOPE + MLA_ROPE)
ROPE_THETA = 10000.0
Q_BLOCK = 128

FFN_HIDDEN = ((8 * D_MODEL + 3 * 256 - 1) // (3 * 256)) * 256

kernel_name = 'ssd_mla_sandwich_stream_step'


def _rmsnorm(x, g):
    xf = x.astype(jnp.float32)
    r = lax.rsqrt(jnp.mean(xf * xf, axis=-1, keepdims=True) + NORM_EPS)
    return (xf * r).astype(x.dtype) * g


def _rope(t, pos):
    half = MLA_ROPE // 2
    inv = ROPE_THETA ** (-jnp.arange(half, dtype=jnp.float32) / half)
    ang = pos.astype(jnp.float32)[:, None] * inv[None, :]
    cos = jnp.cos(ang)[None, :, None, :]
    sin = jnp.sin(ang)[None, :, None, :]
    tf = t.astype(jnp.float32)
    t1, t2 = tf[..., :half], tf[..., half:]
    return jnp.concatenate([t1 * cos - t2 * sin, t1 * sin + t2 * cos], axis=-1).astype(t.dtype)


def _ssd_scan(x, dt, a, bm, cm, s0):
    b, l = x.shape[:2]
    pad = (-l) % CHUNK
    f32 = jnp.float32

    def prep(t):
        t = t.astype(f32)
        return jnp.pad(t, [(0, 0), (0, pad)] + [(0, 0)] * (t.ndim - 2))

    lp = l + pad
    nc = lp // CHUNK
    G, R, P, N = SSD_GROUPS, SSD_HEADS_PER_GROUP, SSD_HEAD_DIM, SSD_STATE

    def to_chunks(t):
        return jnp.moveaxis(t.reshape((b, nc, CHUNK) + t.shape[2:]), 1, 0)

    xs = (to_chunks(prep(x).reshape(b, lp, G, R, P)),
          to_chunks(prep(dt).reshape(b, lp, G, R)),
          to_chunks(prep(bm)),
          to_chunks(prep(cm)))
    ag = a.astype(f32).reshape(G, R)
    causal = jnp.tril(jnp.ones((CHUNK, CHUNK), dtype=bool))[None, :, :, None, None]

    def step(state, inp):
        xc, dtc, bc, cc = inp
        acum = jnp.cumsum(dtc * ag, axis=1)
        seg = jnp.where(causal, acum[:, :, None] - acum[:, None, :], -jnp.inf)
        decay = jnp.exp(seg)
        cb = jnp.einsum('bqgn,bsgn->bqsg', cc, bc)
        y_in = jnp.einsum('bqsg,bqsgr,bsgr,bsgrp->bqgrp', cb, decay, dtc, xc)
        y_state = jnp.einsum('bqgn,bgrpn,bqgr->bqgrp', cc, state, jnp.exp(acum))
        w_end = jnp.exp(acum[:, -1:] - acum) * dtc
        new_state = (jnp.exp(acum[:, -1])[..., None, None] * state
                     + jnp.einsum('bsgn,bsgr,bsgrp->bgrpn', bc, w_end, xc))
        return new_state, y_in + y_state

    s_init = s0.astype(f32).reshape(b, G, R, P, N)
    s_fin, ys = lax.scan(step, s_init, xs)
    y = jnp.moveaxis(ys, 0, 1).reshape(b, lp, SSD_HEADS, P)[:, :l]
    return y.astype(x.dtype), s_fin.reshape(b, SSD_HEADS, P, N).astype(s0.dtype)


def _ssd_mixer(a, conv_buf, ssm_state, w_in, conv_w, conv_b, dt_bias, a_log, d_skip, gate_norm, w_out):
    b, l, _ = a.shape
    proj = a @ w_in
    z = proj[..., :SSD_D_INNER]
    xbc = proj[..., SSD_D_INNER:SSD_D_INNER + SSD_CONV_DIM]
    dt_raw = proj[..., SSD_D_INNER + SSD_CONV_DIM:]
    xp = jnp.concatenate([conv_buf.astype(xbc.dtype), xbc], axis=1)
    new_buf = xp[:, -(SSD_CONV - 1):]
    conv = lax.conv_general_dilated(xp, conv_w[:, None, :].astype(xp.dtype), (1,), 'VALID',
                                    dimension_numbers=('NWC', 'WIO', 'NWC'),
                                    feature_group_count=SSD_CONV_DIM)
    xbc = jax.nn.silu(conv + conv_b)
    xs = xbc[..., :SSD_D_INNER].reshape(b, l, SSD_HEADS, SSD_HEAD_DIM)
    bm = xbc[..., SSD_D_INNER:SSD_D_INNER + SSD_GN].reshape(b, l, SSD_GROUPS, SSD_STATE)
    cm = xbc[..., SSD_D_INNER + SSD_GN:].reshape(b, l, SSD_GROUPS, SSD_STATE)
    dt = jax.nn.softplus((dt_raw + dt_bias).astype(jnp.float32))
    y, new_state = _ssd_scan(xs, dt, -jnp.exp(a_log.astype(jnp.float32)), bm, cm, ssm_state)
    y = y + d_skip[:, None] * xs
    y = (y.reshape(b, l, SSD_D_INNER) * jax.nn.silu(z)).reshape(b, l, SSD_GROUPS, SSD_D_INNER // SSD_GROUPS)
    y = _rmsnorm(y, gate_norm.reshape(SSD_GROUPS, -1)).reshape(b, l, SSD_D_INNER)
    return y @ w_out, new_buf, new_state


def _mla_attend(qn, qr, qpos, k_lat, k_rope, kpos, w_uk, w_uv):
    q_lat = jnp.einsum('bqhd,chd->bqhc', qn, w_uk)
    s = jnp.einsum('bqhc,bkc->bhqk', q_lat, k_lat) + jnp.einsum('bqhr,bkr->bhqk', qr, k_rope)
    s = s.astype(jnp.float32) * MLA_SCALE
    visible = (kpos[None, :] // CHUNK) <= (qpos[:, None] // CHUNK)
    p = jax.nn.softmax(jnp.where(visible, s, -jnp.inf), axis=-1).astype(k_lat.dtype)
    o_lat = jnp.einsum('bhqk,bkc->bqhc', p, k_lat)
    return jnp.einsum('bqhc,chv->bqhv', o_lat, w_uv)


def _mla_mixer(a, pos, past_lat, past_rope, past_pos, wq_a, q_norm, wq_b, wkv_a, kv_norm, w_uk, w_uv, w_o):
    b, l, _ = a.shape
    q = (_rmsnorm(a @ wq_a, q_norm) @ wq_b).reshape(b, l, MLA_HEADS, MLA_NOPE + MLA_ROPE)
    qn = q[..., :MLA_NOPE]
    qr = _rope(q[..., MLA_NOPE:], pos)
    kv = a @ wkv_a
    lat = _rmsnorm(kv[..., :MLA_KV_LORA], kv_norm)
    kr = _rope(kv[..., None, MLA_KV_LORA:], pos)[:, :, 0]
    k_lat = jnp.concatenate([past_lat.astype(lat.dtype), lat], axis=1)
    k_rope = jnp.concatenate([past_rope.astype(kr.dtype), kr], axis=1)
    kpos = jnp.concatenate([past_pos, pos])
    nblk = -(-l // Q_BLOCK)
    if nblk == 1:
        o = _mla_attend(qn, qr, pos, k_lat, k_rope, kpos, w_uk, w_uv)
    else:
        def blocks(t):
            return jnp.moveaxis(t.reshape((b, nblk, Q_BLOCK) + t.shape[2:]), 1, 0)
        ob = lax.map(lambda blk: _mla_attend(blk[0], blk[1], blk[2], k_lat, k_rope, kpos, w_uk, w_uv),
                     (blocks(qn), blocks(qr), pos.reshape(nblk, Q_BLOCK)))
        o = jnp.moveaxis(ob, 0, 1).reshape(b, l, MLA_HEADS, MLA_V)
    return o.reshape(b, l, MLA_HEADS * MLA_V) @ w_o, lat, kr


def _swiglu(a, w_gate, w_up, w_down):
    return (jax.nn.silu(a @ w_gate) * (a @ w_up)) @ w_down


def _trunk(x, pos, conv_bufs, ssm_states, past_lat, past_rope, past_pos, p):
    h = x
    new_conv, new_ssm, new_lat, new_rope = [], [], [], []
    for i in range(DEPTH):
        j = i // N_MIXERS
        a = _rmsnorm(h, p['ln_mix_pre'][i])
        if i % N_MIXERS == 0:
            m, cb, st = _ssd_mixer(a, conv_bufs[j], ssm_states[j], p['ssd_w_in'][j], p['ssd_conv_w'][j],
                                   p['ssd_conv_b'][j], p['ssd_dt_bias'][j], p['ssd_a_log'][j], p['ssd_d'][j],
                                   p['ssd_gate_norm'][j], p['ssd_w_out'][j])
            new_conv.append(cb)
            new_ssm.append(st)
        else:
            m, lat, kr = _mla_mixer(a, pos, past_lat[j], past_rope[j], past_pos, p['mla_wq_a'][j],
                                    p['mla_q_norm'][j], p['mla_wq_b'][j], p['mla_wkv_a'][j], p['mla_kv_norm'][j],
                                    p['mla_w_uk'][j], p['mla_w_uv'][j], p['mla_w_o'][j])
            new_lat.append(lat)
            new_rope.append(kr)
        h = h + _rmsnorm(m, p['ln_mix_post'][i])
        f = _swiglu(_rmsnorm(h, p['ln_ffn_pre'][i]), p['ffn_w_gate'][i], p['ffn_w_up'][i], p['ffn_w_down'][i])
        h = h + _rmsnorm(f, p['ln_ffn_post'][i])
    return h, jnp.stack(new_conv), jnp.stack(new_ssm), jnp.stack(new_lat), jnp.stack(new_rope)


def setup_inputs(seed: int = 0) -> dict:
    key = jax.random.key(seed)
    ks = iter(jax.random.split(key, 40))
    f32 = jnp.float32

    def nrm(shape, scale):
        return jax.random.normal(next(ks), shape, f32) * scale

    def gain(shape):
        return 1.0 + 0.05 * jax.random.normal(next(ks), shape, f32)

    u = jax.random.uniform(next(ks), (N_SSD_LAYERS, SSD_HEADS), f32)
    dt0 = jnp.exp(u * (math.log(0.1) - math.log(0.001)) + math.log(0.001))
    dt_bias = dt0 + jnp.log(-jnp.expm1(-dt0))
    a_log = jnp.log(jax.random.uniform(next(ks), (N_SSD_LAYERS, SSD_HEADS), f32, 1.0, 16.0))
    return {
        'x_prompt': nrm((BATCH, SEQ, D_MODEL), 1.0),
        'x_sample': nrm((DEC_BATCH, DEC_SEQ, D_MODEL), 1.0),
        'state_ssd_conv': nrm((N_SSD_LAYERS, DEC_BATCH, SSD_CONV - 1, SSD_CONV_DIM), 1.0),
        'state_ssd_ssm': nrm((N_SSD_LAYERS, DEC_BATCH, SSD_HEADS, SSD_HEAD_DIM, SSD_STATE), 0.1),
        'cache_mla_latent': nrm((N_MLA_LAYERS, DEC_BATCH, PAST_LEN, MLA_KV_LORA), 1.0),
        'cache_mla_krope': nrm((N_MLA_LAYERS, DEC_BATCH, PAST_LEN, MLA_ROPE), 1.0),
        'ln_mix_pre': gain((DEPTH, D_MODEL)),
        'ln_mix_post': gain((DEPTH, D_MODEL)),
        'ln_ffn_pre': gain((DEPTH, D_MODEL)),
        'ln_ffn_post': gain((DEPTH, D_MODEL)),
        'ssd_w_in': nrm((N_SSD_LAYERS, D_MODEL, SSD_IN_DIM), D_MODEL ** -0.5),
        'ssd_conv_w': nrm((N_SSD_LAYERS, SSD_CONV, SSD_CONV_DIM), SSD_CONV ** -0.5),
        'ssd_conv_b': nrm((N_SSD_LAYERS, SSD_CONV_DIM), 0.01),
        'ssd_dt_bias': dt_bias,
        'ssd_a_log': a_log,
        'ssd_d': gain((N_SSD_LAYERS, SSD_HEADS)),
        'ssd_gate_norm': gain((N_SSD_LAYERS, SSD_D_INNER)),
        'ssd_w_out': nrm((N_SSD_LAYERS, SSD_D_INNER, D_MODEL), SSD_D_INNER ** -0.5),
        'mla_wq_a': nrm((N_MLA_LAYERS, D_MODEL, MLA_Q_LORA), D_MODEL ** -0.5),
        'mla_q_norm': gain((N_MLA_LAYERS, MLA_Q_LORA)),
        'mla_wq_b': nrm((N_MLA_LAYERS, MLA_Q_LORA, MLA_HEADS * (MLA_NOPE + MLA_ROPE)), MLA_Q_LORA ** -0.5),
        'mla_wkv_a': nrm((N_MLA_LAYERS, D_MODEL, MLA_KV_LORA + MLA_ROPE), D_MODEL ** -0.5),
        'mla_kv_norm': gain((N_MLA_LAYERS, MLA_KV_LORA)),
        'mla_w_uk': nrm((N_MLA_LAYERS, MLA_KV_LORA, MLA_HEADS, MLA_NOPE), MLA_KV_LORA ** -0.5),
        'mla_w_uv': nrm((N_MLA_LAYERS, MLA_KV_LORA, MLA_HEADS, MLA_V), MLA_KV_LORA ** -0.5),
        'mla_w_o': nrm((N_MLA_LAYERS, MLA_HEADS * MLA_V, D_MODEL), (MLA_HEADS * MLA_V) ** -0.5),
        'ffn_w_gate': nrm((DEPTH, D_MODEL, FFN_HIDDEN), D_MODEL ** -0.5),
        'ffn_w_up': nrm((DEPTH, D_MODEL, FFN_HIDDEN), D_MODEL ** -0.5),
        'ffn_w_down': nrm((DEPTH, FFN_HIDDEN, D_MODEL), FFN_HIDDEN ** -0.5),
    }


def reference(x_prompt, x_sample, state_ssd_conv, state_ssd_ssm, cache_mla_latent, cache_mla_krope,
              ln_mix_pre, ln_mix_post, ln_ffn_pre, ln_ffn_post,
              ssd_w_in, ssd_conv_w, ssd_conv_b, ssd_dt_bias, ssd_a_log, ssd_d, ssd_gate_norm, ssd_w_out,
              mla_wq_a, mla_q_norm, mla_wq_b, mla_wkv_a, mla_kv_norm, mla_w_uk, mla_w_uv, mla_w_o,
              ffn_w_gate, ffn_w_up, ffn_w_down):
    p = dict(ln_mix_pre=ln_mix_pre, ln_mix_post=ln_mix_post, ln_ffn_pre=ln_ffn_pre, ln_ffn_post=ln_ffn_post,
             ssd_w_in=ssd_w_in, ssd_conv_w=ssd_conv_w, ssd_conv_b=ssd_conv_b, ssd_dt_bias=ssd_dt_bias,
             ssd_a_log=ssd_a_log, ssd_d=ssd_d, ssd_gate_norm=ssd_gate_norm, ssd_w_out=ssd_w_out,
             mla_wq_a=mla_wq_a, mla_q_norm=mla_q_norm, mla_wq_b=mla_wq_b, mla_wkv_a=mla_wkv_a,
             mla_kv_norm=mla_kv_norm, mla_w_uk=mla_w_uk, mla_w_uv=mla_w_uv, mla_w_o=mla_w_o,
             ffn_w_gate=ffn_w_gate, ffn_w_up=ffn_w_up, ffn_w_down=ffn_w_down)
    b, l = x_prompt.shape[:2]
    dt_ = x_prompt.dtype
    y_prompt, p_conv, p_ssm, p_latent, p_krope = _trunk(
        x_prompt, jnp.arange(l, dtype=jnp.int32),
        jnp.zeros((N_SSD_LAYERS, b, SSD_CONV - 1, SSD_CONV_DIM), dt_),
        jnp.zeros((N_SSD_LAYERS, b, SSD_HEADS, SSD_HEAD_DIM, SSD_STATE), dt_),
        jnp.zeros((N_MLA_LAYERS, b, 0, MLA_KV_LORA), dt_),
        jnp.zeros((N_MLA_LAYERS, b, 0, MLA_ROPE), dt_),
        jnp.zeros((0,), jnp.int32), p)
    ls = x_sample.shape[1]
    past_len = cache_mla_latent.shape[2]
    y_sample, s_conv, s_ssm, s_latent, s_krope = _trunk(
        x_sample, past_len + jnp.arange(ls, dtype=jnp.int32),
        state_ssd_conv, state_ssd_ssm, cache_mla_latent, cache_mla_krope,
        jnp.arange(past_len, dtype=jnp.int32), p)
    return (y_prompt, y_sample, p_conv, p_ssm, p_latent, p_krope, s_conv, s_ssm, s_latent, s_krope)
```

```python
import math
import numpy as np
import concourse.bass as bass
import concourse.mybir as mybir
from concourse.bass_utils import run_bass_kernel_spmd

F32 = mybir.dt.float32
BF16 = mybir.dt.bfloat16
AF = mybir.ActivationFunctionType
ALU = mybir.AluOpType
AX = mybir.AxisListType

D_MODEL = 1024
SEQ_FULL = 8192
PAST = 1024
LS = 32
TT = 256
FFN = 2816
EPS = 1e-6
SCALE = 1.0 / math.sqrt(96.0)
NSLOT = 3
WB = 4096

L0_BLOCKS = ([("xbc", 8, 512)] * 8 + [("z", 8, 512)] * 4 + [("wout", 16, 256)] * 4
             + [("gate", 8, 512)] * 6 + [("up", 8, 512)] * 6 + [("down", 22, 128)] * 8)
L1_BLOCKS = ([("wqa", 8, 512)] + [("wkva", 8, 320)] + [("wqb", 4, 1024)] * 2 + [("wo", 8, 512)] * 2
             + [("gate", 8, 512)] * 6 + [("up", 8, 512)] * 6 + [("down", 22, 128)] * 8)
NB0, NB1 = len(L0_BLOCKS), len(L1_BLOCKS)

C_LN = 0
C_CW = 64
C_CB = 192
C_QN = 224
C_KN = 228
NCOLS = 232
R_DTB, R_ALOG, R_D, R_GN = 0, 32, 64, 96
NROWS = 96 + 2048


class Sched:
    ENG = ("pe", "act", "dve", "pool", "sp")

    def __init__(self, nc, ndma=14):
        self.nc = nc
        self.ops = {e: [] for e in self.ENG}
        self.count = {e: 0 for e in self.ENG}
        self.waited = {e: {} for e in self.ENG}
        self.res = {}
        self.ndma = ndma
        self.epoch = 0
        self.nepoch = 12
        self.dma_val = {}
        self.dma_rr = {"sp": 0, "pool": 0, "act": 0}
        for q in ("sp", "pool"):
            for k in range(ndma):
                self.dma_val["d_%s_%d" % (q, k)] = 0

    def sem_names(self):
        return ["%s@%d" % (e, k) for e in self.ENG[:4] for k in range(self.nepoch)] + list(self.dma_val.keys())

    def cname(self, eng):
        return "%s@%d" % (eng, self.epoch)

    def new_epoch(self):
        self.barrier()
        self.epoch += 1
        assert self.epoch < self.nepoch
        for e in self.ENG:
            self.count[e] = 0
            self.waited[e] = {}

    def emit(self, eng, fn, reads=(), writes=(), dma=False):
        deps = []
        for r in reads:
            st = self.res.get(r)
            if st is not None and st[0] is not None:
                deps.append(st[0])
        for w in writes:
            st = self.res.get(w)
            if st is not None:
                if st[0] is not None:
                    deps.append(st[0])
                deps.extend(st[1])
        if dma:
            k = self.dma_rr[eng]
            self.dma_rr[eng] = (k + 1) % self.ndma
            sname = "d_%s_%d" % (eng, k)
            if self.dma_val[sname] > 0:
                deps.append((sname, self.dma_val[sname]))
            self.dma_val[sname] += 16
            token = (sname, self.dma_val[sname])
        else:
            self.count[eng] += 1
            token = (self.cname(eng), self.count[eng])
        waits = {}
        for (s, v) in deps:
            if s == self.cname(eng) and eng in ("pe", "sp"):
                continue
            if self.waited[eng].get(s, 0) >= v:
                continue
            if waits.get(s, 0) < v:
                waits[s] = v
        for s, v in waits.items():
            self.waited[eng][s] = v
        self.ops[eng].append((list(waits.items()), fn, token, dma))
        for w in writes:
            self.res[w] = [token, []]
        for r in reads:
            st = self.res.get(r)
            if st is None:
                self.res[r] = [None, [token]]
            else:
                st[1].append(token)
        return token

    def barrier(self):
        toks = [(self.cname(e), self.count[e]) for e in self.ENG if self.count[e] > 0]
        toks += [(s, v) for s, v in self.dma_val.items() if v > 0]
        for e in self.ENG:
            waits = []
            for (s, v) in toks:
                if s == self.cname(e):
                    continue
                if self.waited[e].get(s, 0) >= v:
                    continue
                self.waited[e][s] = v
                waits.append((s, v))
            if waits:
                self.ops[e].append((waits, None, None, False))
        self.res = {}

    def check(self, semvals):
        pos = {e: 0 for e in self.ENG}
        progress = True
        while progress:
            progress = False
            for e in self.ENG:
                while pos[e] < len(self.ops[e]):
                    waits, fn, token, dma = self.ops[e][pos[e]]
                    if any(semvals.get(s, 0) < v for s, v in waits):
                        break
                    if fn is not None:
                        semvals[token[0]] = semvals.get(token[0], 0) + (16 if dma else 1)
                        assert semvals[token[0]] == token[1], (e, pos[e], token, semvals[token[0]])
                    pos[e] += 1
                    progress = True
        stuck = {e: (pos[e], len(self.ops[e])) for e in self.ENG if pos[e] < len(self.ops[e])}
        for e in stuck:
            waits, fn, token, dma = self.ops[e][pos[e]]
            print("STUCK", e, pos[e], [(s, v, semvals.get(s, 0)) for s, v in waits if semvals.get(s, 0) < v], token)
        return not stuck

    def replay(self, eng, E, sems):
        for waits, fn, token, dma in self.ops[eng]:
            for s, v in waits:
                E.wait_ge(sems[s], v)
            if fn is None:
                continue
            ins = fn(E)
            ins.then_inc(sems[token[0]], 16 if dma else 1)


def build_program(NT):
    SEQ = NT * TT
    nc = bass.Bass("TRN2", target_bir_lowering=False)
    S = Sched(nc)

    def din(name, shape, dt=F32):
        return nc.dram_tensor(name, list(shape), dt, kind="ExternalInput").ap()

    def dout(name, shape, dt=F32):
        return nc.dram_tensor(name, list(shape), dt, kind="ExternalOutput").ap()

    xT = din("xT", [8, 128, SEQ])
    xsT = din("xsT", [8, 128, 2 * LS])
    w0 = din("w0", [NB0 * 128, WB])
    w1 = din("w1", [NB1 * 128, WB])
    cols_d = din("cols", [128, NCOLS])
    rows_d = din("rows", [128, NROWS])
    wdt_d = din("wdt", [128, 8 * 32])
    wuk_d = din("wuk", [128, 8 * 256])
    wuv_d = din("wuv", [128, 2 * 16 * 128])
    cs_d = din("cs", [32, 2, SEQ])
    css_d = din("css", [32, 2, 2 * LS])
    consts_d = din("consts", [128, 512])
    convst_d = din("convst", [2, 128, 96])
    ssmst_d = din("ssmst", [2, 128, 2048])
    clat_d = din("clat", [2, 8, 128, 256])
    clatT_d = din("clatT", [2, 2, 128, PAST])
    ckrT_d = din("ckrT", [2, 32, PAST])

    yT_o = dout("yT", [8, 128, SEQ])
    ysT_o = dout("ysT", [8, 128, 2 * LS])
    pconv_o = dout("pconv", [128, 96])
    pssm_o = dout("pssm", [128, 2048])
    platT_o = dout("platT", [2, 128, SEQ])
    pkrT_o = dout("pkrT", [32, SEQ])
    sconv_o = dout("sconv", [2, 128, 96])
    sssm_o = dout("sssm", [2, 128, 2048])
    slatT_o = dout("slatT", [2, 128, 2 * LS])
    skrT_o = dout("skrT", [32, 2 * LS])

    wb0 = nc.dram_tensor("wb0", [NB0 * 128, WB], BF16).ap()
    wb1 = nc.dram_tensor("wb1", [NB1 * 128, WB], BF16).ap()
    hs = nc.dram_tensor("hs", [NT + 1, 128, 8 * TT], F32).ap()

    from contextlib import ExitStack
    es = ExitStack()

    def sb(name, shape, dt=F32):
        return es.enter_context(nc.sbuf_tensor("sb_" + name, list(shape), dt))

    ps = [es.enter_context(nc.psum_tensor("ps%d" % i, [128, 512], F32)) for i in range(8)]
    psb = [p.bitcast(BF16) for p in ps]
    sems = {}
    for n in S.sem_names():
        sems[n] = es.enter_context(nc.semaphore("s_" + n))

    def PK(i):
        return ("ps", i)

    semvals = {}

    def replay_all():
        import os
        if os.environ.get("KCHECK"):
            print("deadlock check ok:", S.check(semvals))
        with nc.Block() as block:
            @block.tensor
            def _(E):
                S.replay("pe", E, sems)

            @block.scalar
            def _(E):
                S.replay("act", E, sems)

            @block.vector
            def _(E):
                S.replay("dve", E, sems)

            @block.gpsimd
            def _(E):
                S.replay("pool", E, sems)

            @block.sync
            def _(E):
                S.replay("sp", E, sems)
        for e in S.ENG:
            S.ops[e] = []

    consts = sb("consts", [128, 512])
    identb = sb("identb", [128, 128], BF16)
    onesb = sb("onesb", [128, 128], BF16)
    cols = sb("cols", [128, NCOLS])
    epsc = sb("epsc", [128, 1])
    wring = [sb("wring%d" % i, [128, WB], BF16) for i in range(NSLOT)]
    hT = sb("hT", [128, 8, TT])
    aT = sb("aT", [128, 8, TT], BF16)
    mT = sb("mT", [128, 8, TT])
    sq = sb("sq", [128, 8, TT], BF16)
    rstd = sb("rstd", [128, TT])
    triu = consts[:, 128:256]
    lstr = consts[:, 256:384]
    onesf = consts[:, 384:512]

    S.emit("pool", lambda E: E.dma_start(out=consts[:], in_=consts_d[:, :]), writes=["consts"], dma=True)
    S.emit("pool", lambda E: E.dma_start(out=identb[:], in_=consts_d[:, 0:128]), writes=["identb"], dma=True)
    S.emit("pool", lambda E: E.dma_start(out=onesb[:], in_=consts_d[:, 384:512]), writes=["onesb"], dma=True)
    S.emit("pool", lambda E: E.dma_start(out=cols[:], in_=cols_d[:, :]), writes=["cols"], dma=True)
    S.emit("pool", lambda E: E.memset(epsc[:], EPS), writes=["epsc"])
    for (src, dst, nb) in ((w0, wb0, NB0), (w1, wb1, NB1)):
        for b0 in range(0, nb, 2):
            b1 = min(nb, b0 + 2)
            S.emit("pool", lambda E, src=src, dst=dst, b0=b0, b1=b1: E.dma_start(
                out=dst[b0 * 128:b1 * 128, :], in_=src[b0 * 128:b1 * 128, :]),
                writes=[("wb", id(dst) % 1000, b) for b in range(b0, b1)], dma=True)

    class WStream:
        def __init__(self):
            self.seq = []
            self.issued = 0
            self.used = 0

        def add(self, dram, tag, nblocks, reps):
            for _ in range(reps):
                for b in range(nblocks):
                    self.seq.append((dram, b, ("wb", tag, b)))

        def issue(self):
            if self.issued >= len(self.seq):
                return
            dram, b, key = self.seq[self.issued]
            slot = self.issued % NSLOT
            S.emit("sp", lambda E, dram=dram, b=b, slot=slot: E.dma_start(
                out=wring[slot][:], in_=dram[b * 128:(b + 1) * 128, :]),
                reads=[key], writes=[("ws", slot)], dma=True)
            self.issued += 1

        def start(self):
            for _ in range(NSLOT):
                self.issue()

        def next(self, name, KC, NB):
            i = self.used
            slot = i % NSLOT
            self.used += 1
            view = wring[slot][:, 0:KC * NB].rearrange("p (k n) -> p k n", k=KC)
            return view, ("ws", slot)

        def done(self):
            self.issue()

    WS = WStream()
    WS.add(wb0, id(wb0) % 1000, NB0, NT + 1)
    WS.add(wb1, id(wb1) % 1000, NB1, NT + 1)

    dense_rr = [0]

    def dbank():
        dense_rr[0] ^= 1
        return dense_rr[0]

    def ln_col(idx):
        return cols[:, C_LN + idx * 8: C_LN + idx * 8 + 8]

    HK = [("hT", k) for k in range(8)]
    MK = [("mT", k) for k in range(8)]

    def compute_rstd(src, srckeys, KC, D, T):
        S.emit("act", lambda E: E.activation(out=sq[:, 0:KC, 0:T], in_=src, func=AF.Square),
               reads=list(srckeys), writes=["sq"])

        def f(E):
            for kc in range(KC):
                ins = E.matmul(ps[2][:, 0:T], lhsT=onesb[:, :], rhs=sq[:, kc, 0:T], start=(kc == 0), stop=(kc == KC - 1))
            return ins
        S.emit("pe", f, reads=["sq", "onesb"], writes=[PK(2)])
        S.emit("act", lambda E: E.activation(out=rstd[:, 0:T], in_=ps[2][:, 0:T], func=AF.Sqrt, bias=epsc[:], scale=1.0 / D),
               reads=[PK(2), "epsc"], writes=["rstd"])
        S.emit("dve", lambda E: E.reciprocal(out=rstd[:, 0:T], in_=rstd[:, 0:T]), reads=["rstd"], writes=["rstd"])

    def norm_to(src3, srckeys, KC, D, T, gcols, dst3, dstkey):
        compute_rstd(src3[:, 0:KC, 0:T], srckeys, KC, D, T)
        for kc in range(KC):
            S.emit("dve", lambda E, kc=kc: E.scalar_tensor_tensor(
                out=dst3[:, kc, 0:T], in0=src3[:, kc, 0:T], scalar=gcols[:, kc:kc + 1], in1=rstd[:, 0:T],
                op0=ALU.mult, op1=ALU.mult), reads=list(srckeys) + ["rstd", "cols"], writes=[dstkey])

    def postnorm_residual(T, gcols):
        compute_rstd(mT[:, :, 0:T], MK, 8, D_MODEL, T)
        for kc in range(8):
            S.emit("dve", lambda E, kc=kc: E.scalar_tensor_tensor(
                out=mT[:, kc, 0:T], in0=mT[:, kc, 0:T], scalar=gcols[:, kc:kc + 1], in1=rstd[:, 0:T],
                op0=ALU.mult, op1=ALU.mult), reads=[("mT", kc), "rstd", "cols"], writes=[("mT", kc)])
            S.emit("pool", lambda E, kc=kc: E.tensor_tensor(
                out=hT[:, kc, 0:T], in0=hT[:, kc, 0:T], in1=mT[:, kc, 0:T], op=ALU.add),
                reads=[("mT", kc), ("hT", kc)], writes=[("hT", kc)])

    def dense_ws(name, KC, NB, nvalid, T, rhs_fn, rhskeys, evac):
        view, wkey = WS.next(name, KC, NB)
        for oc in range(nvalid // 128):
            bank = dbank()

            def f(E, oc=oc, bank=bank):
                for kc in range(KC):
                    ins = E.matmul(ps[bank][:, 0:T], lhsT=view[:, kc, oc * 128:(oc + 1) * 128], rhs=rhs_fn(kc),
                                   start=(kc == 0), stop=(kc == KC - 1))
                return ins
            S.emit("pe", f, reads=[wkey] + rhskeys, writes=[PK(bank)])
            evac(oc, bank)
        WS.done()

    def ffn(layer, T, hidT):
        norm_to(hT, HK, 8, D_MODEL, T, ln_col(4 + layer), aT, "aT")
        for b in range(6):
            nv = 512 if b < 5 else 256

            def ev(oc, bank, b=b):
                j = b * 4 + oc
                S.emit("act", lambda E: E.activation(out=hidT[:, j, 0:T], in_=ps[bank][:, 0:T], func=AF.Silu),
                       reads=[PK(bank)], writes=[("hid", j)])
            dense_ws("gate", 8, 512, nv, T, lambda kc: aT[:, kc, 0:T], ["aT"], ev)
        for b in range(6):
            nv = 512 if b < 5 else 256

            def ev(oc, bank, b=b):
                j = b * 4 + oc
                S.emit("dve", lambda E: E.tensor_tensor(out=hidT[:, j, 0:T], in0=ps[bank][:, 0:T], in1=hidT[:, j, 0:T], op=ALU.mult),
                       reads=[PK(bank), ("hid", j)], writes=[("hid", j)])
            dense_ws("up", 8, 512, nv, T, lambda kc: aT[:, kc, 0:T], ["aT"], ev)
        for b in range(8):
            def ev(oc, bank, b=b):
                S.emit("act", lambda E: E.copy(out=mT[:, b, 0:T], in_=ps[bank][:, 0:T]), reads=[PK(bank)], writes=[("mT", b)])
            dense_ws("down", 22, 128, 128, T, lambda kc: hidT[:, kc, 0:T], [("hid", j) for j in range(22)], ev)
        postnorm_residual(T, ln_col(6 + layer))

    es1 = ExitStack()

    def sb1(name, shape, dt=F32):
        return es1.enter_context(nc.sbuf_tensor("s1_" + name, list(shape), dt))

    rows = sb1("rows", [128, NROWS])
    abc = sb1("abc", [128, 32])
    wdt = sb1("wdt", [128, 8, 32], BF16)
    Sst = [sb1("Sst%d" % i, [128, 2048]) for i in range(2)]
    Sbf = [sb1("Sbf%d" % i, [128, 2048], BF16) for i in range(2)]
    tail = [sb1("tail%d" % i, [128, 32, 3]) for i in range(2)]
    xbcT = sb1("xbcT", [128, 32, TT], BF16)
    pre = [sb1("pre%d" % i, [128, 2 * (LS + 3) if False else TT + 8]) for i in range(2)]
    cacc = [sb1("cacc%d" % i, [128, TT]) for i in range(2)]
    cq2 = [sb1("cq2%d" % i, [128, TT]) for i in range(2)]
    cq3 = [sb1("cq3%d" % i, [128, TT]) for i in range(2)]
    sz = sb1("sz", [128, 2, 2048], BF16)
    dtb = sb1("dtb", [128, 2, 32])
    dtA = sb1("dtA", [128, 2, 32])
    x_tok = [sb1("x_tok%d" % i, [128, 2048], BF16) for i in range(2)]
    B_tok = [sb1("B_tok%d" % i, [128, 1024], BF16) for i in range(2)]
    Rm = [sb1("Rm%d" % i, [128, 512]) for i in range(2)]
    decay = [sb1("decay%d" % i, [128, 512], BF16) for i in range(2)]
    MT = [sb1("MT%d" % i, [128, 512], BF16) for i in range(2)]
    cbm = [sb1("cbm%d" % i, [128, 128], BF16) for i in range(2)]
    xdt = [sb1("xdt%d" % i, [128, 256], BF16) for i in range(2)]
    xw = [sb1("xw%d" % i, [128, 256], BF16) for i in range(2)]
    xD = [sb1("xD%d" % i, [128, 256], BF16) for i in range(2)]
    tmp = [sb1("tmp%d" % i, [128, 256]) for i in range(2)]
    yall = [sb1("yall%d" % i, [128, 2048]) for i in range(2)]
    ysq = sb1("ysq", [128, 2048], BF16)
    yn = sb1("yn", [128, 2048], BF16)
    ynT = sb1("ynT", [128, 16, TT], BF16)
    eacum = [sb1("eacum%d" % i, [128, 32]) for i in range(2)]
    Elast = [sb1("Elast%d" % i, [128, 32]) for i in range(2)]
    ss8 = sb1("ss8", [128, 8])
    hidT1 = sb1("hidT1", [128, 22, TT], BF16)

    S.emit("pool", lambda E: E.dma_start(out=rows[:], in_=rows_d[:, :]), writes=["rows"], dma=True)
    S.emit("pool", lambda E: E.dma_start(out=wdt[:].rearrange("p k n -> p (k n)"), in_=wdt_d[:, :]), writes=["wdt"], dma=True)
    S.emit("act", lambda E: E.activation(out=abc[:], in_=rows[:, R_ALOG:R_ALOG + 32], func=AF.Exp), reads=["rows"], writes=["abc"])
    S.emit("dve", lambda E: E.tensor_scalar(out=abc[:], in0=abc[:], scalar1=-1.0, scalar2=None, op0=ALU.mult), reads=["abc"], writes=["abc"])
    WS.start()

    def SKs(si):
        return [(("S", si), g) for g in range(8)]

    def SBKs(si):
        return [(("Sbf", si), g) for g in range(8)]

    def ssd_chunk_gen(ci, c0, L, si):
        tok = slice(c0, c0 + L)
        cp = ci % 2
        SK, SBK = ("S", si), ("Sbf", si)
        XT, BT, EA, EL, YA = x_tok[cp], B_tok[cp], eacum[cp], Elast[cp], yall[cp]
        kXT, kBT, kEA, kEL = ("x_tok", cp), ("B_tok", cp), ("eacum", cp), ("Elast", cp)
        YK = [("yall", cp, g) for g in range(8)]
        for half in range(2):
            bank = dbank()

            def f(E, half=half, bank=bank):
                for k in range(8):
                    ins = E.transpose(psb[bank][0:L, k * 128:(k + 1) * 128], xbcT[:, half * 8 + k, tok], identb[:, :])
                return ins
            S.emit("pe", f, reads=["xbcT", "identb"], writes=[PK(bank)])
            S.emit("act", lambda E, half=half, bank=bank: E.copy(out=XT[0:L, half * 1024:(half + 1) * 1024], in_=psb[bank][0:L, 0:1024]),
                   reads=[PK(bank)], writes=[kXT])
        bank = dbank()

        def f(E, bank=bank):
            for k in range(8):
                ins = E.transpose(psb[bank][0:L, k * 128:(k + 1) * 128], xbcT[:, 16 + k, tok], identb[:, :])
            return ins
        S.emit("pe", f, reads=["xbcT", "identb"], writes=[PK(bank)])
        S.emit("act", lambda E, bank=bank: E.copy(out=BT[0:L, :], in_=psb[bank][0:L, 0:1024]), reads=[PK(bank)], writes=[kBT])

        def f(E):
            E.matmul(ps[2][0:L, 0:32], lhsT=triu[0:L, 0:L], rhs=dtA[0:L, ci, :], start=True, stop=True)
            return E.matmul(ps[2][:, 32:64], lhsT=onesf[0:L, :], rhs=dtA[0:L, ci, :], start=True, stop=True)
        S.emit("pe", f, reads=["dtA", "consts"], writes=[PK(2)])
        S.emit("act", lambda E: E.activation(out=EA[0:L, :], in_=ps[2][0:L, 0:32], func=AF.Exp), reads=[PK(2)], writes=[kEA])
        S.emit("act", lambda E: E.activation(out=EL[:, :], in_=ps[2][:, 32:64], func=AF.Exp), reads=[PK(2)], writes=[kEL])
        yield "pro"

        def unit_gen(g):
            u = g % 2
            hs4 = slice(4 * g, 4 * g + 4)
            gcols = slice(256 * g, 256 * g + 256)
            W4 = 4 * L
            Rm_, dec_, MT_, cbm_, xdt_, xw_, xD_, tmp_ = Rm[u], decay[u], MT[u], cbm[u], xdt[u], xw[u], xD[u], tmp[u]
            kR, kD, kM, kC, kX, kW, kXD, kT = ("Rm", u), ("decay", u), ("MT", u), ("cbm", u), ("xdt", u), ("xw", u), ("xD", u), ("tmp", u)
            segb = 3 + u
            cbv = ps[5][0:L, 0:L] if u == 0 else ps[2][0:L, 256:256 + L]
            dbk = dbank()
            dsv, kds_ = ps[dbk][:, 0:256], PK(dbk)
            yv = ps[6][0:L, u * 256:(u + 1) * 256]
            ysv = ps[7][0:L, u * 256:(u + 1) * 256]
            kcb, kds, ky, kys = (PK(5) if u == 0 else PK(2)), kds_, ("ps6", u), ("ps7", u)
            xv = XT[0:L, gcols].rearrange("p (h d) -> p h d", h=4)
            S.emit("pool", lambda E: E.tensor_tensor(
                out=Rm_[0:L, 0:W4].rearrange("p (h q) -> p h q", h=4),
                in0=dtA[0:L, ci, hs4].unsqueeze(2).to_broadcast([L, 4, L]),
                in1=triu[0:L, 0:L].unsqueeze(1).to_broadcast([L, 4, L]), op=ALU.mult),
                reads=["dtA", "consts"], writes=[kR])
            S.emit("pe", lambda E: E.matmul(ps[segb][0:L, 0:W4], lhsT=lstr[0:L, 0:L], rhs=Rm_[0:L, 0:W4], start=True, stop=True),
                   reads=[kR, "consts"], writes=[PK(segb)])
            S.emit("act", lambda E: E.activation(out=dec_[0:L, 0:W4], in_=ps[segb][0:L, 0:W4], func=AF.Exp),
                   reads=[PK(segb)], writes=[kD])
            S.emit("pe", lambda E: E.matmul(cbv, lhsT=xbcT[:, 16 + g, tok], rhs=xbcT[:, 24 + g, tok], start=True, stop=True),
                   reads=["xbcT"], writes=[kcb])
            S.emit("dve", lambda E: E.tensor_tensor(out=cbm_[0:L, 0:L], in0=cbv, in1=triu[0:L, 0:L], op=ALU.mult),
                   reads=[kcb, "consts"], writes=[kC])
            S.emit("pool", lambda E: E.tensor_tensor(
                out=xdt_[0:L, :].rearrange("p (h d) -> p h d", h=4), in0=xv,
                in1=dtb[0:L, ci, hs4].unsqueeze(2).to_broadcast([L, 4, 64]), op=ALU.mult),
                reads=[kXT, "dtb"], writes=[kX])
            S.emit("pool", lambda E: E.tensor_tensor(
                out=xD_[0:L, :].rearrange("p (h d) -> p h d", h=4), in0=xv,
                in1=rows[0:L, R_D + 4 * g:R_D + 4 * g + 4].unsqueeze(2).to_broadcast([L, 4, 64]), op=ALU.mult),
                reads=[kXT, "rows"], writes=[kXD])
            yield "A"
            S.emit("dve", lambda E: E.tensor_tensor(
                out=MT_[0:L, 0:W4].rearrange("p (r q) -> p r q", r=4),
                in0=dec_[0:L, 0:W4].rearrange("p (r q) -> p r q", r=4),
                in1=cbm_[0:L, 0:L].unsqueeze(1).to_broadcast([L, 4, L]), op=ALU.mult),
                reads=[kD, kC], writes=[kM])
            S.emit("dve", lambda E: E.tensor_tensor(
                out=xw_[0:L, :].rearrange("p (h d) -> p h d", h=4), in0=xdt_[0:L, :].rearrange("p (h d) -> p h d", h=4),
                in1=dec_[0:L, 0:W4].rearrange("p (h q) -> p h q", h=4)[:, :, L - 1:L].to_broadcast([L, 4, 64]), op=ALU.mult),
                reads=[kX, kD], writes=[kW])

            def f(E):
                E.matmul(yv, lhsT=identb[0:L, 0:L], rhs=xD_[0:L, :], start=True, stop=False)
                for hh in range(4):
                    ins = E.matmul(yv[:, hh * 64:(hh + 1) * 64], lhsT=MT_[0:L, hh * L:(hh + 1) * L], rhs=xdt_[0:L, hh * 64:(hh + 1) * 64],
                                   start=False, stop=True, skip_group_check=True)
                return ins
            S.emit("pe", f, reads=[kM, kX, kXD, "identb"], writes=[ky])
            S.emit("pe", lambda E: E.matmul(ysv, lhsT=xbcT[:, 24 + g, tok], rhs=Sbf[si][:, g * 256:(g + 1) * 256], start=True, stop=True),
                   reads=["xbcT", (SBK, g)], writes=[kys])
            S.emit("pe", lambda E: E.matmul(dsv, lhsT=BT[0:L, g * 128:(g + 1) * 128], rhs=xw_[0:L, :], start=True, stop=True),
                   reads=[kBT, kW], writes=[kds])
            yield "B"
            S.emit("dve", lambda E: E.tensor_tensor(
                out=tmp_[0:L, :].rearrange("p (h d) -> p h d", h=4), in0=ysv.rearrange("p (h d) -> p h d", h=4),
                in1=EA[0:L, hs4].unsqueeze(2).to_broadcast([L, 4, 64]), op=ALU.mult),
                reads=[kys, kEA], writes=[kT])
            S.emit("dve", lambda E: E.tensor_tensor(out=tmp_[0:L, :], in0=yv, in1=tmp_[0:L, :], op=ALU.add),
                   reads=[ky, kT], writes=[kT])
            S.emit("dve", lambda E: E.tensor_tensor(out=YA[0:L, gcols], in0=tmp_[0:L, :], in1=sz[0:L, ci, gcols], op=ALU.mult),
                   reads=[kT, "sz"], writes=[("yall", cp, g)])
            S.emit("dve", lambda E: E.tensor_tensor(
                out=Sst[si][:, gcols].rearrange("p (h d) -> p h d", h=4), in0=Sst[si][:, gcols].rearrange("p (h d) -> p h d", h=4),
                in1=EL[:, hs4].unsqueeze(2).to_broadcast([128, 4, 64]), op=ALU.mult),
                reads=[(SK, g), kEL], writes=[(SK, g)])
            S.emit("dve", lambda E: E.tensor_tensor(out=Sst[si][:, gcols], in0=dsv, in1=Sst[si][:, gcols], op=ALU.add),
                   reads=[kds, (SK, g)], writes=[(SK, g)])
            S.emit("act", lambda E: E.copy(out=Sbf[si][:, gcols], in_=Sst[si][:, gcols]), reads=[(SK, g)], writes=[(SBK, g)])

        gens = [unit_gen(g) for g in range(8)]
        next(gens[0])
        for g in range(8):
            if g + 1 < 8:
                next(gens[g + 1])
            next(gens[g])
            next(gens[g], None)
            if g == 3:
                yield "mid"
        yield "units"
        S.emit("act", lambda E: E.activation(out=ysq[0:L, :], in_=YA[0:L, :], func=AF.Square), reads=YK, writes=["ysq"])
        S.emit("dve", lambda E: E.tensor_reduce(out=ss8[0:L, :], in_=ysq[0:L, :].rearrange("p (g c) -> p g c", g=8), axis=AX.X, op=ALU.add),
               reads=["ysq"], writes=["ss8"])
        S.emit("act", lambda E: E.activation(out=ss8[0:L, :], in_=ss8[0:L, :], func=AF.Sqrt, bias=epsc[0:L, :], scale=1.0 / 256.0),
               reads=["ss8", "epsc"], writes=["ss8"])
        S.emit("dve", lambda E: E.reciprocal(out=ss8[0:L, :], in_=ss8[0:L, :]), reads=["ss8"], writes=["ss8"])
        S.emit("dve", lambda E: E.tensor_tensor(
            out=YA[0:L, :].rearrange("p (g c) -> p g c", g=8), in0=YA[0:L, :].rearrange("p (g c) -> p g c", g=8),
            in1=ss8[0:L, :].unsqueeze(2).to_broadcast([L, 8, 256]), op=ALU.mult), reads=YK + ["ss8"], writes=YK)
        S.emit("pool", lambda E: E.tensor_tensor(out=yn[0:L, :], in0=YA[0:L, :], in1=rows[0:L, R_GN:R_GN + 2048], op=ALU.mult),
               reads=YK + ["rows"], writes=["yn"])
        for half in range(2):
            bank = dbank()

            def f(E, half=half, bank=bank):
                for k in range(8):
                    c = half * 8 + k
                    ins = E.transpose(psb[bank][:, k * L:(k + 1) * L], yn[0:L, c * 128:(c + 1) * 128], identb[0:L, 0:L])
                return ins
            S.emit("pe", f, reads=["yn", "identb"], writes=[PK(bank)])
            S.emit("act", lambda E, half=half, bank=bank: E.copy(
                out=ynT[:, half * 8:(half + 1) * 8, tok], in_=psb[bank][:, 0:8 * L].rearrange("p (k q) -> p k q", k=8)),
                reads=[PK(bank)], writes=["ynT"])
        yield "epi"

    def ssd_tile(chunk_list):
        gens = [ssd_chunk_gen(*c) for c in chunk_list]
        for gch in gens:
            next(gch)
        prev = None
        for gch in gens:
            next(gch)
            if prev is not None:
                next(prev)
            next(gch)
            prev = gch
        next(prev)

    def pass1_tile(T, segs, load_fn, hs_idx):
        load_fn()
        norm_to(hT, HK, 8, D_MODEL, T, ln_col(0), aT, "aT")
        ci = 0
        chunk_list = []
        for (s0, Ls, si, chunks) in segs:
            for (c0, L) in chunks:
                chunk_list.append((ci, c0, L, si))

                def f(E, c0=c0, L=L):
                    for kc in range(8):
                        ins = E.matmul(ps[2][0:L, 0:32], lhsT=aT[:, kc, c0:c0 + L], rhs=wdt[:, kc, :], start=(kc == 0), stop=(kc == 7))
                    return ins
                S.emit("pe", f, reads=["aT", "wdt"], writes=[PK(2)])
                S.emit("dve", lambda E, L=L, ci=ci: E.tensor_tensor(out=dtb[0:L, ci, :], in0=ps[2][0:L, 0:32], in1=rows[0:L, R_DTB:R_DTB + 32], op=ALU.add),
                       reads=[PK(2), "rows"], writes=["dtb"])
                S.emit("act", lambda E, L=L, ci=ci: E.activation(out=dtb[0:L, ci, :], in_=dtb[0:L, ci, :], func=AF.Exp), reads=["dtb"], writes=["dtb"])
                S.emit("act", lambda E, L=L, ci=ci: E.activation(out=dtb[0:L, ci, :], in_=dtb[0:L, ci, :], func=AF.Ln, bias=1.0, scale=1.0),
                       reads=["dtb"], writes=["dtb"])
                S.emit("dve", lambda E, L=L, ci=ci: E.tensor_tensor(out=dtA[0:L, ci, :], in0=dtb[0:L, ci, :], in1=abc[0:L, :], op=ALU.mult),
                       reads=["dtb", "abc"], writes=["dtA"])
                ci += 1
        for b in range(8):
            def ev(oc, bank, b=b):
                ch = b * 4 + oc
                pb = ch % 2
                P = pre[pb]
                pk, ak = ("pre", pb), ("cacc", pb)
                off = 0
                for (s0, Ls, si, chunks) in segs:
                    base = s0 + 3 * (segs.index((s0, Ls, si, chunks)))
                    S.emit("pool", lambda E, base=base, si=si: E.tensor_copy(out=P[:, base:base + 3], in_=tail[si][:, ch, :]),
                           reads=[("tail", si)], writes=[pk])
                    S.emit("act", lambda E, base=base, s0=s0, Ls=Ls: E.copy(out=P[:, base + 3:base + 3 + Ls], in_=ps[bank][:, s0:s0 + Ls]),
                           reads=[PK(bank)], writes=[pk])
                    S.emit("pool", lambda E, base=base, si=si, Ls=Ls: E.tensor_copy(out=tail[si][:, ch, :], in_=P[:, base + Ls:base + Ls + 3]),
                           reads=[pk], writes=[("tail", si)])
                    cw = cols[:, C_CW + ch * 4:C_CW + ch * 4 + 4]
                    A = cacc[pb]
                    Q2, Q3 = cq2[pb], cq3[pb]
                    qk2, qk3 = ("cq2", pb), ("cq3", pb)
                    S.emit("act", lambda E, s0=s0, Ls=Ls, cw=cw, Q3=Q3: E.activation(
                        out=Q3[:, s0:s0 + Ls], in_=ps[bank][:, s0:s0 + Ls], func=AF.Copy, scale=cw[:, 3:4]),
                        reads=[PK(bank), "cols"], writes=[qk3])
                    S.emit("act", lambda E, base=base, s0=s0, Ls=Ls, cw=cw, Q2=Q2: E.activation(
                        out=Q2[:, s0:s0 + Ls], in_=P[:, base + 2:base + 2 + Ls], func=AF.Copy, scale=cw[:, 2:3]),
                        reads=[pk, "cols"], writes=[qk2])
                    S.emit("pool", lambda E, s0=s0, Ls=Ls, Q2=Q2, Q3=Q3: E.tensor_tensor(
                        out=Q2[:, s0:s0 + Ls], in0=Q2[:, s0:s0 + Ls], in1=Q3[:, s0:s0 + Ls], op=ALU.add),
                        reads=[qk2, qk3], writes=[qk2])
                    S.emit("dve", lambda E, base=base, s0=s0, Ls=Ls, cw=cw, A=A: E.tensor_scalar(
                        out=A[:, s0:s0 + Ls], in0=P[:, base:base + Ls], scalar1=cw[:, 0:1], scalar2=None, op0=ALU.mult),
                        reads=[pk, "cols"], writes=[ak])
                    S.emit("dve", lambda E, base=base, s0=s0, Ls=Ls, cw=cw, A=A: E.scalar_tensor_tensor(
                        out=A[:, s0:s0 + Ls], in0=P[:, base + 1:base + 1 + Ls], scalar=cw[:, 1:2], in1=A[:, s0:s0 + Ls],
                        op0=ALU.mult, op1=ALU.add), reads=[pk, ak, "cols"], writes=[ak])
                    S.emit("dve", lambda E, s0=s0, Ls=Ls, A=A, Q2=Q2: E.tensor_tensor(
                        out=A[:, s0:s0 + Ls], in0=A[:, s0:s0 + Ls], in1=Q2[:, s0:s0 + Ls], op=ALU.add),
                        reads=[ak, qk2], writes=[ak])
                S.emit("act", lambda E, A=cacc[pb]: E.activation(out=xbcT[:, ch, 0:T], in_=A[:, 0:T], func=AF.Silu,
                                                                 bias=cols[:, C_CB + ch:C_CB + ch + 1], scale=1.0),
                       reads=[ak, "cols"], writes=["xbcT"])
            dense_ws("xbc", 8, 512, 512, T, lambda kc: aT[:, kc, 0:T], ["aT"], ev)
        for b in range(4):
            view, wkey = WS.next("z", 8, 512)
            for (ci, c0, L, si) in chunk_list:
                bank = dbank()

                def f(E, c0=c0, L=L, bank=bank, view=view):
                    for kc in range(8):
                        ins = E.matmul(ps[bank][0:L, 0:512], lhsT=aT[:, kc, c0:c0 + L], rhs=view[:, kc, :], start=(kc == 0), stop=(kc == 7))
                    return ins
                S.emit("pe", f, reads=[wkey, "aT"], writes=[PK(bank)])
                S.emit("act", lambda E, L=L, ci=ci, bank=bank, b=b: E.activation(out=sz[0:L, ci, b * 512:(b + 1) * 512], in_=ps[bank][0:L, 0:512], func=AF.Silu),
                       reads=[PK(bank)], writes=["sz"])
            WS.done()
        ssd_tile(chunk_list)
        for b in range(4):
            def ev(oc, bank, b=b):
                S.emit("act", lambda E: E.copy(out=mT[:, b * 2 + oc, 0:T], in_=ps[bank][:, 0:T]), reads=[PK(bank)], writes=[("mT", b * 2 + oc)])
            dense_ws("wout", 16, 256, 256, T, lambda kc: ynT[:, kc, 0:T], ["ynT"], ev)
        postnorm_residual(T, ln_col(2))
        ffn(0, T, hidT1)
        S.emit("pool", lambda E: E.dma_start(out=hs[hs_idx].rearrange("p (k t) -> p k t", k=8)[:, :, 0:T], in_=hT[:, :, 0:T]),
               reads=HK, writes=[("hs", hs_idx)], dma=True)

    for si in range(2):
        S.emit("pool", lambda E, si=si: E.dma_start(out=Sst[si][:], in_=ssmst_d[si]), writes=SKs(si), dma=True)
        S.emit("pool", lambda E, si=si: E.dma_start(out=tail[si][:].rearrange("p c k -> p (c k)"), in_=convst_d[si]), writes=[("tail", si)], dma=True)
        S.emit("act", lambda E, si=si: E.copy(out=Sbf[si][:], in_=Sst[si][:]), reads=SKs(si), writes=SBKs(si))

    def load_sample():
        S.emit("pool", lambda E: E.dma_start(out=hT[:, :, 0:2 * LS], in_=xsT.rearrange("k p t -> p k t")), writes=HK, dma=True)
    pass1_tile(2 * LS, [(0, LS, 0, [(0, LS)]), (LS, LS, 1, [(LS, LS)])], load_sample, NT)
    for si in range(2):
        S.emit("pool", lambda E, si=si: E.dma_start(out=sssm_o[si], in_=Sst[si][:]), reads=SKs(si), writes=[("o_sssm", si)], dma=True)
        S.emit("pool", lambda E, si=si: E.dma_start(out=sconv_o[si], in_=tail[si][:].rearrange("p c k -> p (c k)")),
               reads=[("tail", si)], writes=[("o_sconv", si)], dma=True)
    S.emit("pool", lambda E: E.memset(Sst[0][:], 0.0), writes=SKs(0))
    S.emit("pool", lambda E: E.memset(Sbf[0][:], 0.0), writes=SBKs(0))
    S.emit("pool", lambda E: E.memset(tail[0][:], 0.0), writes=[("tail", 0)])
    for j in range(NT):
        def load_p(j=j):
            S.emit("pool", lambda E: E.dma_start(out=hT[:, :, :], in_=xT.rearrange("k p t -> p k t")[:, :, j * TT:(j + 1) * TT]),
                   writes=HK, dma=True)
        pass1_tile(TT, [(0, TT, 0, [(0, 128), (128, 128)])], load_p, j)
        if j % 16 == 15 and j != NT - 1:
            S.new_epoch()
    S.emit("pool", lambda E: E.dma_start(out=pssm_o[:, :], in_=Sst[0][:]), reads=SKs(0), writes=["o_pssm"], dma=True)
    S.emit("pool", lambda E: E.dma_start(out=pconv_o[:, :], in_=tail[0][:].rearrange("p c k -> p (c k)")), reads=[("tail", 0)], writes=["o_pconv"], dma=True)

    S.new_epoch()
    replay_all()
    es1.close()

    es2 = ExitStack()

    def sb2(name, shape, dt=F32):
        return es2.enter_context(nc.sbuf_tensor("s2_" + name, list(shape), dt))

    NBLK = SEQ // 128
    latT = sb2("latT", [128, 2, SEQ], BF16)
    krT = sb2("krT", [32, SEQ], BF16)
    lat_tok = sb2("lat_tok", [128, NBLK, 256], BF16)
    if NBLK >= 63:
        flat = lat_tok[:].rearrange("p b c -> p (b c)")
        off = [20 * 256]

        def carve(n):
            v = flat[:, off[0]:off[0] + n]
            off[0] += n
            return v
        s_latT = [carve(2 * (PAST + LS)).rearrange("p (j t) -> p j t", j=2) for i in range(2)]
        s_krT = [carve(PAST + LS)[0:32, :] for i in range(2)]
        s_lat_tok = [carve(9 * 256).rearrange("p (b c) -> p b c", b=9) for i in range(2)]
        assert off[0] <= NBLK * 256
    else:
        s_latT = [sb2("s_latT%d" % i, [128, 2, PAST + LS], BF16) for i in range(2)]
        s_krT = [sb2("s_krT%d" % i, [32, PAST + LS], BF16) for i in range(2)]
        s_lat_tok = [sb2("s_lat_tok%d" % i, [128, 9, 256], BF16) for i in range(2)]
    wuk = sb2("wuk", [128, 8, 256], BF16)
    wuv = sb2("wuv", [128, 2, 16, 128], BF16)
    cqT = mT[:, 0:4, :]
    cqnT = sb2("cqnT", [128, 4, TT], BF16)
    latraw = sb2("latraw", [128, 2, TT])
    latn = sb2("latn", [128, 2, TT])
    krf = sb2("krf", [32, 3, TT])
    cst = sb2("cst", [32, 2, TT])
    qq = sb2("qq", [128, 24 * TT], BF16)
    qnT = qq[:, 0:8 * TT].rearrange("p (k t) -> p k t", k=8)
    qrT = qq[0:32, 8 * TT:24 * TT].rearrange("p (k t) -> p k t", k=16)
    hidT2 = qq[:, 0:22 * TT].rearrange("p (k t) -> p k t", k=22)
    qlT = [sb2("qlT%d" % i, [128, 2, 2, TT], BF16) for i in range(2)]
    pT = [sb2("pT%d" % i, [128, 2 * TT], BF16) for i in range(3)]
    osb = sb2("osb", [128, 3, 2 * TT])
    pacc = [sb2("pacc%d" % i, [128, 2 * TT]) for i in range(2)]
    olT = [sb2("olT%d" % i, [128, 2, 2, TT], BF16) for i in range(2)]
    oT = sb2("oT", [128, 8, TT], BF16)

    S.emit("pool", lambda E: E.dma_start(out=wuk[:].rearrange("p a c -> p (a c)"), in_=wuk_d[:, :]), writes=["wuk"], dma=True)
    S.emit("pool", lambda E: E.dma_start(out=wuv[:].rearrange("p a b c -> p (a b c)"), in_=wuv_d[:, :]), writes=["wuv"], dma=True)
    for si in range(2):
        S.emit("pool", lambda E, si=si: E.dma_start(out=s_lat_tok[si][:, 0:8, :], in_=clat_d[si].rearrange("b p c -> p b c")),
               writes=[("ltok", si)], dma=True)
        S.emit("pool", lambda E, si=si: E.dma_start(out=s_latT[si][:, :, 0:PAST], in_=clatT_d[si].rearrange("j p t -> p j t")),
               writes=[("latT", si)], dma=True)
        S.emit("pool", lambda E, si=si: E.dma_start(out=s_krT[si][:, 0:PAST], in_=ckrT_d[si]), writes=[("krT", si)], dma=True)

    def pass2_tile(T, segs, hs_idx, cs_src, out_fn, final_fn):
        S.emit("pool", lambda E: E.dma_start(out=hT[:, :, 0:T], in_=hs[hs_idx].rearrange("p (k t) -> p k t", k=8)[:, :, 0:T]),
               reads=[("hs", hs_idx)], writes=HK, dma=True)
        S.emit("pool", lambda E: E.dma_start(out=cst[:, :, 0:T], in_=cs_src), writes=["cst"], dma=True)
        norm_to(hT, HK, 8, D_MODEL, T, ln_col(1), aT, "aT")

        def ev(oc, bank):
            S.emit("act", lambda E: E.copy(out=cqT[:, oc, 0:T], in_=ps[bank][:, 0:T]), reads=[PK(bank)], writes=["cqT"])
        dense_ws("wqa", 8, 512, 512, T, lambda kc: aT[:, kc, 0:T], ["aT"], ev)
        norm_to(cqT, ["cqT"], 4, 512, T, cols[:, C_QN:C_QN + 4], cqnT, "cqnT")
        view, wkey = WS.next("wkva", 8, 320)
        for oc in range(2):
            bank = dbank()

            def f(E, oc=oc, bank=bank, view=view):
                for kc in range(8):
                    ins = E.matmul(ps[bank][:, 0:T], lhsT=view[:, kc, oc * 128:(oc + 1) * 128], rhs=aT[:, kc, 0:T], start=(kc == 0), stop=(kc == 7))
                return ins
            S.emit("pe", f, reads=[wkey, "aT"], writes=[PK(bank)])
            S.emit("act", lambda E, oc=oc, bank=bank: E.copy(out=latraw[:, oc, 0:T], in_=ps[bank][:, 0:T]), reads=[PK(bank)], writes=["latraw"])
        bank = dbank()

        def f(E, bank=bank, view=view):
            for w in range(2):
                for kc in range(8):
                    ins = E.matmul(ps[bank][0:32, w * T:(w + 1) * T], lhsT=view[:, kc, 256 + 32 * w:288 + 32 * w], rhs=aT[:, kc, 0:T],
                                   start=(kc == 0), stop=(kc == 7))
            return ins
        S.emit("pe", f, reads=[wkey, "aT"], writes=[PK(bank)])
        WS.done()
        S.emit("dve", lambda E, bank=bank: E.tensor_tensor(out=krf[:, 0, 0:T], in0=ps[bank][0:32, 0:T], in1=cst[:, 0, 0:T], op=ALU.mult),
               reads=[PK(bank), "cst"], writes=["krf0"])
        S.emit("dve", lambda E, bank=bank: E.tensor_tensor(out=krf[:, 1, 0:T], in0=ps[bank][0:32, T:2 * T], in1=cst[:, 1, 0:T], op=ALU.mult),
               reads=[PK(bank), "cst"], writes=["krf1"])
        S.emit("dve", lambda E: E.tensor_tensor(out=krf[:, 2, 0:T], in0=krf[:, 0, 0:T], in1=krf[:, 1, 0:T], op=ALU.add),
               reads=["krf0", "krf1"], writes=["krf2"])
        norm_to(latraw, ["latraw"], 2, 256, T, cols[:, C_KN:C_KN + 2], latn, "latn")
        out_fn()
        for sg in segs:
            c0, L, ctx, kp = sg["c0"], sg["L"], sg["ctx"], sg["kpos"]
            cl, ck, ct, ci = ctx
            S.emit("act", lambda E, c0=c0, L=L, cl=cl, kp=kp: E.copy(out=cl[:, :, kp:kp + L], in_=latn[:, :, c0:c0 + L]),
                   reads=["latn"], writes=[("latT", ci)])
            S.emit("act", lambda E, c0=c0, L=L, ck=ck, kp=kp: E.copy(out=ck[:, kp:kp + L], in_=krf[:, 2, c0:c0 + L]),
                   reads=["krf2"], writes=[("krT", ci)])
            nch = (L + 127) // 128
            for cc in range(nch):
                Lc = min(128, L - cc * 128)
                blk = (kp + cc * 128) // 128
                bank = dbank()

                def f(E, cl=cl, kp=kp, cc=cc, Lc=Lc, bank=bank):
                    for j in range(2):
                        ins = E.transpose(psb[bank][0:Lc, j * 128:(j + 1) * 128], cl[:, j, kp + cc * 128:kp + cc * 128 + Lc], identb[:, :])
                    return ins
                S.emit("pe", f, reads=[("latT", ci), "identb"], writes=[PK(bank)])
                S.emit("act", lambda E, ct=ct, blk=blk, Lc=Lc, bank=bank: E.copy(out=ct[0:Lc, blk, :], in_=psb[bank][0:Lc, 0:256]),
                       reads=[PK(bank)], writes=[("ltok", ci)])
        view, wkey = WS.next("wqb", 4, 1024)
        for i in range(8):
            bank = dbank()

            def f(E, i=i, bank=bank, view=view):
                for kc in range(4):
                    ins = E.matmul(ps[bank][:, 0:T], lhsT=view[:, kc, i * 128:(i + 1) * 128], rhs=cqnT[:, kc, 0:T], start=(kc == 0), stop=(kc == 3))
                return ins
            S.emit("pe", f, reads=[wkey, "cqnT"], writes=[PK(bank)])
            S.emit("act", lambda E, i=i, bank=bank: E.copy(out=qnT[:, i, 0:T], in_=ps[bank][:, 0:T]), reads=[PK(bank)], writes=["qnT"])
        WS.done()
        view, wkey = WS.next("wqb", 4, 1024)
        for h in range(16):
            bank = dbank()

            def f(E, h=h, bank=bank, view=view):
                for w in range(2):
                    for kc in range(4):
                        ins = E.matmul(ps[bank][0:32, w * T:(w + 1) * T], lhsT=view[:, kc, w * 512 + h * 32:w * 512 + h * 32 + 32],
                                       rhs=cqnT[:, kc, 0:T], start=(kc == 0), stop=(kc == 3))
                return ins
            S.emit("pe", f, reads=[wkey, "cqnT"], writes=[PK(bank)])
            S.emit("dve", lambda E, bank=bank: E.tensor_tensor(out=krf[:, 0, 0:T], in0=ps[bank][0:32, 0:T], in1=cst[:, 0, 0:T], op=ALU.mult),
                   reads=[PK(bank), "cst"], writes=["krf0"])
            S.emit("dve", lambda E, bank=bank: E.tensor_tensor(out=krf[:, 1, 0:T], in0=ps[bank][0:32, T:2 * T], in1=cst[:, 1, 0:T], op=ALU.mult),
                   reads=[PK(bank), "cst"], writes=["krf1"])
            S.emit("dve", lambda E, h=h: E.tensor_tensor(out=qrT[:, h, 0:T], in0=krf[:, 0, 0:T], in1=krf[:, 1, 0:T], op=ALU.add),
                   reads=["krf0", "krf1"], writes=["qrT"])
        WS.done()
        def emit_qlat(i):
            QL = qlT[i % 2]
            qk = ("qlT", i % 2)
            for e in range(2):
                for j in range(2):
                    bank = dbank()
                    S.emit("pe", lambda E, e=e, j=j, bank=bank, i=i: E.matmul(
                        ps[bank][:, 0:T], lhsT=wuk[64 * e:64 * e + 64, i, j * 128:(j + 1) * 128], rhs=qnT[64 * e:64 * e + 64, i, 0:T],
                        start=True, stop=True), reads=["wuk", "qnT"], writes=[PK(bank)])
                    S.emit("act", lambda E, e=e, j=j, bank=bank, QL=QL: E.copy(out=QL[:, j, e, 0:T], in_=ps[bank][:, 0:T]),
                           reads=[PK(bank)], writes=[qk])

        def emit_wuv(i):
            OL = olT[i % 2]
            ok = ("olT", i % 2)
            bank = dbank()

            def f(E, i=i, bank=bank, OL=OL):
                n = 0
                for e in range(2):
                    for j in range(2):
                        ins = E.matmul(ps[bank][:, 0:T], lhsT=wuv[:, j, 2 * i + e, :], rhs=OL[:, j, e, 0:T], start=(n == 0), stop=(n == 3))
                        n += 1
                return ins
            S.emit("pe", f, reads=["wuv", ok], writes=[PK(bank)])
            S.emit("act", lambda E, i=i, bank=bank: E.copy(out=oT[:, i, 0:T], in_=ps[bank][:, 0:T]), reads=[PK(bank)], writes=["oT"])

        pti = [0]
        pai = [0]
        pend_fin = []
        emit_qlat(0)
        for i in range(8):
            QL = qlT[i % 2]
            OL = olT[i % 2]
            qk, ok = ("qlT", i % 2), ("olT", i % 2)
            if i + 1 < 8:
                emit_qlat(i + 1)
            for sgi, sg in enumerate(segs):
                c0, L, ctx = sg["c0"], sg["L"], sg["ctx"]
                cl, ck, ct, ci = ctx
                blocks = sg["blocks"]
                nb = len(blocks)
                slots = []
                PA = pacc[pai[0] % 2]
                pak = ("pacc", pai[0] % 2)
                pai[0] += 1

                def emit_qk(bi):
                    blk, key0, nk, qa, diag = blocks[bi]
                    sbank = 3 + (pti[0] % 2)
                    P = pT[pti[0] % 3]
                    pk = ("pT", pti[0] % 3)
                    pti[0] += 1
                    n = L - qa
                    qs = slice(c0 + qa, c0 + L)
                    sv = ps[sbank][0:nk, 0:2 * n].rearrange("p (e q) -> p e q", e=2)

                    def f(E, cl=cl, ck=ck, QL=QL, i=i):
                        E.matmul(sv, lhsT=cl[:, 0, key0:key0 + nk], rhs=QL[:, 0, :, qs], start=True, stop=False)
                        E.matmul(sv, lhsT=cl[:, 1, key0:key0 + nk], rhs=QL[:, 1, :, qs], start=False, stop=False)
                        return E.matmul(sv, lhsT=ck[:, key0:key0 + nk], rhs=qrT[:, 2 * i:2 * i + 2, qs], start=False, stop=True)
                    S.emit("pe", f, reads=[("latT", ci), ("krT", ci), qk, "qrT"], writes=[PK(sbank)])
                    S.emit("act", lambda E: E.activation(out=P[0:nk, 0:2 * n], in_=ps[sbank][0:nk, 0:2 * n], func=AF.Exp, scale=SCALE),
                           reads=[PK(sbank)], writes=[pk])
                    if diag:
                        S.emit("pool", lambda E: E.memset(P[64:128, 0:2 * n].rearrange("p (e q) -> p e q", e=2)[:, :, 0:64], 0.0),
                               reads=[pk], writes=[pk])
                    slots.append((P, pk, n))

                def emit_pv(bi):
                    blk, key0, nk, qa, diag = blocks[bi]
                    P, pk, n = slots[bi]
                    first, last = (bi == 0), (bi == nb - 1)
                    pv = P[0:nk, 0:2 * n].rearrange("p (e q) -> p e q", e=2)

                    def f(E, ct=ct, L=L):
                        o0 = ps[5][:, 0:2 * L].rearrange("p (e q) -> p e q", e=2)[:, :, qa:qa + n]
                        o1 = ps[6][:, 0:2 * L].rearrange("p (e q) -> p e q", e=2)[:, :, qa:qa + n]
                        E.matmul(o0, lhsT=ct[0:nk, blk, 0:128], rhs=pv, start=first, stop=last, skip_group_check=True)
                        return E.matmul(o1, lhsT=ct[0:nk, blk, 128:256], rhs=pv, start=first, stop=last, skip_group_check=True)
                    S.emit("pe", f, reads=[("ltok", ci), pk], writes=[PK(5), PK(6)])
                    pav = PA[0:nk, 0:2 * L].rearrange("p (e q) -> p e q", e=2)[:, :, qa:qa + n]
                    if first:
                        S.emit("dve", lambda E, pav=pav, pv=pv: E.tensor_copy(out=pav, in_=pv), reads=[pk], writes=[pak])
                    else:
                        S.emit("dve", lambda E, pav=pav, pv=pv: E.tensor_tensor(out=pav, in0=pav, in1=pv, op=ALU.add), reads=[pk, pak], writes=[pak])

                emit_qk(0)
                if nb > 1:
                    emit_qk(1)
                if pend_fin:
                    pend_fin.pop()()
                for bi in range(nb):
                    if bi >= 1 and bi + 1 < nb:
                        emit_qk(bi + 1)
                    emit_pv(bi)
                    if bi == min(2, nb - 1) and sgi == 0 and i > 0:
                        emit_wuv(i - 1)
                S.emit("act", lambda E, L=L: E.copy(out=osb[:, 0, 0:2 * L], in_=ps[5][:, 0:2 * L]), reads=[PK(5)], writes=["osb0"])
                S.emit("dve", lambda E, L=L: E.tensor_copy(out=osb[:, 1, 0:2 * L], in_=ps[6][:, 0:2 * L]), reads=[PK(6)], writes=["osb1"])

                def fin(L=L, c0=c0, OL=OL, PA=PA, pak=pak, ok=ok):
                    S.emit("pe", lambda E: E.matmul(ps[7][:, 0:2 * L], lhsT=onesf[:, :], rhs=PA[:, 0:2 * L], start=True, stop=True),
                           reads=[pak, "consts"], writes=[PK(7)])
                    S.emit("act", lambda E: E.copy(out=osb[:, 2, 0:2 * L], in_=ps[7][:, 0:2 * L]), reads=[PK(7)], writes=["osb2"])
                    S.emit("dve", lambda E: E.reciprocal(out=osb[:, 2, 0:2 * L], in_=osb[:, 2, 0:2 * L]), reads=["osb2"], writes=["osb2"])
                    for j in range(2):
                        S.emit("dve" if j == 0 else "pool", lambda E, j=j: E.tensor_tensor(
                            out=OL[:, j, :, c0:c0 + L], in0=osb[:, j, 0:2 * L].rearrange("p (e q) -> p e q", e=2),
                            in1=osb[:, 2, 0:2 * L].rearrange("p (e q) -> p e q", e=2), op=ALU.mult),
                            reads=["osb%d" % j, "osb2"], writes=[ok])
                pend_fin.append(fin)
        if pend_fin:
            pend_fin.pop()()
        emit_wuv(7)
        for b in range(2):
            def ev(oc, bank, b=b):
                S.emit("act", lambda E: E.copy(out=mT[:, b * 4 + oc, 0:T], in_=ps[bank][:, 0:T]), reads=[PK(bank)], writes=[("mT", b * 4 + oc)])
            dense_ws("wo", 8, 512, 512, T, lambda kc: oT[:, kc, 0:T], ["oT"], ev)
        postnorm_residual(T, ln_col(3))
        ffn(1, T, hidT2)
        final_fn()

    def out_sample():
        S.emit("pool", lambda E: E.dma_start(out=slatT_o.rearrange("j p t -> p j t"), in_=latn[:, :, 0:2 * LS]), reads=["latn"], writes=["o_slat"], dma=True)
        S.emit("pool", lambda E: E.dma_start(out=skrT_o[:, :], in_=krf[:, 2, 0:2 * LS]), reads=["krf2"], writes=["o_skr"], dma=True)

    def fin_sample():
        S.emit("pool", lambda E: E.dma_start(out=ysT_o.rearrange("k p t -> p k t"), in_=hT[:, :, 0:2 * LS]), reads=HK, writes=["o_ys"], dma=True)
    segs = []
    for si in range(2):
        blocks = [(b, b * 128, 128, 0, False) for b in range(8)] + [(8, PAST, LS, 0, False)]
        segs.append(dict(c0=si * LS, L=LS, ctx=(s_latT[si], s_krT[si], s_lat_tok[si], si), kpos=PAST, blocks=blocks))
    pass2_tile(2 * LS, segs, NT, css_d[:, :, :], out_sample, fin_sample)

    for j in range(NT):
        cols_j = slice(j * TT, (j + 1) * TT)

        def out_p(cols_j=cols_j):
            S.emit("pool", lambda E: E.dma_start(out=platT_o.rearrange("j p t -> p j t")[:, :, cols_j], in_=latn[:, :, :]),
                   reads=["latn"], writes=[("o_plat", cols_j.start)], dma=True)
            S.emit("pool", lambda E: E.dma_start(out=pkrT_o[:, cols_j], in_=krf[:, 2, :]), reads=["krf2"], writes=[("o_pkr", cols_j.start)], dma=True)

        def fin_p(cols_j=cols_j):
            S.emit("pool", lambda E: E.dma_start(out=yT_o.rearrange("k p t -> p k t")[:, :, cols_j], in_=hT[:, :, :]),
                   reads=HK, writes=[("o_y", cols_j.start)], dma=True)
        blocks = [(b, b * 128, 128, 0, False) for b in range(2 * j)]
        blocks += [(2 * j, 2 * j * 128, 128, 0, True), (2 * j + 1, (2 * j + 1) * 128, 128, 128, True)]
        segs = [dict(c0=0, L=TT, ctx=(latT, krT, lat_tok, 2), kpos=j * TT, blocks=blocks)]
        pass2_tile(TT, segs, j, cs_d[:, :, cols_j], out_p, fin_p)
        if j % 4 == 3 and j != NT - 1:
            S.new_epoch()

    S.barrier()
    replay_all()
    es2.close()
    es.close()
    return nc


def _blk(W, KC, c0, NB):
    K, N = W.shape
    nv = min(NB, N - c0)
    out = np.zeros((128, KC, NB), np.float32)
    out[:, :, :nv] = W[:, c0:c0 + nv].reshape(KC, 128, nv).transpose(1, 0, 2)
    res = np.zeros((128, WB), np.float32)
    res[:, :KC * NB] = out.reshape(128, KC * NB)
    return res


def _colvec(v):
    n = v.shape[0] // 128
    return np.ascontiguousarray(v.reshape(n, 128).T)


def _prep_shared(inp, SEQ):
    f = np.float32
    w_in = inp["ssd_w_in"][0]
    perm = (np.arange(32) + 16) % 32
    blocks0 = []
    for b in range(8):
        blocks0.append(_blk(w_in[:, 2048:6144], 8, b * 512, 512))
    for b in range(4):
        blocks0.append(_blk(w_in[:, 0:2048], 8, b * 512, 512))
    for b in range(4):
        blocks0.append(_blk(inp["ssd_w_out"][0], 16, b * 256, 256))

    def ffn_blocks(l):
        r = []
        for b in range(6):
            r.append(_blk(inp["ffn_w_gate"][l], 8, b * 512, 512))
        for b in range(6):
            r.append(_blk(inp["ffn_w_up"][l], 8, b * 512, 512))
        for b in range(8):
            r.append(_blk(inp["ffn_w_down"][l], 22, b * 128, 128))
        return r
    blocks0 += ffn_blocks(0)
    blocks1 = [_blk(inp["mla_wq_a"][0], 8, 0, 512)]
    wkv = inp["mla_wkv_a"][0]
    wkv2 = np.concatenate([wkv[:, :256], wkv[:, 256:288], wkv[:, 256:288][:, perm]], axis=1)
    blocks1.append(_blk(wkv2, 8, 0, 320))
    wqb = inp["mla_wq_b"][0].reshape(512, 16, 96)
    blocks1.append(_blk(np.ascontiguousarray(wqb[:, :, :64]).reshape(512, 1024), 4, 0, 1024))
    rope = wqb[:, :, 64:]
    blocks1.append(_blk(np.concatenate([rope.reshape(512, 512), rope[:, :, perm].reshape(512, 512)], axis=1), 4, 0, 1024))
    for b in range(2):
        blocks1.append(_blk(inp["mla_w_o"][0], 8, b * 512, 512))
    blocks1 += ffn_blocks(1)
    assert len(blocks0) == NB0 and len(blocks1) == NB1
    w0 = np.concatenate(blocks0, axis=0)
    w1 = np.concatenate(blocks1, axis=0)

    cols = np.zeros((128, NCOLS), f)
    lns = [inp["ln_mix_pre"][0], inp["ln_mix_pre"][1], inp["ln_mix_post"][0], inp["ln_mix_post"][1],
           inp["ln_ffn_pre"][0], inp["ln_ffn_pre"][1], inp["ln_ffn_post"][0], inp["ln_ffn_post"][1]]
    for i, v in enumerate(lns):
        cols[:, C_LN + 8 * i:C_LN + 8 * i + 8] = _colvec(v)
    cw = inp["ssd_conv_w"][0]
    cols[:, C_CW:C_CW + 128] = cw.T.reshape(32, 128, 4).transpose(1, 0, 2).reshape(128, 128)
    cols[:, C_CB:C_CB + 32] = _colvec(inp["ssd_conv_b"][0])
    cols[:, C_QN:C_QN + 4] = _colvec(inp["mla_q_norm"][0])
    cols[:, C_KN:C_KN + 2] = _colvec(inp["mla_kv_norm"][0])
    rows = np.zeros((128, NROWS), f)
    rows[:, R_DTB:R_DTB + 32] = inp["ssd_dt_bias"][0][None, :]
    rows[:, R_ALOG:R_ALOG + 32] = inp["ssd_a_log"][0][None, :]
    rows[:, R_D:R_D + 32] = inp["ssd_d"][0][None, :]
    rows[:, R_GN:R_GN + 2048] = inp["ssd_gate_norm"][0][None, :]
    wdt = np.ascontiguousarray(w_in[:, 6144:6176].reshape(8, 128, 32).transpose(1, 0, 2)).reshape(128, 256)
    wuk_src = inp["mla_w_uk"][0]
    wuk = wuk_src.transpose(1, 2, 0).reshape(8, 2, 64, 256).transpose(1, 2, 0, 3).reshape(128, 8 * 256)
    wuv_src = inp["mla_w_uv"][0]
    wuv = np.zeros((128, 2, 16, 128), f)
    for h in range(16):
        for j in range(2):
            wuv[:, j, h, (h % 2) * 64:(h % 2) * 64 + 64] = wuv_src[j * 128:(j + 1) * 128, h, :]
    wuv = wuv.reshape(128, 2 * 16 * 128)

    def rope_tab(pos):
        half = 16
        inv = (np.float32(10000.0) ** (-np.arange(half, dtype=np.float32) / np.float32(half))).astype(np.float32)
        ang = pos.astype(np.float32)[:, None] * inv[None, :]
        c, s_ = np.cos(ang).astype(np.float32), np.sin(ang).astype(np.float32)
        tab = np.zeros((32, 2, pos.shape[0]), np.float32)
        tab[:16, 0] = c.T
        tab[16:, 0] = c.T
        tab[:16, 1] = -s_.T
        tab[16:, 1] = s_.T
        return tab
    cs = rope_tab(np.arange(SEQ))
    c1 = rope_tab(PAST + np.arange(LS))
    css = np.concatenate([c1, c1], axis=2)
    consts = np.zeros((128, 512), f)
    i = np.arange(128)
    consts[:, 0:128] = (i[:, None] == i[None, :])
    consts[:, 128:256] = (i[:, None] <= i[None, :])
    consts[:, 256:384] = (i[:, None] > i[None, :])
    consts[:, 384:512] = 1.0
    return dict(w0=w0, w1=w1, cols=cols, rows=rows, wdt=wdt, wuk=np.ascontiguousarray(wuk), wuv=wuv, cs=cs, css=css, consts=consts)


_NT_OVERRIDE = [None]


def kernel(**inputs):
    inp = {k: np.asarray(v) for k, v in inputs.items()}
    B = inp["x_prompt"].shape[0]
    SEQ = inp["x_prompt"].shape[1]
    NT = SEQ // TT
    shared = _prep_shared(inp, SEQ)
    in_maps = []
    for b in range(B):
        m = dict(shared)
        m["xT"] = np.ascontiguousarray(inp["x_prompt"][b].T).reshape(8, 128, SEQ)
        xs = inp["x_sample"][2 * b:2 * b + 2].reshape(2 * LS, D_MODEL)
        m["xsT"] = np.ascontiguousarray(xs.T).reshape(8, 128, 2 * LS)
        cst = inp["state_ssd_conv"][0, 2 * b:2 * b + 2]
        m["convst"] = np.ascontiguousarray(cst.transpose(0, 2, 1).reshape(2, 32, 128, 3).transpose(0, 2, 1, 3)).reshape(2, 128, 96)
        sst = inp["state_ssd_ssm"][0, 2 * b:2 * b + 2]
        m["ssmst"] = np.ascontiguousarray(sst.transpose(0, 3, 1, 2)).reshape(2, 128, 2048)
        cl = inp["cache_mla_latent"][0, 2 * b:2 * b + 2]
        m["clat"] = np.ascontiguousarray(cl).reshape(2, 8, 128, 256)
        m["clatT"] = np.ascontiguousarray(cl.transpose(0, 2, 1)).reshape(2, 2, 128, PAST)
        ck = inp["cache_mla_krope"][0, 2 * b:2 * b + 2]
        m["ckrT"] = np.ascontiguousarray(ck.transpose(0, 2, 1))
        in_maps.append(m)
    nc = build_program(NT)
    res = run_bass_kernel_spmd(nc, in_maps, core_ids=list(range(B)))
    f = np.float32
    y_prompt = np.zeros((B, SEQ, D_MODEL), f)
    y_sample = np.zeros((2 * B, LS, D_MODEL), f)
    p_conv = np.zeros((1, B, 3, 4096), f)
    p_ssm = np.zeros((1, B, 32, 64, 128), f)
    p_lat = np.zeros((1, B, SEQ, 256), f)
    p_kr = np.zeros((1, B, SEQ, 32), f)
    s_conv = np.zeros((1, 2 * B, 3, 4096), f)
    s_ssm = np.zeros((1, 2 * B, 32, 64, 128), f)
    s_lat = np.zeros((1, 2 * B, LS, 256), f)
    s_kr = np.zeros((1, 2 * B, LS, 32), f)
    for b in range(B):
        r = res.results[b]
        y_prompt[b] = r["yT"].reshape(1024, SEQ).T
        ys = r["ysT"].reshape(1024, 2 * LS).T
        p_conv[0, b] = r["pconv"].reshape(128, 32, 3).transpose(2, 1, 0).reshape(3, 4096)
        p_ssm[0, b] = r["pssm"].T.reshape(32, 64, 128)
        p_lat[0, b] = r["platT"].reshape(256, SEQ).T
        p_kr[0, b] = r["pkrT"].T
        sl = r["slatT"].reshape(256, 2 * LS).T
        sk = r["skrT"].T
        for si in range(2):
            y_sample[2 * b + si] = ys[si * LS:(si + 1) * LS]
            s_conv[0, 2 * b + si] = r["sconv"][si].reshape(128, 32, 3).transpose(2, 1, 0).reshape(3, 4096)
            s_ssm[0, 2 * b + si] = r["sssm"][si].T.reshape(32, 64, 128)
            s_lat[0, 2 * b + si] = sl[si * LS:(si + 1) * LS]
            s_kr[0, 2 * b + si] = sk[si * LS:(si + 1) * LS]
    return (y_prompt, y_sample, p_conv, p_ssm, p_lat, p_kr, s_conv, s_ssm, s_lat, s_kr)
```

```python
import math
import numpy as np
import concourse.bass as bass
import concourse.mybir as mybir
from concourse.bass_utils import run_bass_kernel_spmd

F32 = mybir.dt.float32
BF16 = mybir.dt.bfloat16
AF = mybir.ActivationFunctionType
ALU = mybir.AluOpType
AX = mybir.AxisListType

D_MODEL = 1024
SEQ_FULL = 8192
PAST = 1024
LS = 32
TT = 256
FFN = 2816
EPS = 1e-6
SCALE = 1.0 / math.sqrt(96.0)
NSLOT = 3
WB = 4096

L0_BLOCKS = ([("xbc", 8, 512)] * 8 + [("z", 8, 512)] * 4 + [("wout", 16, 256)] * 4
             + [("gate", 8, 512)] * 6 + [("up", 8, 512)] * 6 + [("down", 22, 128)] * 8)
L1_BLOCKS = ([("wqa", 8, 512)] + [("wkva", 8, 320)] + [("wqb", 4, 1024)] * 2 + [("wo", 8, 512)] * 2
             + [("gate", 8, 512)] * 6 + [("up", 8, 512)] * 6 + [("down", 22, 128)] * 8)
NB0, NB1 = len(L0_BLOCKS), len(L1_BLOCKS)

C_LN = 0
C_CW = 64
C_CB = 192
C_QN = 224
C_KN = 228
NCOLS = 232
R_DTB, R_ALOG, R_D, R_GN = 0, 32, 64, 96
NROWS = 96 + 2048


class Sched:
    ENG = ("pe", "act", "dve", "pool", "sp")

    def __init__(self, nc, ndma=14):
        self.nc = nc
        self.ops = {e: [] for e in self.ENG}
        self.count = {e: 0 for e in self.ENG}
        self.waited = {e: {} for e in self.ENG}
        self.res = {}
        self.ndma = ndma
        self.epoch = 0
        self.nepoch = 12
        self.dma_val = {}
        self.dma_rr = {"sp": 0, "pool": 0, "act": 0}
        for q in ("sp", "pool"):
            for k in range(ndma):
                self.dma_val["d_%s_%d" % (q, k)] = 0

    def sem_names(self):
        return ["%s@%d" % (e, k) for e in self.ENG[:4] for k in range(self.nepoch)] + list(self.dma_val.keys())

    def cname(self, eng):
        return "%s@%d" % (eng, self.epoch)

    def new_epoch(self):
        self.barrier()
        self.epoch += 1
        assert self.epoch < self.nepoch
        for e in self.ENG:
            self.count[e] = 0
            self.waited[e] = {}

    def emit(self, eng, fn, reads=(), writes=(), dma=False):
        deps = []
        for r in reads:
            st = self.res.get(r)
            if st is not None and st[0] is not None:
                deps.append(st[0])
        for w in writes:
            st = self.res.get(w)
            if st is not None:
                if st[0] is not None:
                    deps.append(st[0])
                deps.extend(st[1])
        if dma:
            k = self.dma_rr[eng]
            self.dma_rr[eng] = (k + 1) % self.ndma
            sname = "d_%s_%d" % (eng, k)
            if self.dma_val[sname] > 0:
                deps.append((sname, self.dma_val[sname]))
            self.dma_val[sname] += 16
            token = (sname, self.dma_val[sname])
        else:
            self.count[eng] += 1
            token = (self.cname(eng), self.count[eng])
        waits = {}
        for (s, v) in deps:
            if s == self.cname(eng) and eng in ("pe", "sp"):
                continue
            if self.waited[eng].get(s, 0) >= v:
                continue
            if waits.get(s, 0) < v:
                waits[s] = v
        for s, v in waits.items():
            self.waited[eng][s] = v
        self.ops[eng].append((list(waits.items()), fn, token, dma))
        for w in writes:
            self.res[w] = [token, []]
        for r in reads:
            st = self.res.get(r)
            if st is None:
                self.res[r] = [None, [token]]
            else:
                st[1].append(token)
        return token

    def barrier(self):
        toks = [(self.cname(e), self.count[e]) for e in self.ENG if self.count[e] > 0]
        toks += [(s, v) for s, v in self.dma_val.items() if v > 0]
        for e in self.ENG:
            waits = []
            for (s, v) in toks:
                if s == self.cname(e):
                    continue
                if self.waited[e].get(s, 0) >= v:
                    continue
                self.waited[e][s] = v
                waits.append((s, v))
            if waits:
                self.ops[e].append((waits, None, None, False))
        self.res = {}

    def check(self, semvals):
        pos = {e: 0 for e in self.ENG}
        progress = True
        while progress:
            progress = False
            for e in self.ENG:
                while pos[e] < len(self.ops[e]):
                    waits, fn, token, dma = self.ops[e][pos[e]]
                    if any(semvals.get(s, 0) < v for s, v in waits):
                        break
                    if fn is not None:
                        semvals[token[0]] = semvals.get(token[0], 0) + (16 if dma else 1)
                        assert semvals[token[0]] == token[1], (e, pos[e], token, semvals[token[0]])
                    pos[e] += 1
                    progress = True
        stuck = {e: (pos[e], len(self.ops[e])) for e in self.ENG if pos[e] < len(self.ops[e])}
        for e in stuck:
            waits, fn, token, dma = self.ops[e][pos[e]]
            print("STUCK", e, pos[e], [(s, v, semvals.get(s, 0)) for s, v in waits if semvals.get(s, 0) < v], token)
        return not stuck

    def replay(self, eng, E, sems):
        for waits, fn, token, dma in self.ops[eng]:
            for s, v in waits:
                E.wait_ge(sems[s], v)
            if fn is None:
                continue
            ins = fn(E)
            ins.then_inc(sems[token[0]], 16 if dma else 1)


def build_program(NT):
    SEQ = NT * TT
    nc = bass.Bass("TRN2", target_bir_lowering=False)
    S = Sched(nc)

    def din(name, shape, dt=F32):
        return nc.dram_tensor(name, list(shape), dt, kind="ExternalInput").ap()

    def dout(name, shape, dt=F32):
        return nc.dram_tensor(name, list(shape), dt, kind="ExternalOutput").ap()

    xT = din("xT", [8, 128, SEQ])
    xsT = din("xsT", [8, 128, 2 * LS])
    w0 = din("w0", [NB0 * 128, WB])
    w1 = din("w1", [NB1 * 128, WB])
    cols_d = din("cols", [128, NCOLS])
    rows_d = din("rows", [128, NROWS])
    wdt_d = din("wdt", [128, 8 * 32])
    wuk_d = din("wuk", [128, 8 * 256])
    wuv_d = din("wuv", [128, 2 * 16 * 128])
    cs_d = din("cs", [32, 2, SEQ])
    css_d = din("css", [32, 2, 2 * LS])
    consts_d = din("consts", [128, 512])
    convst_d = din("convst", [2, 128, 96])
    ssmst_d = din("ssmst", [2, 128, 2048])
    clat_d = din("clat", [2, 8, 128, 256])
    clatT_d = din("clatT", [2, 2, 128, PAST])
    ckrT_d = din("ckrT", [2, 32, PAST])

    yT_o = dout("yT", [8, 128, SEQ])
    ysT_o = dout("ysT", [8, 128, 2 * LS])
    pconv_o = dout("pconv", [128, 96])
    pssm_o = dout("pssm", [128, 2048])
    platT_o = dout("platT", [2, 128, SEQ])
    pkrT_o = dout("pkrT", [32, SEQ])
    sconv_o = dout("sconv", [2, 128, 96])
    sssm_o = dout("sssm", [2, 128, 2048])
    slatT_o = dout("slatT", [2, 128, 2 * LS])
    skrT_o = dout("skrT", [32, 2 * LS])

    wb0 = nc.dram_tensor("wb0", [NB0 * 128, WB], BF16).ap()
    wb1 = nc.dram_tensor("wb1", [NB1 * 128, WB], BF16).ap()
    hs = nc.dram_tensor("hs", [NT + 1, 128, 8 * TT], F32).ap()

    from contextlib import ExitStack
    es = ExitStack()

    def sb(name, shape, dt=F32):
        return es.enter_context(nc.sbuf_tensor("sb_" + name, list(shape), dt))

    ps = [es.enter_context(nc.psum_tensor("ps%d" % i, [128, 512], F32)) for i in range(8)]
    psb = [p.bitcast(BF16) for p in ps]
    sems = {}
    for n in S.sem_names():
        sems[n] = es.enter_context(nc.semaphore("s_" + n))

    def PK(i):
        return ("ps", i)

    semvals = {}

    def replay_all():
        import os
        if os.environ.get("KCHECK"):
            print("deadlock check ok:", S.check(semvals))
        with nc.Block() as block:
            @block.tensor
            def _(E):
                S.replay("pe", E, sems)

            @block.scalar
            def _(E):
                S.replay("act", E, sems)

            @block.vector
            def _(E):
                S.replay("dve", E, sems)

            @block.gpsimd
            def _(E):
                S.replay("pool", E, sems)

            @block.sync
            def _(E):
                S.replay("sp", E, sems)
        for e in S.ENG:
            S.ops[e] = []

    consts = sb("consts", [128, 512])
    identb = sb("identb", [128, 128], BF16)
    onesb = sb("onesb", [128, 128], BF16)
    cols = sb("cols", [128, NCOLS])
    epsc = sb("epsc", [128, 1])
    wring = [sb("wring%d" % i, [128, WB], BF16) for i in range(NSLOT)]
    hT = sb("hT", [128, 8, TT])
    aT = sb("aT", [128, 8, TT], BF16)
    mT = sb("mT", [128, 8, TT])
    sq = sb("sq", [128, 8, TT], BF16)
    rstd = sb("rstd", [128, TT])
    triu = consts[:, 128:256]
    lstr = consts[:, 256:384]
    onesf = consts[:, 384:512]

    S.emit("pool", lambda E: E.dma_start(out=consts[:], in_=consts_d[:, :]), writes=["consts"], dma=True)
    S.emit("pool", lambda E: E.dma_start(out=identb[:], in_=consts_d[:, 0:128]), writes=["identb"], dma=True)
    S.emit("pool", lambda E: E.dma_start(out=onesb[:], in_=consts_d[:, 384:512]), writes=["onesb"], dma=True)
    S.emit("pool", lambda E: E.dma_start(out=cols[:], in_=cols_d[:, :]), writes=["cols"], dma=True)
    S.emit("pool", lambda E: E.memset(epsc[:], EPS), writes=["epsc"])
    for (src, dst, nb) in ((w0, wb0, NB0), (w1, wb1, NB1)):
        for b0 in range(0, nb, 2):
            b1 = min(nb, b0 + 2)
            S.emit("pool", lambda E, src=src, dst=dst, b0=b0, b1=b1: E.dma_start(
                out=dst[b0 * 128:b1 * 128, :], in_=src[b0 * 128:b1 * 128, :]),
                writes=[("wb", id(dst) % 1000, b) for b in range(b0, b1)], dma=True)

    class WStream:
        def __init__(self):
            self.seq = []
            self.issued = 0
            self.used = 0

        def add(self, dram, tag, nblocks, reps):
            for _ in range(reps):
                for b in range(nblocks):
                    self.seq.append((dram, b, ("wb", tag, b)))

        def issue(self):
            if self.issued >= len(self.seq):
                return
            dram, b, key = self.seq[self.issued]
            slot = self.issued % NSLOT
            S.emit("sp", lambda E, dram=dram, b=b, slot=slot: E.dma_start(
                out=wring[slot][:], in_=dram[b * 128:(b + 1) * 128, :]),
                reads=[key], writes=[("ws", slot)], dma=True)
            self.issued += 1

        def start(self):
            for _ in range(NSLOT):
                self.issue()

        def next(self, name, KC, NB):
            i = self.used
            slot = i % NSLOT
            self.used += 1
            view = wring[slot][:, 0:KC * NB].rearrange("p (k n) -> p k n", k=KC)
            return view, ("ws", slot)

        def done(self):
            self.issue()

    WS = WStream()
    WS.add(wb0, id(wb0) % 1000, NB0, NT + 1)
    WS.add(wb1, id(wb1) % 1000, NB1, NT + 1)

    dense_rr = [0]

    def dbank():
        dense_rr[0] ^= 1
        return dense_rr[0]

    def ln_col(idx):
        return cols[:, C_LN + idx * 8: C_LN + idx * 8 + 8]

    HK = [("hT", k) for k in range(8)]
    MK = [("mT", k) for k in range(8)]

    def compute_rstd(src, srckeys, KC, D, T):
        S.emit("act", lambda E: E.activation(out=sq[:, 0:KC, 0:T], in_=src, func=AF.Square),
               reads=list(srckeys), writes=["sq"])

        def f(E):
            for kc in range(KC):
                ins = E.matmul(ps[2][:, 0:T], lhsT=onesb[:, :], rhs=sq[:, kc, 0:T], start=(kc == 0), stop=(kc == KC - 1))
            return ins
        S.emit("pe", f, reads=["sq", "onesb"], writes=[PK(2)])
        S.emit("act", lambda E: E.activation(out=rstd[:, 0:T], in_=ps[2][:, 0:T], func=AF.Sqrt, bias=epsc[:], scale=1.0 / D),
               reads=[PK(2), "epsc"], writes=["rstd"])
        S.emit("dve", lambda E: E.reciprocal(out=rstd[:, 0:T], in_=rstd[:, 0:T]), reads=["rstd"], writes=["rstd"])

    def norm_to(src3, srckeys, KC, D, T, gcols, dst3, dstkey):
        compute_rstd(src3[:, 0:KC, 0:T], srckeys, KC, D, T)
        for kc in range(KC):
            S.emit("dve", lambda E, kc=kc: E.scalar_tensor_tensor(
                out=dst3[:, kc, 0:T], in0=src3[:, kc, 0:T], scalar=gcols[:, kc:kc + 1], in1=rstd[:, 0:T],
                op0=ALU.mult, op1=ALU.mult), reads=list(srckeys) + ["rstd", "cols"], writes=[dstkey])

    def postnorm_residual(T, gcols):
        compute_rstd(mT[:, :, 0:T], MK, 8, D_MODEL, T)
        for kc in range(8):
            S.emit("dve", lambda E, kc=kc: E.scalar_tensor_tensor(
                out=mT[:, kc, 0:T], in0=mT[:, kc, 0:T], scalar=gcols[:, kc:kc + 1], in1=rstd[:, 0:T],
                op0=ALU.mult, op1=ALU.mult), reads=[("mT", kc), "rstd", "cols"], writes=[("mT", kc)])
            S.emit("pool", lambda E, kc=kc: E.tensor_tensor(
                out=hT[:, kc, 0:T], in0=hT[:, kc, 0:T], in1=mT[:, kc, 0:T], op=ALU.add),
                reads=[("mT", kc), ("hT", kc)], writes=[("hT", kc)])

    def dense_ws(name, KC, NB, nvalid, T, rhs_fn, rhskeys, evac):
        view, wkey = WS.next(name, KC, NB)
        for oc in range(nvalid // 128):
            bank = dbank()

            def f(E, oc=oc, bank=bank):
                for kc in range(KC):
                    ins = E.matmul(ps[bank][:, 0:T], lhsT=view[:, kc, oc * 128:(oc + 1) * 128], rhs=rhs_fn(kc),
                                   start=(kc == 0), stop=(kc == KC - 1))
                return ins
            S.emit("pe", f, reads=[wkey] + rhskeys, writes=[PK(bank)])
            evac(oc, bank)
        WS.done()

    def ffn(layer, T, hidT):
        norm_to(hT, HK, 8, D_MODEL, T, ln_col(4 + layer), aT, "aT")
        for b in range(6):
            nv = 512 if b < 5 else 256

            def ev(oc, bank, b=b):
                j = b * 4 + oc
                S.emit("act", lambda E: E.activation(out=hidT[:, j, 0:T], in_=ps[bank][:, 0:T], func=AF.Silu),
                       reads=[PK(bank)], writes=[("hid", j)])
            dense_ws("gate", 8, 512, nv, T, lambda kc: aT[:, kc, 0:T], ["aT"], ev)
        for b in range(6):
            nv = 512 if b < 5 else 256

            def ev(oc, bank, b=b):
                j = b * 4 + oc
                S.emit("dve", lambda E: E.tensor_tensor(out=hidT[:, j, 0:T], in0=ps[bank][:, 0:T], in1=hidT[:, j, 0:T], op=ALU.mult),
                       reads=[PK(bank), ("hid", j)], writes=[("hid", j)])
            dense_ws("up", 8, 512, nv, T, lambda kc: aT[:, kc, 0:T], ["aT"], ev)
        for b in range(8):
            def ev(oc, bank, b=b):
                S.emit("act", lambda E: E.copy(out=mT[:, b, 0:T], in_=ps[bank][:, 0:T]), reads=[PK(bank)], writes=[("mT", b)])
            dense_ws("down", 22, 128, 128, T, lambda kc: hidT[:, kc, 0:T], [("hid", j) for j in range(22)], ev)
        postnorm_residual(T, ln_col(6 + layer))

    es1 = ExitStack()

    def sb1(name, shape, dt=F32):
        return es1.enter_context(nc.sbuf_tensor("s1_" + name, list(shape), dt))

    rows = sb1("rows", [128, NROWS])
    abc = sb1("abc", [128, 32])
    wdt = sb1("wdt", [128, 8, 32], BF16)
    Sst = [sb1("Sst%d" % i, [128, 2048]) for i in range(2)]
    Sbf = [sb1("Sbf%d" % i, [128, 2048], BF16) for i in range(2)]
    tail = [sb1("tail%d" % i, [128, 32, 3]) for i in range(2)]
    xbcT = sb1("xbcT", [128, 32, TT], BF16)
    pre = [sb1("pre%d" % i, [128, 2 * (LS + 3) if False else TT + 8]) for i in range(2)]
    cacc = [sb1("cacc%d" % i, [128, TT]) for i in range(2)]
    cq2 = [sb1("cq2%d" % i, [128, TT]) for i in range(2)]
    cq3 = [sb1("cq3%d" % i, [128, TT]) for i in range(2)]
    sz = sb1("sz", [128, 2, 2048], BF16)
    dtb = sb1("dtb", [128, 2, 32])
    dtA = sb1("dtA", [128, 2, 32])
    x_tok = [sb1("x_tok%d" % i, [128, 2048], BF16) for i in range(2)]
    B_tok = [sb1("B_tok%d" % i, [128, 1024], BF16) for i in range(2)]
    Rm = [sb1("Rm%d" % i, [128, 512]) for i in range(2)]
    decay = [sb1("decay%d" % i, [128, 512], BF16) for i in range(2)]
    MT = [sb1("MT%d" % i, [128, 512], BF16) for i in range(2)]
    cbm = [sb1("cbm%d" % i, [128, 128], BF16) for i in range(2)]
    xdt = [sb1("xdt%d" % i, [128, 256], BF16) for i in range(2)]
    xw = [sb1("xw%d" % i, [128, 256], BF16) for i in range(2)]
    xD = [sb1("xD%d" % i, [128, 256], BF16) for i in range(2)]
    tmp = [sb1("tmp%d" % i, [128, 256]) for i in range(2)]
    yall = [sb1("yall%d" % i, [128, 2048]) for i in range(2)]
    ysq = sb1("ysq", [128, 2048], BF16)
    yn = sb1("yn", [128, 2048], BF16)
    ynT = sb1("ynT", [128, 16, TT], BF16)
    eacum = [sb1("eacum%d" % i, [128, 32]) for i in range(2)]
    Elast = [sb1("Elast%d" % i, [128, 32]) for i in range(2)]
    ss8 = sb1("ss8", [128, 8])
    hidT1 = sb1("hidT1", [128, 22, TT], BF16)

    S.emit("pool", lambda E: E.dma_start(out=rows[:], in_=rows_d[:, :]), writes=["rows"], dma=True)
    S.emit("pool", lambda E: E.dma_start(out=wdt[:].rearrange("p k n -> p (k n)"), in_=wdt_d[:, :]), writes=["wdt"], dma=True)
    S.emit("act", lambda E: E.activation(out=abc[:], in_=rows[:, R_ALOG:R_ALOG + 32], func=AF.Exp), reads=["rows"], writes=["abc"])
    S.emit("dve", lambda E: E.tensor_scalar(out=abc[:], in0=abc[:], scalar1=-1.0, scalar2=None, op0=ALU.mult), reads=["abc"], writes=["abc"])
    WS.start()

    def SKs(si):
        return [(("S", si), g) for g in range(8)]

    def SBKs(si):
        return [(("Sbf", si), g) for g in range(8)]

    def ssd_chunk_gen(ci, c0, L, si):
        tok = slice(c0, c0 + L)
        cp = ci % 2
        SK, SBK = ("S", si), ("Sbf", si)
        XT, BT, EA, EL, YA = x_tok[cp], B_tok[cp], eacum[cp], Elast[cp], yall[cp]
        kXT, kBT, kEA, kEL = ("x_tok", cp), ("B_tok", cp), ("eacum", cp), ("Elast", cp)
        YK = [("yall", cp, g) for g in range(8)]
        for half in range(2):
            bank = dbank()

            def f(E, half=half, bank=bank):
                for k in range(8):
                    ins = E.transpose(psb[bank][0:L, k * 128:(k + 1) * 128], xbcT[:, half * 8 + k, tok], identb[:, :])
                return ins
            S.emit("pe", f, reads=["xbcT", "identb"], writes=[PK(bank)])
            S.emit("act", lambda E, half=half, bank=bank: E.copy(out=XT[0:L, half * 1024:(half + 1) * 1024], in_=psb[bank][0:L, 0:1024]),
                   reads=[PK(bank)], writes=[kXT])
        bank = dbank()

        def f(E, bank=bank):
            for k in range(8):
                ins = E.transpose(psb[bank][0:L, k * 128:(k + 1) * 128], xbcT[:, 16 + k, tok], identb[:, :])
            return ins
        S.emit("pe", f, reads=["xbcT", "identb"], writes=[PK(bank)])
        S.emit("act", lambda E, bank=bank: E.copy(out=BT[0:L, :], in_=psb[bank][0:L, 0:1024]), reads=[PK(bank)], writes=[kBT])

        def f(E):
            E.matmul(ps[2][0:L, 0:32], lhsT=triu[0:L, 0:L], rhs=dtA[0:L, ci, :], start=True, stop=True)
            return E.matmul(ps[2][:, 32:64], lhsT=onesf[0:L, :], rhs=dtA[0:L, ci, :], start=True, stop=True)
        S.emit("pe", f, reads=["dtA", "consts"], writes=[PK(2)])
        S.emit("act", lambda E: E.activation(out=EA[0:L, :], in_=ps[2][0:L, 0:32], func=AF.Exp), reads=[PK(2)], writes=[kEA])
        S.emit("act", lambda E: E.activation(out=EL[:, :], in_=ps[2][:, 32:64], func=AF.Exp), reads=[PK(2)], writes=[kEL])
        yield "pro"

        def unit_gen(g):
            u = g % 2
            hs4 = slice(4 * g, 4 * g + 4)
            gcols = slice(256 * g, 256 * g + 256)
            W4 = 4 * L
            Rm_, dec_, MT_, cbm_, xdt_, xw_, xD_, tmp_ = Rm[u], decay[u], MT[u], cbm[u], xdt[u], xw[u], xD[u], tmp[u]
            kR, kD, kM, kC, kX, kW, kXD, kT = ("Rm", u), ("decay", u), ("MT", u), ("cbm", u), ("xdt", u), ("xw", u), ("xD", u), ("tmp", u)
            segb = 3 + u
            cbv = ps[5][0:L, 0:L] if u == 0 else ps[2][0:L, 256:256 + L]
            dbk = dbank()
            dsv, kds_ = ps[dbk][:, 0:256], PK(dbk)
            yv = ps[6][0:L, u * 256:(u + 1) * 256]
            ysv = ps[7][0:L, u * 256:(u + 1) * 256]
            kcb, kds, ky, kys = (PK(5) if u == 0 else PK(2)), kds_, ("ps6", u), ("ps7", u)
            xv = XT[0:L, gcols].rearrange("p (h d) -> p h d", h=4)
            S.emit("pool", lambda E: E.tensor_tensor(
                out=Rm_[0:L, 0:W4].rearrange("p (h q) -> p h q", h=4),
                in0=dtA[0:L, ci, hs4].unsqueeze(2).to_broadcast([L, 4, L]),
                in1=triu[0:L, 0:L].unsqueeze(1).to_broadcast([L, 4, L]), op=ALU.mult),
                reads=["dtA", "consts"], writes=[kR])
            S.emit("pe", lambda E: E.matmul(ps[segb][0:L, 0:W4], lhsT=lstr[0:L, 0:L], rhs=Rm_[0:L, 0:W4], start=True, stop=True),
                   reads=[kR, "consts"], writes=[PK(segb)])
            S.emit("act", lambda E: E.activation(out=dec_[0:L, 0:W4], in_=ps[segb][0:L, 0:W4], func=AF.Exp),
                   reads=[PK(segb)], writes=[kD])
            S.emit("pe", lambda E: E.matmul(cbv, lhsT=xbcT[:, 16 + g, tok], rhs=xbcT[:, 24 + g, tok], start=True, stop=True),
                   reads=["xbcT"], writes=[kcb])
            S.emit("dve", lambda E: E.tensor_tensor(out=cbm_[0:L, 0:L], in0=cbv, in1=triu[0:L, 0:L], op=ALU.mult),
                   reads=[kcb, "consts"], writes=[kC])
            S.emit("pool", lambda E: E.tensor_tensor(
                out=xdt_[0:L, :].rearrange("p (h d) -> p h d", h=4), in0=xv,
                in1=dtb[0:L, ci, hs4].unsqueeze(2).to_broadcast([L, 4, 64]), op=ALU.mult),
                reads=[kXT, "dtb"], writes=[kX])
            S.emit("pool", lambda E: E.tensor_tensor(
                out=xD_[0:L, :].rearrange("p (h d) -> p h d", h=4), in0=xv,
                in1=rows[0:L, R_D + 4 * g:R_D + 4 * g + 4].unsqueeze(2).to_broadcast([L, 4, 64]), op=ALU.mult),
                reads=[kXT, "rows"], writes=[kXD])
            yield "A"
            S.emit("dve", lambda E: E.tensor_tensor(
                out=MT_[0:L, 0:W4].rearrange("p (r q) -> p r q", r=4),
                in0=dec_[0:L, 0:W4].rearrange("p (r q) -> p r q", r=4),
                in1=cbm_[0:L, 0:L].unsqueeze(1).to_broadcast([L, 4, L]), op=ALU.mult),
                reads=[kD, kC], writes=[kM])
            S.emit("dve", lambda E: E.tensor_tensor(
                out=xw_[0:L, :].rearrange("p (h d) -> p h d", h=4), in0=xdt_[0:L, :].rearrange("p (h d) -> p h d", h=4),
                in1=dec_[0:L, 0:W4].rearrange("p (h q) -> p h q", h=4)[:, :, L - 1:L].to_broadcast([L, 4, 64]), op=ALU.mult),
                reads=[kX, kD], writes=[kW])

            def f(E):
                E.matmul(yv, lhsT=identb[0:L, 0:L], rhs=xD_[0:L, :], start=True, stop=False)
                for hh in range(4):
                    ins = E.matmul(yv[:, hh * 64:(hh + 1) * 64], lhsT=MT_[0:L, hh * L:(hh + 1) * L], rhs=xdt_[0:L, hh * 64:(hh + 1) * 64],
                                   start=False, stop=True, skip_group_check=True)
                return ins
            S.emit("pe", f, reads=[kM, kX, kXD, "identb"], writes=[ky])
            S.emit("pe", lambda E: E.matmul(ysv, lhsT=xbcT[:, 24 + g, tok], rhs=Sbf[si][:, g * 256:(g + 1) * 256], start=True, stop=True),
                   reads=["xbcT", (SBK, g)], writes=[kys])
            S.emit("pe", lambda E: E.matmul(dsv, lhsT=BT[0:L, g * 128:(g + 1) * 128], rhs=xw_[0:L, :], start=True, stop=True),
                   reads=[kBT, kW], writes=[kds])
            yield "B"
            S.emit("dve", lambda E: E.tensor_tensor(
                out=tmp_[0:L, :].rearrange("p (h d) -> p h d", h=4), in0=ysv.rearrange("p (h d) -> p h d", h=4),
                in1=EA[0:L, hs4].unsqueeze(2).to_broadcast([L, 4, 64]), op=ALU.mult),
                reads=[kys, kEA], writes=[kT])
            S.emit("dve", lambda E: E.tensor_tensor(out=tmp_[0:L, :], in0=yv, in1=tmp_[0:L, :], op=ALU.add),
                   reads=[ky, kT], writes=[kT])
            S.emit("dve", lambda E: E.tensor_tensor(out=YA[0:L, gcols], in0=tmp_[0:L, :], in1=sz[0:L, ci, gcols], op=ALU.mult),
                   reads=[kT, "sz"], writes=[("yall", cp, g)])
            S.emit("dve", lambda E: E.tensor_tensor(
                out=Sst[si][:, gcols].rearrange("p (h d) -> p h d", h=4), in0=Sst[si][:, gcols].rearrange("p (h d) -> p h d", h=4),
                in1=EL[:, hs4].unsqueeze(2).to_broadcast([128, 4, 64]), op=ALU.mult),
                reads=[(SK, g), kEL], writes=[(SK, g)])
            S.emit("dve", lambda E: E.tensor_tensor(out=Sst[si][:, gcols], in0=dsv, in1=Sst[si][:, gcols], op=ALU.add),
                   reads=[kds, (SK, g)], writes=[(SK, g)])
            S.emit("act", lambda E: E.copy(out=Sbf[si][:, gcols], in_=Sst[si][:, gcols]), reads=[(SK, g)], writes=[(SBK, g)])

        gens = [unit_gen(g) for g in range(8)]
        next(gens[0])
        for g in range(8):
            if g + 1 < 8:
                next(gens[g + 1])
            next(gens[g])
            next(gens[g], None)
            if g == 3:
                yield "mid"
        yield "units"
        S.emit("act", lambda E: E.activation(out=ysq[0:L, :], in_=YA[0:L, :], func=AF.Square), reads=YK, writes=["ysq"])
        S.emit("dve", lambda E: E.tensor_reduce(out=ss8[0:L, :], in_=ysq[0:L, :].rearrange("p (g c) -> p g c", g=8), axis=AX.X, op=ALU.add),
               reads=["ysq"], writes=["ss8"])
        S.emit("act", lambda E: E.activation(out=ss8[0:L, :], in_=ss8[0:L, :], func=AF.Sqrt, bias=epsc[0:L, :], scale=1.0 / 256.0),
               reads=["ss8", "epsc"], writes=["ss8"])
        S.emit("dve", lambda E: E.reciprocal(out=ss8[0:L, :], in_=ss8[0:L, :]), reads=["ss8"], writes=["ss8"])
        S.emit("dve", lambda E: E.tensor_tensor(
            out=YA[0:L, :].rearrange("p (g c) -> p g c", g=8), in0=YA[0:L, :].rearrange("p (g c) -> p g c", g=8),
            in1=ss8[0:L, :].unsqueeze(2).to_broadcast([L, 8, 256]), op=ALU.mult), reads=YK + ["ss8"], writes=YK)
        S.emit("pool", lambda E: E.tensor_tensor(out=yn[0:L, :], in0=YA[0:L, :], in1=rows[0:L, R_GN:R_GN + 2048], op=ALU.mult),
               reads=YK + ["rows"], writes=["yn"])
        for half in range(2):
            bank = dbank()

            def f(E, half=half, bank=bank):
                for k in range(8):
                    c = half * 8 + k
                    ins = E.transpose(psb[bank][:, k * L:(k + 1) * L], yn[0:L, c * 128:(c + 1) * 128], identb[0:L, 0:L])
                return ins
            S.emit("pe", f, reads=["yn", "identb"], writes=[PK(bank)])
            S.emit("act", lambda E, half=half, bank=bank: E.copy(
                out=ynT[:, half * 8:(half + 1) * 8, tok], in_=psb[bank][:, 0:8 * L].rearrange("p (k q) -> p k q", k=8)),
                reads=[PK(bank)], writes=["ynT"])
        yield "epi"

    def ssd_tile(chunk_list):
        gens = [ssd_chunk_gen(*c) for c in chunk_list]
        for gch in gens:
            next(gch)
        prev = None
        for gch in gens:
            next(gch)
            if prev is not None:
                next(prev)
            next(gch)
            prev = gch
        next(prev)

    def pass1_tile(T, segs, load_fn, hs_idx):
        load_fn()
        norm_to(hT, HK, 8, D_MODEL, T, ln_col(0), aT, "aT")
        ci = 0
        chunk_list = []
        for (s0, Ls, si, chunks) in segs:
            for (c0, L) in chunks:
                chunk_list.append((ci, c0, L, si))

                def f(E, c0=c0, L=L):
                    for kc in range(8):
                        ins = E.matmul(ps[2][0:L, 0:32], lhsT=aT[:, kc, c0:c0 + L], rhs=wdt[:, kc, :], start=(kc == 0), stop=(kc == 7))
                    return ins
                S.emit("pe", f, reads=["aT", "wdt"], writes=[PK(2)])
                S.emit("dve", lambda E, L=L, ci=ci: E.tensor_tensor(out=dtb[0:L, ci, :], in0=ps[2][0:L, 0:32], in1=rows[0:L, R_DTB:R_DTB + 32], op=ALU.add),
                       reads=[PK(2), "rows"], writes=["dtb"])
                S.emit("act", lambda E, L=L, ci=ci: E.activation(out=dtb[0:L, ci, :], in_=dtb[0:L, ci, :], func=AF.Exp), reads=["dtb"], writes=["dtb"])
                S.emit("act", lambda E, L=L, ci=ci: E.activation(out=dtb[0:L, ci, :], in_=dtb[0:L, ci, :], func=AF.Ln, bias=1.0, scale=1.0),
                       reads=["dtb"], writes=["dtb"])
                S.emit("dve", lambda E, L=L, ci=ci: E.tensor_tensor(out=dtA[0:L, ci, :], in0=dtb[0:L, ci, :], in1=abc[0:L, :], op=ALU.mult),
                       reads=["dtb", "abc"], writes=["dtA"])
                ci += 1
        pend_silu = []
        for b in range(8):
            def ev(oc, bank, b=b):
                ch = b * 4 + oc
                pb = ch % 2
                P = pre[pb]
                pk, ak = ("pre", pb), ("cacc", pb)
                off = 0
                for (s0, Ls, si, chunks) in segs:
                    base = s0 + 3 * (segs.index((s0, Ls, si, chunks)))
                    S.emit("pool", lambda E, base=base, si=si: E.tensor_copy(out=P[:, base:base + 3], in_=tail[si][:, ch, :]),
                           reads=[("tail", si)], writes=[pk])
                    S.emit("act", lambda E, base=base, s0=s0, Ls=Ls: E.copy(out=P[:, base + 3:base + 3 + Ls], in_=ps[bank][:, s0:s0 + Ls]),
                           reads=[PK(bank)], writes=[pk])
                    S.emit("pool", lambda E, base=base, si=si, Ls=Ls: E.tensor_copy(out=tail[si][:, ch, :], in_=P[:, base + Ls:base + Ls + 3]),
                           reads=[pk], writes=[("tail", si)])
                    cw = cols[:, C_CW + ch * 4:C_CW + ch * 4 + 4]
                    A = cacc[pb]
                    Q2, Q3 = cq2[pb], cq3[pb]
                    qk2, qk3 = ("cq2", pb), ("cq3", pb)
                    S.emit("act", lambda E, s0=s0, Ls=Ls, cw=cw, Q3=Q3: E.activation(
                        out=Q3[:, s0:s0 + Ls], in_=ps[bank][:, s0:s0 + Ls], func=AF.Copy, scale=cw[:, 3:4]),
                        reads=[PK(bank), "cols"], writes=[qk3])
                    S.emit("act", lambda E, base=base, s0=s0, Ls=Ls, cw=cw, Q2=Q2: E.activation(
                        out=Q2[:, s0:s0 + Ls], in_=P[:, base + 2:base + 2 + Ls], func=AF.Copy, scale=cw[:, 2:3]),
                        reads=[pk, "cols"], writes=[qk2])
                    S.emit("pool", lambda E, s0=s0, Ls=Ls, Q2=Q2, Q3=Q3: E.tensor_tensor(
                        out=Q2[:, s0:s0 + Ls], in0=Q2[:, s0:s0 + Ls], in1=Q3[:, s0:s0 + Ls], op=ALU.add),
                        reads=[qk2, qk3], writes=[qk2])
                    S.emit("dve", lambda E, base=base, s0=s0, Ls=Ls, cw=cw, A=A: E.tensor_scalar(
                        out=A[:, s0:s0 + Ls], in0=P[:, base:base + Ls], scalar1=cw[:, 0:1], scalar2=None, op0=ALU.mult),
                        reads=[pk, "cols"], writes=[ak])
                    S.emit("dve", lambda E, base=base, s0=s0, Ls=Ls, cw=cw, A=A: E.scalar_tensor_tensor(
                        out=A[:, s0:s0 + Ls], in0=P[:, base + 1:base + 1 + Ls], scalar=cw[:, 1:2], in1=A[:, s0:s0 + Ls],
                        op0=ALU.mult, op1=ALU.add), reads=[pk, ak, "cols"], writes=[ak])
                    S.emit("dve", lambda E, s0=s0, Ls=Ls, A=A, Q2=Q2: E.tensor_tensor(
                        out=A[:, s0:s0 + Ls], in0=A[:, s0:s0 + Ls], in1=Q2[:, s0:s0 + Ls], op=ALU.add),
                        reads=[ak, qk2], writes=[ak])
                def silu(A=cacc[pb], ch=ch, ak=ak):
                    S.emit("act", lambda E: E.activation(out=xbcT[:, ch, 0:T], in_=A[:, 0:T], func=AF.Silu,
                                                         bias=cols[:, C_CB + ch:C_CB + ch + 1], scale=1.0),
                           reads=[ak, "cols"], writes=["xbcT"])
                if pend_silu:
                    pend_silu.pop()()
                pend_silu.append(silu)
            dense_ws("xbc", 8, 512, 512, T, lambda kc: aT[:, kc, 0:T], ["aT"], ev)
        if pend_silu:
            pend_silu.pop()()
        for b in range(4):
            view, wkey = WS.next("z", 8, 512)
            for (ci, c0, L, si) in chunk_list:
                bank = dbank()

                def f(E, c0=c0, L=L, bank=bank, view=view):
                    for kc in range(8):
                        ins = E.matmul(ps[bank][0:L, 0:512], lhsT=aT[:, kc, c0:c0 + L], rhs=view[:, kc, :], start=(kc == 0), stop=(kc == 7))
                    return ins
                S.emit("pe", f, reads=[wkey, "aT"], writes=[PK(bank)])
                S.emit("act", lambda E, L=L, ci=ci, bank=bank, b=b: E.activation(out=sz[0:L, ci, b * 512:(b + 1) * 512], in_=ps[bank][0:L, 0:512], func=AF.Silu),
                       reads=[PK(bank)], writes=["sz"])
            WS.done()
        ssd_tile(chunk_list)
        for b in range(4):
            def ev(oc, bank, b=b):
                S.emit("act", lambda E: E.copy(out=mT[:, b * 2 + oc, 0:T], in_=ps[bank][:, 0:T]), reads=[PK(bank)], writes=[("mT", b * 2 + oc)])
            dense_ws("wout", 16, 256, 256, T, lambda kc: ynT[:, kc, 0:T], ["ynT"], ev)
        postnorm_residual(T, ln_col(2))
        ffn(0, T, hidT1)
        S.emit("pool", lambda E: E.dma_start(out=hs[hs_idx].rearrange("p (k t) -> p k t", k=8)[:, :, 0:T], in_=hT[:, :, 0:T]),
               reads=HK, writes=[("hs", hs_idx)], dma=True)

    for si in range(2):
        S.emit("pool", lambda E, si=si: E.dma_start(out=Sst[si][:], in_=ssmst_d[si]), writes=SKs(si), dma=True)
        S.emit("pool", lambda E, si=si: E.dma_start(out=tail[si][:].rearrange("p c k -> p (c k)"), in_=convst_d[si]), writes=[("tail", si)], dma=True)
        S.emit("act", lambda E, si=si: E.copy(out=Sbf[si][:], in_=Sst[si][:]), reads=SKs(si), writes=SBKs(si))

    def load_sample():
        S.emit("pool", lambda E: E.dma_start(out=hT[:, :, 0:2 * LS], in_=xsT.rearrange("k p t -> p k t")), writes=HK, dma=True)
    pass1_tile(2 * LS, [(0, LS, 0, [(0, LS)]), (LS, LS, 1, [(LS, LS)])], load_sample, NT)
    for si in range(2):
        S.emit("pool", lambda E, si=si: E.dma_start(out=sssm_o[si], in_=Sst[si][:]), reads=SKs(si), writes=[("o_sssm", si)], dma=True)
        S.emit("pool", lambda E, si=si: E.dma_start(out=sconv_o[si], in_=tail[si][:].rearrange("p c k -> p (c k)")),
               reads=[("tail", si)], writes=[("o_sconv", si)], dma=True)
    S.emit("pool", lambda E: E.memset(Sst[0][:], 0.0), writes=SKs(0))
    S.emit("pool", lambda E: E.memset(Sbf[0][:], 0.0), writes=SBKs(0))
    S.emit("pool", lambda E: E.memset(tail[0][:], 0.0), writes=[("tail", 0)])
    for j in range(NT):
        def load_p(j=j):
            S.emit("pool", lambda E: E.dma_start(out=hT[:, :, :], in_=xT.rearrange("k p t -> p k t")[:, :, j * TT:(j + 1) * TT]),
                   writes=HK, dma=True)
        pass1_tile(TT, [(0, TT, 0, [(0, 128), (128, 128)])], load_p, j)
        if j % 16 == 15 and j != NT - 1:
            S.new_epoch()
    S.emit("pool", lambda E: E.dma_start(out=pssm_o[:, :], in_=Sst[0][:]), reads=SKs(0), writes=["o_pssm"], dma=True)
    S.emit("pool", lambda E: E.dma_start(out=pconv_o[:, :], in_=tail[0][:].rearrange("p c k -> p (c k)")), reads=[("tail", 0)], writes=["o_pconv"], dma=True)

    S.new_epoch()
    replay_all()
    es1.close()

    es2 = ExitStack()

    def sb2(name, shape, dt=F32):
        return es2.enter_context(nc.sbuf_tensor("s2_" + name, list(shape), dt))

    NBLK = SEQ // 128
    latT = sb2("latT", [128, 2, SEQ], BF16)
    krT = sb2("krT", [32, SEQ], BF16)
    lat_tok = sb2("lat_tok", [128, NBLK, 256], BF16)
    if NBLK >= 63:
        flat = lat_tok[:].rearrange("p b c -> p (b c)")
        off = [20 * 256]

        def carve(n):
            v = flat[:, off[0]:off[0] + n]
            off[0] += n
            return v
        s_latT = [carve(2 * (PAST + LS)).rearrange("p (j t) -> p j t", j=2) for i in range(2)]
        s_krT = [carve(PAST + LS)[0:32, :] for i in range(2)]
        s_lat_tok = [carve(9 * 256).rearrange("p (b c) -> p b c", b=9) for i in range(2)]
        assert off[0] <= NBLK * 256
    else:
        s_latT = [sb2("s_latT%d" % i, [128, 2, PAST + LS], BF16) for i in range(2)]
        s_krT = [sb2("s_krT%d" % i, [32, PAST + LS], BF16) for i in range(2)]
        s_lat_tok = [sb2("s_lat_tok%d" % i, [128, 9, 256], BF16) for i in range(2)]
    wuk = sb2("wuk", [128, 8, 256], BF16)
    wuv = sb2("wuv", [128, 2, 16, 128], BF16)
    cqT = mT[:, 0:4, :]
    cqnT = sb2("cqnT", [128, 4, TT], BF16)
    latraw = sb2("latraw", [128, 2, TT])
    latn = sb2("latn", [128, 2, TT])
    krf = sb2("krf", [32, 3, TT])
    cst = sb2("cst", [32, 2, TT])
    qq = sb2("qq", [128, 24 * TT], BF16)
    qnT = qq[:, 0:8 * TT].rearrange("p (k t) -> p k t", k=8)
    qrT = qq[0:32, 8 * TT:24 * TT].rearrange("p (k t) -> p k t", k=16)
    hidT2 = qq[:, 0:22 * TT].rearrange("p (k t) -> p k t", k=22)
    qlT = [sb2("qlT%d" % i, [128, 2, 2, TT], BF16) for i in range(2)]
    pT = [sb2("pT%d" % i, [128, 2 * TT], BF16) for i in range(3)]
    osb = sb2("osb", [128, 3, 2 * TT])
    pacc = [sb2("pacc%d" % i, [128, 2 * TT]) for i in range(2)]
    olT = [sb2("olT%d" % i, [128, 2, 2, TT], BF16) for i in range(2)]
    oT = sb2("oT", [128, 8, TT], BF16)

    S.emit("pool", lambda E: E.dma_start(out=wuk[:].rearrange("p a c -> p (a c)"), in_=wuk_d[:, :]), writes=["wuk"], dma=True)
    S.emit("pool", lambda E: E.dma_start(out=wuv[:].rearrange("p a b c -> p (a b c)"), in_=wuv_d[:, :]), writes=["wuv"], dma=True)
    for si in range(2):
        S.emit("pool", lambda E, si=si: E.dma_start(out=s_lat_tok[si][:, 0:8, :], in_=clat_d[si].rearrange("b p c -> p b c")),
               writes=[("ltok", si)], dma=True)
        S.emit("pool", lambda E, si=si: E.dma_start(out=s_latT[si][:, :, 0:PAST], in_=clatT_d[si].rearrange("j p t -> p j t")),
               writes=[("latT", si)], dma=True)
        S.emit("pool", lambda E, si=si: E.dma_start(out=s_krT[si][:, 0:PAST], in_=ckrT_d[si]), writes=[("krT", si)], dma=True)

    def pass2_tile(T, segs, hs_idx, cs_src, out_fn, final_fn):
        S.emit("pool", lambda E: E.dma_start(out=hT[:, :, 0:T], in_=hs[hs_idx].rearrange("p (k t) -> p k t", k=8)[:, :, 0:T]),
               reads=[("hs", hs_idx)], writes=HK, dma=True)
        S.emit("pool", lambda E: E.dma_start(out=cst[:, :, 0:T], in_=cs_src), writes=["cst"], dma=True)
        norm_to(hT, HK, 8, D_MODEL, T, ln_col(1), aT, "aT")

        def ev(oc, bank):
            S.emit("act", lambda E: E.copy(out=cqT[:, oc, 0:T], in_=ps[bank][:, 0:T]), reads=[PK(bank)], writes=["cqT"])
        dense_ws("wqa", 8, 512, 512, T, lambda kc: aT[:, kc, 0:T], ["aT"], ev)
        norm_to(cqT, ["cqT"], 4, 512, T, cols[:, C_QN:C_QN + 4], cqnT, "cqnT")
        view, wkey = WS.next("wkva", 8, 320)
        for oc in range(2):
            bank = dbank()

            def f(E, oc=oc, bank=bank, view=view):
                for kc in range(8):
                    ins = E.matmul(ps[bank][:, 0:T], lhsT=view[:, kc, oc * 128:(oc + 1) * 128], rhs=aT[:, kc, 0:T], start=(kc == 0), stop=(kc == 7))
                return ins
            S.emit("pe", f, reads=[wkey, "aT"], writes=[PK(bank)])
            S.emit("act", lambda E, oc=oc, bank=bank: E.copy(out=latraw[:, oc, 0:T], in_=ps[bank][:, 0:T]), reads=[PK(bank)], writes=["latraw"])
        bank = dbank()

        def f(E, bank=bank, view=view):
            for w in range(2):
                for kc in range(8):
                    ins = E.matmul(ps[bank][0:32, w * T:(w + 1) * T], lhsT=view[:, kc, 256 + 32 * w:288 + 32 * w], rhs=aT[:, kc, 0:T],
                                   start=(kc == 0), stop=(kc == 7))
            return ins
        S.emit("pe", f, reads=[wkey, "aT"], writes=[PK(bank)])
        WS.done()
        S.emit("dve", lambda E, bank=bank: E.tensor_tensor(out=krf[:, 0, 0:T], in0=ps[bank][0:32, 0:T], in1=cst[:, 0, 0:T], op=ALU.mult),
               reads=[PK(bank), "cst"], writes=["krf0"])
        S.emit("dve", lambda E, bank=bank: E.tensor_tensor(out=krf[:, 1, 0:T], in0=ps[bank][0:32, T:2 * T], in1=cst[:, 1, 0:T], op=ALU.mult),
               reads=[PK(bank), "cst"], writes=["krf1"])
        S.emit("dve", lambda E: E.tensor_tensor(out=krf[:, 2, 0:T], in0=krf[:, 0, 0:T], in1=krf[:, 1, 0:T], op=ALU.add),
               reads=["krf0", "krf1"], writes=["krf2"])
        norm_to(latraw, ["latraw"], 2, 256, T, cols[:, C_KN:C_KN + 2], latn, "latn")
        out_fn()
        for sg in segs:
            c0, L, ctx, kp = sg["c0"], sg["L"], sg["ctx"], sg["kpos"]
            cl, ck, ct, ci = ctx
            S.emit("act", lambda E, c0=c0, L=L, cl=cl, kp=kp: E.copy(out=cl[:, :, kp:kp + L], in_=latn[:, :, c0:c0 + L]),
                   reads=["latn"], writes=[("latT", ci)])
            S.emit("act", lambda E, c0=c0, L=L, ck=ck, kp=kp: E.copy(out=ck[:, kp:kp + L], in_=krf[:, 2, c0:c0 + L]),
                   reads=["krf2"], writes=[("krT", ci)])
            nch = (L + 127) // 128
            for cc in range(nch):
                Lc = min(128, L - cc * 128)
                blk = (kp + cc * 128) // 128
                bank = dbank()

                def f(E, cl=cl, kp=kp, cc=cc, Lc=Lc, bank=bank):
                    for j in range(2):
                        ins = E.transpose(psb[bank][0:Lc, j * 128:(j + 1) * 128], cl[:, j, kp + cc * 128:kp + cc * 128 + Lc], identb[:, :])
                    return ins
                S.emit("pe", f, reads=[("latT", ci), "identb"], writes=[PK(bank)])
                S.emit("act", lambda E, ct=ct, blk=blk, Lc=Lc, bank=bank: E.copy(out=ct[0:Lc, blk, :], in_=psb[bank][0:Lc, 0:256]),
                       reads=[PK(bank)], writes=[("ltok", ci)])
        view, wkey = WS.next("wqb", 4, 1024)
        for i in range(8):
            bank = dbank()

            def f(E, i=i, bank=bank, view=view):
                for kc in range(4):
                    ins = E.matmul(ps[bank][:, 0:T], lhsT=view[:, kc, i * 128:(i + 1) * 128], rhs=cqnT[:, kc, 0:T], start=(kc == 0), stop=(kc == 3))
                return ins
            S.emit("pe", f, reads=[wkey, "cqnT"], writes=[PK(bank)])
            S.emit("act", lambda E, i=i, bank=bank: E.copy(out=qnT[:, i, 0:T], in_=ps[bank][:, 0:T]), reads=[PK(bank)], writes=["qnT"])
        WS.done()
        view, wkey = WS.next("wqb", 4, 1024)
        for h in range(16):
            bank = dbank()

            def f(E, h=h, bank=bank, view=view):
                for w in range(2):
                    for kc in range(4):
                        ins = E.matmul(ps[bank][0:32, w * T:(w + 1) * T], lhsT=view[:, kc, w * 512 + h * 32:w * 512 + h * 32 + 32],
                                       rhs=cqnT[:, kc, 0:T], start=(kc == 0), stop=(kc == 3))
                return ins
            S.emit("pe", f, reads=[wkey, "cqnT"], writes=[PK(bank)])
            S.emit("dve", lambda E, bank=bank: E.tensor_tensor(out=krf[:, 0, 0:T], in0=ps[bank][0:32, 0:T], in1=cst[:, 0, 0:T], op=ALU.mult),
                   reads=[PK(bank), "cst"], writes=["krf0"])
            S.emit("dve", lambda E, bank=bank: E.tensor_tensor(out=krf[:, 1, 0:T], in0=ps[bank][0:32, T:2 * T], in1=cst[:, 1, 0:T], op=ALU.mult),
                   reads=[PK(bank), "cst"], writes=["krf1"])
            S.emit("dve", lambda E, h=h: E.tensor_tensor(out=qrT[:, h, 0:T], in0=krf[:, 0, 0:T], in1=krf[:, 1, 0:T], op=ALU.add),
                   reads=["krf0", "krf1"], writes=["qrT"])
        WS.done()
        def emit_qlat(i):
            QL = qlT[i % 2]
            qk = ("qlT", i % 2)
            for e in range(2):
                for j in range(2):
                    bank = dbank()
                    S.emit("pe", lambda E, e=e, j=j, bank=bank, i=i: E.matmul(
                        ps[bank][:, 0:T], lhsT=wuk[64 * e:64 * e + 64, i, j * 128:(j + 1) * 128], rhs=qnT[64 * e:64 * e + 64, i, 0:T],
                        start=True, stop=True), reads=["wuk", "qnT"], writes=[PK(bank)])
                    S.emit("act", lambda E, e=e, j=j, bank=bank, QL=QL: E.copy(out=QL[:, j, e, 0:T], in_=ps[bank][:, 0:T]),
                           reads=[PK(bank)], writes=[qk])

        def emit_wuv(i):
            OL = olT[i % 2]
            ok = ("olT", i % 2)
            bank = dbank()

            def f(E, i=i, bank=bank, OL=OL):
                n = 0
                for e in range(2):
                    for j in range(2):
                        ins = E.matmul(ps[bank][:, 0:T], lhsT=wuv[:, j, 2 * i + e, :], rhs=OL[:, j, e, 0:T], start=(n == 0), stop=(n == 3))
                        n += 1
                return ins
            S.emit("pe", f, reads=["wuv", ok], writes=[PK(bank)])
            S.emit("act", lambda E, i=i, bank=bank: E.copy(out=oT[:, i, 0:T], in_=ps[bank][:, 0:T]), reads=[PK(bank)], writes=["oT"])

        pti = [0]
        pai = [0]
        pend_fin = []
        emit_qlat(0)
        for i in range(8):
            QL = qlT[i % 2]
            OL = olT[i % 2]
            qk, ok = ("qlT", i % 2), ("olT", i % 2)
            if i + 1 < 8:
                emit_qlat(i + 1)
            for sgi, sg in enumerate(segs):
                c0, L, ctx = sg["c0"], sg["L"], sg["ctx"]
                cl, ck, ct, ci = ctx
                blocks = sg["blocks"]
                nb = len(blocks)
                slots = []
                PA = pacc[pai[0] % 2]
                pak = ("pacc", pai[0] % 2)
                pai[0] += 1

                def emit_qk(bi):
                    blk, key0, nk, qa, diag = blocks[bi]
                    sbank = 3 + (pti[0] % 2)
                    P = pT[pti[0] % 3]
                    pk = ("pT", pti[0] % 3)
                    pti[0] += 1
                    n = L - qa
                    qs = slice(c0 + qa, c0 + L)
                    sv = ps[sbank][0:nk, 0:2 * n].rearrange("p (e q) -> p e q", e=2)

                    def f(E, cl=cl, ck=ck, QL=QL, i=i):
                        E.matmul(sv, lhsT=cl[:, 0, key0:key0 + nk], rhs=QL[:, 0, :, qs], start=True, stop=False)
                        E.matmul(sv, lhsT=cl[:, 1, key0:key0 + nk], rhs=QL[:, 1, :, qs], start=False, stop=False)
                        return E.matmul(sv, lhsT=ck[:, key0:key0 + nk], rhs=qrT[:, 2 * i:2 * i + 2, qs], start=False, stop=True)
                    S.emit("pe", f, reads=[("latT", ci), ("krT", ci), qk, "qrT"], writes=[PK(sbank)])
                    S.emit("act", lambda E: E.activation(out=P[0:nk, 0:2 * n], in_=ps[sbank][0:nk, 0:2 * n], func=AF.Exp, scale=SCALE),
                           reads=[PK(sbank)], writes=[pk])
                    if diag:
                        S.emit("pool", lambda E: E.memset(P[64:128, 0:2 * n].rearrange("p (e q) -> p e q", e=2)[:, :, 0:64], 0.0),
                               reads=[pk], writes=[pk])
                    slots.append((P, pk, n))

                def emit_pv(bi):
                    blk, key0, nk, qa, diag = blocks[bi]
                    P, pk, n = slots[bi]
                    first, last = (bi == 0), (bi == nb - 1)
                    pv = P[0:nk, 0:2 * n].rearrange("p (e q) -> p e q", e=2)

                    def f(E, ct=ct, L=L):
                        o0 = ps[5][:, 0:2 * L].rearrange("p (e q) -> p e q", e=2)[:, :, qa:qa + n]
                        o1 = ps[6][:, 0:2 * L].rearrange("p (e q) -> p e q", e=2)[:, :, qa:qa + n]
                        E.matmul(o0, lhsT=ct[0:nk, blk, 0:128], rhs=pv, start=first, stop=last, skip_group_check=True)
                        return E.matmul(o1, lhsT=ct[0:nk, blk, 128:256], rhs=pv, start=first, stop=last, skip_group_check=True)
                    S.emit("pe", f, reads=[("ltok", ci), pk], writes=[PK(5), PK(6)])
                    pav = PA[0:nk, 0:2 * L].rearrange("p (e q) -> p e q", e=2)[:, :, qa:qa + n]
                    if first:
                        S.emit("dve", lambda E, pav=pav, pv=pv: E.tensor_copy(out=pav, in_=pv), reads=[pk], writes=[pak])
                    else:
                        S.emit("dve", lambda E, pav=pav, pv=pv: E.tensor_tensor(out=pav, in0=pav, in1=pv, op=ALU.add), reads=[pk, pak], writes=[pak])

                emit_qk(0)
                if nb > 1:
                    emit_qk(1)
                if pend_fin:
                    pend_fin.pop()()
                for bi in range(nb):
                    if bi >= 1 and bi + 1 < nb:
                        emit_qk(bi + 1)
                    emit_pv(bi)
                    if bi == min(2, nb - 1) and sgi == 0 and i > 0:
                        emit_wuv(i - 1)
                S.emit("act", lambda E, L=L: E.copy(out=osb[:, 0, 0:2 * L], in_=ps[5][:, 0:2 * L]), reads=[PK(5)], writes=["osb0"])
                S.emit("dve", lambda E, L=L: E.tensor_copy(out=osb[:, 1, 0:2 * L], in_=ps[6][:, 0:2 * L]), reads=[PK(6)], writes=["osb1"])

                def fin(L=L, c0=c0, OL=OL, PA=PA, pak=pak, ok=ok):
                    S.emit("pe", lambda E: E.matmul(ps[7][:, 0:2 * L], lhsT=onesf[:, :], rhs=PA[:, 0:2 * L], start=True, stop=True),
                           reads=[pak, "consts"], writes=[PK(7)])
                    S.emit("act", lambda E: E.copy(out=osb[:, 2, 0:2 * L], in_=ps[7][:, 0:2 * L]), reads=[PK(7)], writes=["osb2"])
                    S.emit("dve", lambda E: E.reciprocal(out=osb[:, 2, 0:2 * L], in_=osb[:, 2, 0:2 * L]), reads=["osb2"], writes=["osb2"])
                    for j in range(2):
                        S.emit("dve" if j == 0 else "pool", lambda E, j=j: E.tensor_tensor(
                            out=OL[:, j, :, c0:c0 + L], in0=osb[:, j, 0:2 * L].rearrange("p (e q) -> p e q", e=2),
                            in1=osb[:, 2, 0:2 * L].rearrange("p (e q) -> p e q", e=2), op=ALU.mult),
                            reads=["osb%d" % j, "osb2"], writes=[ok])
                pend_fin.append(fin)
        if pend_fin:
            pend_fin.pop()()
        emit_wuv(7)
        for b in range(2):
            def ev(oc, bank, b=b):
                S.emit("act", lambda E: E.copy(out=mT[:, b * 4 + oc, 0:T], in_=ps[bank][:, 0:T]), reads=[PK(bank)], writes=[("mT", b * 4 + oc)])
            dense_ws("wo", 8, 512, 512, T, lambda kc: oT[:, kc, 0:T], ["oT"], ev)
        postnorm_residual(T, ln_col(3))
        ffn(1, T, hidT2)
        final_fn()

    def out_sample():
        S.emit("pool", lambda E: E.dma_start(out=slatT_o.rearrange("j p t -> p j t"), in_=latn[:, :, 0:2 * LS]), reads=["latn"], writes=["o_slat"], dma=True)
        S.emit("pool", lambda E: E.dma_start(out=skrT_o[:, :], in_=krf[:, 2, 0:2 * LS]), reads=["krf2"], writes=["o_skr"], dma=True)

    def fin_sample():
        S.emit("pool", lambda E: E.dma_start(out=ysT_o.rearrange("k p t -> p k t"), in_=hT[:, :, 0:2 * LS]), reads=HK, writes=["o_ys"], dma=True)
    segs = []
    for si in range(2):
        blocks = [(b, b * 128, 128, 0, False) for b in range(8)] + [(8, PAST, LS, 0, False)]
        segs.append(dict(c0=si * LS, L=LS, ctx=(s_latT[si], s_krT[si], s_lat_tok[si], si), kpos=PAST, blocks=blocks))
    pass2_tile(2 * LS, segs, NT, css_d[:, :, :], out_sample, fin_sample)

    for j in range(NT):
        cols_j = slice(j * TT, (j + 1) * TT)

        def out_p(cols_j=cols_j):
            S.emit("pool", lambda E: E.dma_start(out=platT_o.rearrange("j p t -> p j t")[:, :, cols_j], in_=latn[:, :, :]),
                   reads=["latn"], writes=[("o_plat", cols_j.start)], dma=True)
            S.emit("pool", lambda E: E.dma_start(out=pkrT_o[:, cols_j], in_=krf[:, 2, :]), reads=["krf2"], writes=[("o_pkr", cols_j.start)], dma=True)

        def fin_p(cols_j=cols_j):
            S.emit("pool", lambda E: E.dma_start(out=yT_o.rearrange("k p t -> p k t")[:, :, cols_j], in_=hT[:, :, :]),
                   reads=HK, writes=[("o_y", cols_j.start)], dma=True)
        blocks = [(b, b * 128, 128, 0, False) for b in range(2 * j)]
        blocks += [(2 * j, 2 * j * 128, 128, 0, True), (2 * j + 1, (2 * j + 1) * 128, 128, 128, True)]
        segs = [dict(c0=0, L=TT, ctx=(latT, krT, lat_tok, 2), kpos=j * TT, blocks=blocks)]
        pass2_tile(TT, segs, j, cs_d[:, :, cols_j], out_p, fin_p)
        if j % 4 == 3 and j != NT - 1:
            S.new_epoch()

    S.barrier()
    replay_all()
    es2.close()
    es.close()
    return nc


def _blk(W, KC, c0, NB):
    K, N = W.shape
    nv = min(NB, N - c0)
    out = np.zeros((128, KC, NB), np.float32)
    out[:, :, :nv] = W[:, c0:c0 + nv].reshape(KC, 128, nv).transpose(1, 0, 2)
    res = np.zeros((128, WB), np.float32)
    res[:, :KC * NB] = out.reshape(128, KC * NB)
    return res


def _colvec(v):
    n = v.shape[0] // 128
    return np.ascontiguousarray(v.reshape(n, 128).T)


def _prep_shared(inp, SEQ):
    f = np.float32
    w_in = inp["ssd_w_in"][0]
    perm = (np.arange(32) + 16) % 32
    blocks0 = []
    for b in range(8):
        blocks0.append(_blk(w_in[:, 2048:6144], 8, b * 512, 512))
    for b in range(4):
        blocks0.append(_blk(w_in[:, 0:2048], 8, b * 512, 512))
    for b in range(4):
        blocks0.append(_blk(inp["ssd_w_out"][0], 16, b * 256, 256))

    def ffn_blocks(l):
        r = []
        for b in range(6):
            r.append(_blk(inp["ffn_w_gate"][l], 8, b * 512, 512))
        for b in range(6):
            r.append(_blk(inp["ffn_w_up"][l], 8, b * 512, 512))
        for b in range(8):
            r.append(_blk(inp["ffn_w_down"][l], 22, b * 128, 128))
        return r
    blocks0 += ffn_blocks(0)
    blocks1 = [_blk(inp["mla_wq_a"][0], 8, 0, 512)]
    wkv = inp["mla_wkv_a"][0]
    wkv2 = np.concatenate([wkv[:, :256], wkv[:, 256:288], wkv[:, 256:288][:, perm]], axis=1)
    blocks1.append(_blk(wkv2, 8, 0, 320))
    wqb = inp["mla_wq_b"][0].reshape(512, 16, 96)
    blocks1.append(_blk(np.ascontiguousarray(wqb[:, :, :64]).reshape(512, 1024), 4, 0, 1024))
    rope = wqb[:, :, 64:]
    blocks1.append(_blk(np.concatenate([rope.reshape(512, 512), rope[:, :, perm].reshape(512, 512)], axis=1), 4, 0, 1024))
    for b in range(2):
        blocks1.append(_blk(inp["mla_w_o"][0], 8, b * 512, 512))
    blocks1 += ffn_blocks(1)
    assert len(blocks0) == NB0 and len(blocks1) == NB1
    w0 = np.concatenate(blocks0, axis=0)
    w1 = np.concatenate(blocks1, axis=0)

    cols = np.zeros((128, NCOLS), f)
    lns = [inp["ln_mix_pre"][0], inp["ln_mix_pre"][1], inp["ln_mix_post"][0], inp["ln_mix_post"][1],
           inp["ln_ffn_pre"][0], inp["ln_ffn_pre"][1], inp["ln_ffn_post"][0], inp["ln_ffn_post"][1]]
    for i, v in enumerate(lns):
        cols[:, C_LN + 8 * i:C_LN + 8 * i + 8] = _colvec(v)
    cw = inp["ssd_conv_w"][0]
    cols[:, C_CW:C_CW + 128] = cw.T.reshape(32, 128, 4).transpose(1, 0, 2).reshape(128, 128)
    cols[:, C_CB:C_CB + 32] = _colvec(inp["ssd_conv_b"][0])
    cols[:, C_QN:C_QN + 4] = _colvec(inp["mla_q_norm"][0])
    cols[:, C_KN:C_KN + 2] = _colvec(inp["mla_kv_norm"][0])
    rows = np.zeros((128, NROWS), f)
    rows[:, R_DTB:R_DTB + 32] = inp["ssd_dt_bias"][0][None, :]
    rows[:, R_ALOG:R_ALOG + 32] = inp["ssd_a_log"][0][None, :]
    rows[:, R_D:R_D + 32] = inp["ssd_d"][0][None, :]
    rows[:, R_GN:R_GN + 2048] = inp["ssd_gate_norm"][0][None, :]
    wdt = np.ascontiguousarray(w_in[:, 6144:6176].reshape(8, 128, 32).transpose(1, 0, 2)).reshape(128, 256)
    wuk_src = inp["mla_w_uk"][0]
    wuk = wuk_src.transpose(1, 2, 0).reshape(8, 2, 64, 256).transpose(1, 2, 0, 3).reshape(128, 8 * 256)
    wuv_src = inp["mla_w_uv"][0]
    wuv = np.zeros((128, 2, 16, 128), f)
    for h in range(16):
        for j in range(2):
            wuv[:, j, h, (h % 2) * 64:(h % 2) * 64 + 64] = wuv_src[j * 128:(j + 1) * 128, h, :]
    wuv = wuv.reshape(128, 2 * 16 * 128)

    def rope_tab(pos):
        half = 16
        inv = (np.float32(10000.0) ** (-np.arange(half, dtype=np.float32) / np.float32(half))).astype(np.float32)
        ang = pos.astype(np.float32)[:, None] * inv[None, :]
        c, s_ = np.cos(ang).astype(np.float32), np.sin(ang).astype(np.float32)
        tab = np.zeros((32, 2, pos.shape[0]), np.float32)
        tab[:16, 0] = c.T
        tab[16:, 0] = c.T
        tab[:16, 1] = -s_.T
        tab[16:, 1] = s_.T
        return tab
    cs = rope_tab(np.arange(SEQ))
    c1 = rope_tab(PAST + np.arange(LS))
    css = np.concatenate([c1, c1], axis=2)
    consts = np.zeros((128, 512), f)
    i = np.arange(128)
    consts[:, 0:128] = (i[:, None] == i[None, :])
    consts[:, 128:256] = (i[:, None] <= i[None, :])
    consts[:, 256:384] = (i[:, None] > i[None, :])
    consts[:, 384:512] = 1.0
    return dict(w0=w0, w1=w1, cols=cols, rows=rows, wdt=wdt, wuk=np.ascontiguousarray(wuk), wuv=wuv, cs=cs, css=css, consts=consts)


_NT_OVERRIDE = [None]


def kernel(**inputs):
    inp = {k: np.asarray(v) for k, v in inputs.items()}
    B = inp["x_prompt"].shape[0]
    SEQ = inp["x_prompt"].shape[1]
    NT = SEQ // TT
    shared = _prep_shared(inp, SEQ)
    in_maps = []
    for b in range(B):
        m = dict(shared)
        m["xT"] = np.ascontiguousarray(inp["x_prompt"][b].T).reshape(8, 128, SEQ)
        xs = inp["x_sample"][2 * b:2 * b + 2].reshape(2 * LS, D_MODEL)
        m["xsT"] = np.ascontiguousarray(xs.T).reshape(8, 128, 2 * LS)
        cst = inp["state_ssd_conv"][0, 2 * b:2 * b + 2]
        m["convst"] = np.ascontiguousarray(cst.transpose(0, 2, 1).reshape(2, 32, 128, 3).transpose(0, 2, 1, 3)).reshape(2, 128, 96)
        sst = inp["state_ssd_ssm"][0, 2 * b:2 * b + 2]
        m["ssmst"] = np.ascontiguousarray(sst.transpose(0, 3, 1, 2)).reshape(2, 128, 2048)
        cl = inp["cache_mla_latent"][0, 2 * b:2 * b + 2]
        m["clat"] = np.ascontiguousarray(cl).reshape(2, 8, 128, 256)
        m["clatT"] = np.ascontiguousarray(cl.transpose(0, 2, 1)).reshape(2, 2, 128, PAST)
        ck = inp["cache_mla_krope"][0, 2 * b:2 * b + 2]
        m["ckrT"] = np.ascontiguousarray(ck.transpose(0, 2, 1))
        in_maps.append(m)
    nc = build_program(NT)
    res = run_bass_kernel_spmd(nc, in_maps, core_ids=list(range(B)))
    f = np.float32
    y_prompt = np.zeros((B, SEQ, D_MODEL), f)
    y_sample = np.zeros((2 * B, LS, D_MODEL), f)
    p_conv = np.zeros((1, B, 3, 4096), f)
    p_ssm = np.zeros((1, B, 32, 64, 128), f)
    p_lat = np.zeros((1, B, SEQ, 256), f)
    p_kr = np.zeros((1, B, SEQ, 32), f)
    s_conv = np.zeros((1, 2 * B, 3, 4096), f)
    s_ssm = np.zeros((1, 2 * B, 32, 64, 128), f)
    s_lat = np.zeros((1, 2 * B, LS, 256), f)
    s_kr = np.zeros((1, 2 * B, LS, 32), f)
    for b in range(B):
        r = res.results[b]
        y_prompt[b] = r["yT"].reshape(1024, SEQ).T
        ys = r["ysT"].reshape(1024, 2 * LS).T
        p_conv[0, b] = r["pconv"].reshape(128, 32, 3).transpose(2, 1, 0).reshape(3, 4096)
        p_ssm[0, b] = r["pssm"].T.reshape(32, 64, 128)
        p_lat[0, b] = r["platT"].reshape(256, SEQ).T
        p_kr[0, b] = r["pkrT"].T
        sl = r["slatT"].reshape(256, 2 * LS).T
        sk = r["skrT"].T
        for si in range(2):
            y_sample[2 * b + si] = ys[si * LS:(si + 1) * LS]
            s_conv[0, 2 * b + si] = r["sconv"][si].reshape(128, 32, 3).transpose(2, 1, 0).reshape(3, 4096)
            s_ssm[0, 2 * b + si] = r["sssm"][si].T.reshape(32, 64, 128)
            s_lat[0, 2 * b + si] = sl[si * LS:(si + 1) * LS]
            s_kr[0, 2 * b + si] = sk[si * LS:(si + 1) * LS]
    return (y_prompt, y_sample, p_conv, p_ssm, p_lat, p_kr, s_conv, s_ssm, s_lat, s_kr)
```

```python
import math
import numpy as np
import concourse.bass as bass
import concourse.mybir as mybir
from concourse.bass_utils import run_bass_kernel_spmd

F32 = mybir.dt.float32
BF16 = mybir.dt.bfloat16
AF = mybir.ActivationFunctionType
ALU = mybir.AluOpType
AX = mybir.AxisListType

D_MODEL = 1024
SEQ_FULL = 8192
PAST = 1024
LS = 32
TT = 256
FFN = 2816
EPS = 1e-6
SCALE = 1.0 / math.sqrt(96.0)
NSLOT = 3
WB = 4096

L0_BLOCKS = ([("xbc", 8, 512)] * 8 + [("z", 8, 512)] * 4 + [("wout", 16, 256)] * 4
             + [("gate", 8, 512)] * 6 + [("up", 8, 512)] * 6 + [("down", 22, 128)] * 8)
L1_BLOCKS = ([("wqa", 8, 512)] + [("wkva", 8, 320)] + [("wqb", 4, 1024)] * 2 + [("wo", 8, 512)] * 2
             + [("gate", 8, 512)] * 6 + [("up", 8, 512)] * 6 + [("down", 22, 128)] * 8)
NB0, NB1 = len(L0_BLOCKS), len(L1_BLOCKS)

C_LN = 0
C_CW = 64
C_CB = 192
C_QN = 224
C_KN = 228
NCOLS = 232
R_DTB, R_ALOG, R_D, R_GN = 0, 32, 64, 96
NROWS = 96 + 2048


class Sched:
    ENG = ("pe", "act", "dve", "pool", "sp")

    def __init__(self, nc, ndma=14):
        self.nc = nc
        self.ops = {e: [] for e in self.ENG}
        self.count = {e: 0 for e in self.ENG}
        self.waited = {e: {} for e in self.ENG}
        self.res = {}
        self.ndma = ndma
        self.epoch = 0
        self.nepoch = 12
        self.dma_val = {}
        self.dma_rr = {"sp": 0, "pool": 0, "act": 0}
        for q in ("sp", "pool"):
            for k in range(ndma):
                self.dma_val["d_%s_%d" % (q, k)] = 0

    def sem_names(self):
        return ["%s@%d" % (e, k) for e in self.ENG[:4] for k in range(self.nepoch)] + list(self.dma_val.keys())

    def cname(self, eng):
        return "%s@%d" % (eng, self.epoch)

    def new_epoch(self):
        self.barrier()
        self.epoch += 1
        assert self.epoch < self.nepoch
        for e in self.ENG:
            self.count[e] = 0
            self.waited[e] = {}

    def emit(self, eng, fn, reads=(), writes=(), dma=False):
        deps = []
        for r in reads:
            st = self.res.get(r)
            if st is not None and st[0] is not None:
                deps.append(st[0])
        for w in writes:
            st = self.res.get(w)
            if st is not None:
                if st[0] is not None:
                    deps.append(st[0])
                deps.extend(st[1])
        if dma:
            k = self.dma_rr[eng]
            self.dma_rr[eng] = (k + 1) % self.ndma
            sname = "d_%s_%d" % (eng, k)
            if self.dma_val[sname] > 0:
                deps.append((sname, self.dma_val[sname]))
            self.dma_val[sname] += 16
            token = (sname, self.dma_val[sname])
        else:
            self.count[eng] += 1
            token = (self.cname(eng), self.count[eng])
        waits = {}
        for (s, v) in deps:
            if s == self.cname(eng) and eng in ("pe", "sp"):
                continue
            if self.waited[eng].get(s, 0) >= v:
                continue
            if waits.get(s, 0) < v:
                waits[s] = v
        for s, v in waits.items():
            self.waited[eng][s] = v
        self.ops[eng].append((list(waits.items()), fn, token, dma))
        for w in writes:
            self.res[w] = [token, []]
        for r in reads:
            st = self.res.get(r)
            if st is None:
                self.res[r] = [None, [token]]
            else:
                st[1].append(token)
        return token

    def barrier(self):
        toks = [(self.cname(e), self.count[e]) for e in self.ENG if self.count[e] > 0]
        toks += [(s, v) for s, v in self.dma_val.items() if v > 0]
        for e in self.ENG:
            waits = []
            for (s, v) in toks:
                if s == self.cname(e):
                    continue
                if self.waited[e].get(s, 0) >= v:
                    continue
                self.waited[e][s] = v
                waits.append((s, v))
            if waits:
                self.ops[e].append((waits, None, None, False))
        self.res = {}

    def check(self, semvals):
        pos = {e: 0 for e in self.ENG}
        progress = True
        while progress:
            progress = False
            for e in self.ENG:
                while pos[e] < len(self.ops[e]):
                    waits, fn, token, dma = self.ops[e][pos[e]]
                    if any(semvals.get(s, 0) < v for s, v in waits):
                        break
                    if fn is not None:
                        semvals[token[0]] = semvals.get(token[0], 0) + (16 if dma else 1)
                        assert semvals[token[0]] == token[1], (e, pos[e], token, semvals[token[0]])
                    pos[e] += 1
                    progress = True
        stuck = {e: (pos[e], len(self.ops[e])) for e in self.ENG if pos[e] < len(self.ops[e])}
        for e in stuck:
            waits, fn, token, dma = self.ops[e][pos[e]]
            print("STUCK", e, pos[e], [(s, v, semvals.get(s, 0)) for s, v in waits if semvals.get(s, 0) < v], token)
        return not stuck

    def replay(self, eng, E, sems):
        for waits, fn, token, dma in self.ops[eng]:
            for s, v in waits:
                E.wait_ge(sems[s], v)
            if fn is None:
                continue
            ins = fn(E)
            ins.then_inc(sems[token[0]], 16 if dma else 1)


def build_program(NT):
    SEQ = NT * TT
    nc = bass.Bass("TRN2", target_bir_lowering=False)
    S = Sched(nc)

    def din(name, shape, dt=F32):
        return nc.dram_tensor(name, list(shape), dt, kind="ExternalInput").ap()

    def dout(name, shape, dt=F32):
        return nc.dram_tensor(name, list(shape), dt, kind="ExternalOutput").ap()

    xT = din("xT", [8, 128, SEQ])
    xsT = din("xsT", [8, 128, 2 * LS])
    w0 = din("w0", [NB0 * 128, WB])
    w1 = din("w1", [NB1 * 128, WB])
    cols_d = din("cols", [128, NCOLS])
    rows_d = din("rows", [128, NROWS])
    wdt_d = din("wdt", [128, 8 * 32])
    wuk_d = din("wuk", [128, 8 * 256])
    wuv_d = din("wuv", [128, 2 * 16 * 128])
    cs_d = din("cs", [32, 2, SEQ])
    css_d = din("css", [32, 2, 2 * LS])
    consts_d = din("consts", [128, 512])
    convst_d = din("convst", [2, 128, 96])
    ssmst_d = din("ssmst", [2, 128, 2048])
    clat_d = din("clat", [2, 8, 128, 256])
    clatT_d = din("clatT", [2, 2, 128, PAST])
    ckrT_d = din("ckrT", [2, 32, PAST])

    yT_o = dout("yT", [8, 128, SEQ])
    ysT_o = dout("ysT", [8, 128, 2 * LS])
    pconv_o = dout("pconv", [128, 96])
    pssm_o = dout("pssm", [128, 2048])
    platT_o = dout("platT", [2, 128, SEQ])
    pkrT_o = dout("pkrT", [32, SEQ])
    sconv_o = dout("sconv", [2, 128, 96])
    sssm_o = dout("sssm", [2, 128, 2048])
    slatT_o = dout("slatT", [2, 128, 2 * LS])
    skrT_o = dout("skrT", [32, 2 * LS])

    wb0 = nc.dram_tensor("wb0", [NB0 * 128, WB], BF16).ap()
    wb1 = nc.dram_tensor("wb1", [NB1 * 128, WB], BF16).ap()
    hs = nc.dram_tensor("hs", [NT + 1, 128, 8 * TT], F32).ap()

    from contextlib import ExitStack
    es = ExitStack()

    def sb(name, shape, dt=F32):
        return es.enter_context(nc.sbuf_tensor("sb_" + name, list(shape), dt))

    ps = [es.enter_context(nc.psum_tensor("ps%d" % i, [128, 512], F32)) for i in range(8)]
    psb = [p.bitcast(BF16) for p in ps]
    sems = {}
    for n in S.sem_names():
        sems[n] = es.enter_context(nc.semaphore("s_" + n))

    def PK(i):
        return ("ps", i)

    semvals = {}

    def replay_all():
        import os
        if os.environ.get("KCHECK"):
            print("deadlock check ok:", S.check(semvals))
        with nc.Block() as block:
            @block.tensor
            def _(E):
                S.replay("pe", E, sems)

            @block.scalar
            def _(E):
                S.replay("act", E, sems)

            @block.vector
            def _(E):
                S.replay("dve", E, sems)

            @block.gpsimd
            def _(E):
                S.replay("pool", E, sems)

            @block.sync
            def _(E):
                S.replay("sp", E, sems)
        for e in S.ENG:
            S.ops[e] = []

    consts = sb("consts", [128, 512])
    identb = sb("identb", [128, 128], BF16)
    onesb = sb("onesb", [128, 128], BF16)
    cols = sb("cols", [128, NCOLS])
    epsc = sb("epsc", [128, 1])
    wring = [sb("wring%d" % i, [128, WB], BF16) for i in range(NSLOT)]
    hT = sb("hT", [128, 8, TT])
    hT2 = sb("hT2", [128, 8, TT])
    aT = sb("aT", [128, 8, TT], BF16)
    mT = sb("mT", [128, 8, TT])
    sq = sb("sq", [128, 8, TT], BF16)
    rstd = sb("rstd", [128, TT])
    triu = consts[:, 128:256]
    lstr = consts[:, 256:384]
    onesf = consts[:, 384:512]

    S.emit("pool", lambda E: E.dma_start(out=consts[:], in_=consts_d[:, :]), writes=["consts"], dma=True)
    S.emit("pool", lambda E: E.dma_start(out=identb[:], in_=consts_d[:, 0:128]), writes=["identb"], dma=True)
    S.emit("pool", lambda E: E.dma_start(out=onesb[:], in_=consts_d[:, 384:512]), writes=["onesb"], dma=True)
    S.emit("pool", lambda E: E.dma_start(out=cols[:], in_=cols_d[:, :]), writes=["cols"], dma=True)
    S.emit("pool", lambda E: E.memset(epsc[:], EPS), writes=["epsc"])
    for (src, dst, nb) in ((w0, wb0, NB0), (w1, wb1, NB1)):
        for b0 in range(0, nb, 2):
            b1 = min(nb, b0 + 2)
            S.emit("pool", lambda E, src=src, dst=dst, b0=b0, b1=b1: E.dma_start(
                out=dst[b0 * 128:b1 * 128, :], in_=src[b0 * 128:b1 * 128, :]),
                writes=[("wb", id(dst) % 1000, b) for b in range(b0, b1)], dma=True)

    class WStream:
        def __init__(self):
            self.seq = []
            self.issued = 0
            self.used = 0

        def add(self, dram, tag, nblocks, reps):
            for _ in range(reps):
                for b in range(nblocks):
                    self.seq.append((dram, b, ("wb", tag, b)))

        def issue(self):
            if self.issued >= len(self.seq):
                return
            dram, b, key = self.seq[self.issued]
            slot = self.issued % NSLOT
            S.emit("sp", lambda E, dram=dram, b=b, slot=slot: E.dma_start(
                out=wring[slot][:], in_=dram[b * 128:(b + 1) * 128, :]),
                reads=[key], writes=[("ws", slot)], dma=True)
            self.issued += 1

        def start(self):
            for _ in range(NSLOT):
                self.issue()

        def next(self, name, KC, NB):
            i = self.used
            slot = i % NSLOT
            self.used += 1
            view = wring[slot][:, 0:KC * NB].rearrange("p (k n) -> p k n", k=KC)
            return view, ("ws", slot)

        def done(self):
            self.issue()

    WS = WStream()
    WS.add(wb0, id(wb0) % 1000, NB0, NT + 1)
    WS.add(wb1, id(wb1) % 1000, NB1, NT + 1)

    dense_rr = [0]

    def dbank():
        dense_rr[0] ^= 1
        return dense_rr[0]

    def ln_col(idx):
        return cols[:, C_LN + idx * 8: C_LN + idx * 8 + 8]

    HK = [("hT", k) for k in range(8)]
    HK2 = [("hT2", k) for k in range(8)]
    HB = [(hT, HK), (hT2, HK2)]
    H = [hT, HK]

    def use_buf(n):
        H[0], H[1] = HB[n % 2]
    MK = [("mT", k) for k in range(8)]

    def compute_rstd(src, srckeys, KC, D, T):
        S.emit("act", lambda E: E.activation(out=sq[:, 0:KC, 0:T], in_=src, func=AF.Square),
               reads=list(srckeys), writes=["sq"])

        def f(E):
            for kc in range(KC):
                ins = E.matmul(ps[2][:, 0:T], lhsT=onesb[:, :], rhs=sq[:, kc, 0:T], start=(kc == 0), stop=(kc == KC - 1))
            return ins
        S.emit("pe", f, reads=["sq", "onesb"], writes=[PK(2)])
        S.emit("act", lambda E: E.activation(out=rstd[:, 0:T], in_=ps[2][:, 0:T], func=AF.Sqrt, bias=epsc[:], scale=1.0 / D),
               reads=[PK(2), "epsc"], writes=["rstd"])
        S.emit("dve", lambda E: E.reciprocal(out=rstd[:, 0:T], in_=rstd[:, 0:T]), reads=["rstd"], writes=["rstd"])

    def norm_to(src3, srckeys, KC, D, T, gcols, dst3, dstkey):
        compute_rstd(src3[:, 0:KC, 0:T], srckeys, KC, D, T)
        for kc in range(KC):
            S.emit("dve", lambda E, kc=kc: E.scalar_tensor_tensor(
                out=dst3[:, kc, 0:T], in0=src3[:, kc, 0:T], scalar=gcols[:, kc:kc + 1], in1=rstd[:, 0:T],
                op0=ALU.mult, op1=ALU.mult), reads=list(srckeys) + ["rstd", "cols"], writes=[dstkey])

    def postnorm_residual(T, gcols):
        compute_rstd(mT[:, :, 0:T], MK, 8, D_MODEL, T)
        for kc in range(8):
            S.emit("dve", lambda E, kc=kc: E.scalar_tensor_tensor(
                out=mT[:, kc, 0:T], in0=mT[:, kc, 0:T], scalar=gcols[:, kc:kc + 1], in1=rstd[:, 0:T],
                op0=ALU.mult, op1=ALU.mult), reads=[("mT", kc), "rstd", "cols"], writes=[("mT", kc)])
            S.emit("pool", lambda E, kc=kc, hT_=H[0]: E.tensor_tensor(
                out=hT_[:, kc, 0:T], in0=hT_[:, kc, 0:T], in1=mT[:, kc, 0:T], op=ALU.add),
                reads=[("mT", kc), H[1][kc]], writes=[H[1][kc]])

    def dense_ws(name, KC, NB, nvalid, T, rhs_fn, rhskeys, evac):
        view, wkey = WS.next(name, KC, NB)
        for oc in range(nvalid // 128):
            bank = dbank()

            def f(E, oc=oc, bank=bank):
                for kc in range(KC):
                    ins = E.matmul(ps[bank][:, 0:T], lhsT=view[:, kc, oc * 128:(oc + 1) * 128], rhs=rhs_fn(kc),
                                   start=(kc == 0), stop=(kc == KC - 1))
                return ins
            S.emit("pe", f, reads=[wkey] + rhskeys, writes=[PK(bank)])
            evac(oc, bank)
        WS.done()

    def ffn(layer, T, hidT, mid_fn=None):
        norm_to(H[0], H[1], 8, D_MODEL, T, ln_col(4 + layer), aT, "aT")
        for b in range(6):
            nv = 512 if b < 5 else 256

            def ev(oc, bank, b=b):
                j = b * 4 + oc
                S.emit("act", lambda E: E.activation(out=hidT[:, j, 0:T], in_=ps[bank][:, 0:T], func=AF.Silu),
                       reads=[PK(bank)], writes=[("hid", j)])
            dense_ws("gate", 8, 512, nv, T, lambda kc: aT[:, kc, 0:T], ["aT"], ev)
        for b in range(6):
            nv = 512 if b < 5 else 256

            def ev(oc, bank, b=b):
                j = b * 4 + oc
                S.emit("dve", lambda E: E.tensor_tensor(out=hidT[:, j, 0:T], in0=ps[bank][:, 0:T], in1=hidT[:, j, 0:T], op=ALU.mult),
                       reads=[PK(bank), ("hid", j)], writes=[("hid", j)])
            dense_ws("up", 8, 512, nv, T, lambda kc: aT[:, kc, 0:T], ["aT"], ev)
        if mid_fn is not None:
            mid_fn()
        for b in range(8):
            def ev(oc, bank, b=b):
                S.emit("act", lambda E: E.copy(out=mT[:, b, 0:T], in_=ps[bank][:, 0:T]), reads=[PK(bank)], writes=[("mT", b)])
            dense_ws("down", 22, 128, 128, T, lambda kc: hidT[:, kc, 0:T], [("hid", j) for j in range(22)], ev)
        postnorm_residual(T, ln_col(6 + layer))

    es1 = ExitStack()

    def sb1(name, shape, dt=F32):
        return es1.enter_context(nc.sbuf_tensor("s1_" + name, list(shape), dt))

    rows = sb1("rows", [128, NROWS])
    abc = sb1("abc", [128, 32])
    wdt = sb1("wdt", [128, 8, 32], BF16)
    Sst = [sb1("Sst%d" % i, [128, 2048]) for i in range(2)]
    Sbf = [sb1("Sbf%d" % i, [128, 2048], BF16) for i in range(2)]
    tail = [sb1("tail%d" % i, [128, 32, 3]) for i in range(2)]
    xbcT = sb1("xbcT", [128, 32, TT], BF16)
    pre = [sb1("pre%d" % i, [128, 2 * (LS + 3) if False else TT + 8]) for i in range(2)]
    cacc = [sb1("cacc%d" % i, [128, TT]) for i in range(2)]
    cq2 = [sb1("cq2%d" % i, [128, TT]) for i in range(2)]
    cq3 = [sb1("cq3%d" % i, [128, TT]) for i in range(2)]
    sz = sb1("sz", [128, 2, 2048], BF16)
    dtb = sb1("dtb", [128, 2, 32])
    dtA = sb1("dtA", [128, 2, 32])
    x_tok = [sb1("x_tok%d" % i, [128, 2048], BF16) for i in range(2)]
    B_tok = [sb1("B_tok%d" % i, [128, 1024], BF16) for i in range(2)]
    Rm = [sb1("Rm%d" % i, [128, 512]) for i in range(2)]
    decay = [sb1("decay%d" % i, [128, 512], BF16) for i in range(2)]
    MT = [sb1("MT%d" % i, [128, 512], BF16) for i in range(2)]
    cbm = [sb1("cbm%d" % i, [128, 128], BF16) for i in range(2)]
    xdt = [sb1("xdt%d" % i, [128, 256], BF16) for i in range(2)]
    xw = [sb1("xw%d" % i, [128, 256], BF16) for i in range(2)]
    xD = [sb1("xD%d" % i, [128, 256], BF16) for i in range(2)]
    tmp = [sb1("tmp%d" % i, [128, 256]) for i in range(2)]
    yall = [sb1("yall%d" % i, [128, 2048]) for i in range(2)]
    ysq = sb1("ysq", [128, 2048], BF16)
    yn = sb1("yn", [128, 2048], BF16)
    ynT = sb1("ynT", [128, 16, TT], BF16)
    eacum = [sb1("eacum%d" % i, [128, 32]) for i in range(2)]
    Elast = [sb1("Elast%d" % i, [128, 32]) for i in range(2)]
    ss8 = sb1("ss8", [128, 8])
    hidT1 = sb1("hidT1", [128, 22, TT], BF16)

    S.emit("pool", lambda E: E.dma_start(out=rows[:], in_=rows_d[:, :]), writes=["rows"], dma=True)
    S.emit("pool", lambda E: E.dma_start(out=wdt[:].rearrange("p k n -> p (k n)"), in_=wdt_d[:, :]), writes=["wdt"], dma=True)
    S.emit("act", lambda E: E.activation(out=abc[:], in_=rows[:, R_ALOG:R_ALOG + 32], func=AF.Exp), reads=["rows"], writes=["abc"])
    S.emit("dve", lambda E: E.tensor_scalar(out=abc[:], in0=abc[:], scalar1=-1.0, scalar2=None, op0=ALU.mult), reads=["abc"], writes=["abc"])
    WS.start()

    def SKs(si):
        return [(("S", si), g) for g in range(8)]

    def SBKs(si):
        return [(("Sbf", si), g) for g in range(8)]

    def ssd_chunk_gen(ci, c0, L, si):
        tok = slice(c0, c0 + L)
        cp = ci % 2
        SK, SBK = ("S", si), ("Sbf", si)
        XT, BT, EA, EL, YA = x_tok[cp], B_tok[cp], eacum[cp], Elast[cp], yall[cp]
        kXT, kBT, kEA, kEL = ("x_tok", cp), ("B_tok", cp), ("eacum", cp), ("Elast", cp)
        YK = [("yall", cp, g) for g in range(8)]
        for half in range(2):
            bank = dbank()

            def f(E, half=half, bank=bank):
                for k in range(8):
                    ins = E.transpose(psb[bank][0:L, k * 128:(k + 1) * 128], xbcT[:, half * 8 + k, tok], identb[:, :])
                return ins
            S.emit("pe", f, reads=["xbcT", "identb"], writes=[PK(bank)])
            S.emit("act", lambda E, half=half, bank=bank: E.copy(out=XT[0:L, half * 1024:(half + 1) * 1024], in_=psb[bank][0:L, 0:1024]),
                   reads=[PK(bank)], writes=[kXT])
        bank = dbank()

        def f(E, bank=bank):
            for k in range(8):
                ins = E.transpose(psb[bank][0:L, k * 128:(k + 1) * 128], xbcT[:, 16 + k, tok], identb[:, :])
            return ins
        S.emit("pe", f, reads=["xbcT", "identb"], writes=[PK(bank)])
        S.emit("act", lambda E, bank=bank: E.copy(out=BT[0:L, :], in_=psb[bank][0:L, 0:1024]), reads=[PK(bank)], writes=[kBT])

        def f(E):
            E.matmul(ps[2][0:L, 0:32], lhsT=triu[0:L, 0:L], rhs=dtA[0:L, ci, :], start=True, stop=True)
            return E.matmul(ps[2][:, 32:64], lhsT=onesf[0:L, :], rhs=dtA[0:L, ci, :], start=True, stop=True)
        S.emit("pe", f, reads=["dtA", "consts"], writes=[PK(2)])
        S.emit("act", lambda E: E.activation(out=EA[0:L, :], in_=ps[2][0:L, 0:32], func=AF.Exp), reads=[PK(2)], writes=[kEA])
        S.emit("act", lambda E: E.activation(out=EL[:, :], in_=ps[2][:, 32:64], func=AF.Exp), reads=[PK(2)], writes=[kEL])
        yield "pro"

        def unit_gen(g):
            u = g % 2
            hs4 = slice(4 * g, 4 * g + 4)
            gcols = slice(256 * g, 256 * g + 256)
            W4 = 4 * L
            Rm_, dec_, MT_, cbm_, xdt_, xw_, xD_, tmp_ = Rm[u], decay[u], MT[u], cbm[u], xdt[u], xw[u], xD[u], tmp[u]
            kR, kD, kM, kC, kX, kW, kXD, kT = ("Rm", u), ("decay", u), ("MT", u), ("cbm", u), ("xdt", u), ("xw", u), ("xD", u), ("tmp", u)
            segb = 3 + u
            cbv = ps[5][0:L, 0:L] if u == 0 else ps[2][0:L, 256:256 + L]
            dbk = dbank()
            dsv, kds_ = ps[dbk][:, 0:256], PK(dbk)
            yv = ps[6][0:L, u * 256:(u + 1) * 256]
            ysv = ps[7][0:L, u * 256:(u + 1) * 256]
            kcb, kds, ky, kys = (PK(5) if u == 0 else PK(2)), kds_, ("ps6", u), ("ps7", u)
            xv = XT[0:L, gcols].rearrange("p (h d) -> p h d", h=4)
            S.emit("pool", lambda E: E.tensor_tensor(
                out=Rm_[0:L, 0:W4].rearrange("p (h q) -> p h q", h=4),
                in0=dtA[0:L, ci, hs4].unsqueeze(2).to_broadcast([L, 4, L]),
                in1=triu[0:L, 0:L].unsqueeze(1).to_broadcast([L, 4, L]), op=ALU.mult),
                reads=["dtA", "consts"], writes=[kR])
            S.emit("pe", lambda E: E.matmul(ps[segb][0:L, 0:W4], lhsT=lstr[0:L, 0:L], rhs=Rm_[0:L, 0:W4], start=True, stop=True),
                   reads=[kR, "consts"], writes=[PK(segb)])
            S.emit("act", lambda E: E.activation(out=dec_[0:L, 0:W4], in_=ps[segb][0:L, 0:W4], func=AF.Exp),
                   reads=[PK(segb)], writes=[kD])
            S.emit("pe", lambda E: E.matmul(cbv, lhsT=xbcT[:, 16 + g, tok], rhs=xbcT[:, 24 + g, tok], start=True, stop=True),
                   reads=["xbcT"], writes=[kcb])
            S.emit("dve", lambda E: E.tensor_tensor(out=cbm_[0:L, 0:L], in0=cbv, in1=triu[0:L, 0:L], op=ALU.mult),
                   reads=[kcb, "consts"], writes=[kC])
            S.emit("pool", lambda E: E.tensor_tensor(
                out=xdt_[0:L, :].rearrange("p (h d) -> p h d", h=4), in0=xv,
                in1=dtb[0:L, ci, hs4].unsqueeze(2).to_broadcast([L, 4, 64]), op=ALU.mult),
                reads=[kXT, "dtb"], writes=[kX])
            S.emit("pool", lambda E: E.tensor_tensor(
                out=xD_[0:L, :].rearrange("p (h d) -> p h d", h=4), in0=xv,
                in1=rows[0:L, R_D + 4 * g:R_D + 4 * g + 4].unsqueeze(2).to_broadcast([L, 4, 64]), op=ALU.mult),
                reads=[kXT, "rows"], writes=[kXD])
            yield "A"
            S.emit("dve", lambda E: E.tensor_tensor(
                out=MT_[0:L, 0:W4].rearrange("p (r q) -> p r q", r=4),
                in0=dec_[0:L, 0:W4].rearrange("p (r q) -> p r q", r=4),
                in1=cbm_[0:L, 0:L].unsqueeze(1).to_broadcast([L, 4, L]), op=ALU.mult),
                reads=[kD, kC], writes=[kM])
            S.emit("dve", lambda E: E.tensor_tensor(
                out=xw_[0:L, :].rearrange("p (h d) -> p h d", h=4), in0=xdt_[0:L, :].rearrange("p (h d) -> p h d", h=4),
                in1=dec_[0:L, 0:W4].rearrange("p (h q) -> p h q", h=4)[:, :, L - 1:L].to_broadcast([L, 4, 64]), op=ALU.mult),
                reads=[kX, kD], writes=[kW])

            def f(E):
                E.matmul(yv, lhsT=identb[0:L, 0:L], rhs=xD_[0:L, :], start=True, stop=False)
                for hh in range(4):
                    ins = E.matmul(yv[:, hh * 64:(hh + 1) * 64], lhsT=MT_[0:L, hh * L:(hh + 1) * L], rhs=xdt_[0:L, hh * 64:(hh + 1) * 64],
                                   start=False, stop=True, skip_group_check=True)
                return ins
            S.emit("pe", f, reads=[kM, kX, kXD, "identb"], writes=[ky])
            S.emit("pe", lambda E: E.matmul(ysv, lhsT=xbcT[:, 24 + g, tok], rhs=Sbf[si][:, g * 256:(g + 1) * 256], start=True, stop=True),
                   reads=["xbcT", (SBK, g)], writes=[kys])
            S.emit("pe", lambda E: E.matmul(dsv, lhsT=BT[0:L, g * 128:(g + 1) * 128], rhs=xw_[0:L, :], start=True, stop=True),
                   reads=[kBT, kW], writes=[kds])
            yield "B"
            S.emit("dve", lambda E: E.tensor_tensor(
                out=tmp_[0:L, :].rearrange("p (h d) -> p h d", h=4), in0=ysv.rearrange("p (h d) -> p h d", h=4),
                in1=EA[0:L, hs4].unsqueeze(2).to_broadcast([L, 4, 64]), op=ALU.mult),
                reads=[kys, kEA], writes=[kT])
            S.emit("dve", lambda E: E.tensor_tensor(out=tmp_[0:L, :], in0=yv, in1=tmp_[0:L, :], op=ALU.add),
                   reads=[ky, kT], writes=[kT])
            S.emit("dve", lambda E: E.tensor_tensor(out=YA[0:L, gcols], in0=tmp_[0:L, :], in1=sz[0:L, ci, gcols], op=ALU.mult),
                   reads=[kT, "sz"], writes=[("yall", cp, g)])
            S.emit("dve", lambda E: E.tensor_tensor(
                out=Sst[si][:, gcols].rearrange("p (h d) -> p h d", h=4), in0=Sst[si][:, gcols].rearrange("p (h d) -> p h d", h=4),
                in1=EL[:, hs4].unsqueeze(2).to_broadcast([128, 4, 64]), op=ALU.mult),
                reads=[(SK, g), kEL], writes=[(SK, g)])
            S.emit("dve", lambda E: E.tensor_tensor(out=Sst[si][:, gcols], in0=dsv, in1=Sst[si][:, gcols], op=ALU.add),
                   reads=[kds, (SK, g)], writes=[(SK, g)])
            S.emit("act", lambda E: E.copy(out=Sbf[si][:, gcols], in_=Sst[si][:, gcols]), reads=[(SK, g)], writes=[(SBK, g)])

        gens = [unit_gen(g) for g in range(8)]
        next(gens[0])
        for g in range(8):
            if g + 1 < 8:
                next(gens[g + 1])
            next(gens[g])
            next(gens[g], None)
            if g == 3:
                yield "mid"
        yield "units"
        S.emit("act", lambda E: E.activation(out=ysq[0:L, :], in_=YA[0:L, :], func=AF.Square), reads=YK, writes=["ysq"])
        S.emit("dve", lambda E: E.tensor_reduce(out=ss8[0:L, :], in_=ysq[0:L, :].rearrange("p (g c) -> p g c", g=8), axis=AX.X, op=ALU.add),
               reads=["ysq"], writes=["ss8"])
        S.emit("act", lambda E: E.activation(out=ss8[0:L, :], in_=ss8[0:L, :], func=AF.Sqrt, bias=epsc[0:L, :], scale=1.0 / 256.0),
               reads=["ss8", "epsc"], writes=["ss8"])
        S.emit("dve", lambda E: E.reciprocal(out=ss8[0:L, :], in_=ss8[0:L, :]), reads=["ss8"], writes=["ss8"])
        S.emit("dve", lambda E: E.tensor_tensor(
            out=YA[0:L, :].rearrange("p (g c) -> p g c", g=8), in0=YA[0:L, :].rearrange("p (g c) -> p g c", g=8),
            in1=ss8[0:L, :].unsqueeze(2).to_broadcast([L, 8, 256]), op=ALU.mult), reads=YK + ["ss8"], writes=YK)
        S.emit("pool", lambda E: E.tensor_tensor(out=yn[0:L, :], in0=YA[0:L, :], in1=rows[0:L, R_GN:R_GN + 2048], op=ALU.mult),
               reads=YK + ["rows"], writes=["yn"])
        for half in range(2):
            bank = dbank()

            def f(E, half=half, bank=bank):
                for k in range(8):
                    c = half * 8 + k
                    ins = E.transpose(psb[bank][:, k * L:(k + 1) * L], yn[0:L, c * 128:(c + 1) * 128], identb[0:L, 0:L])
                return ins
            S.emit("pe", f, reads=["yn", "identb"], writes=[PK(bank)])
            S.emit("act", lambda E, half=half, bank=bank: E.copy(
                out=ynT[:, half * 8:(half + 1) * 8, tok], in_=psb[bank][:, 0:8 * L].rearrange("p (k q) -> p k q", k=8)),
                reads=[PK(bank)], writes=["ynT"])
        yield "epi"

    def ssd_tile(chunk_list):
        gens = [ssd_chunk_gen(*c) for c in chunk_list]
        for gch in gens:
            next(gch)
        prev = None
        for gch in gens:
            next(gch)
            if prev is not None:
                next(prev)
            next(gch)
            prev = gch
        next(prev)

    def pass1_tile(T, segs, load_fn, hs_idx, prenormed=False, next_prenorm=None):
        load_fn()
        if not prenormed:
            norm_to(H[0], H[1], 8, D_MODEL, T, ln_col(0), aT, "aT")
        ci = 0
        chunk_list = []
        for (s0, Ls, si, chunks) in segs:
            for (c0, L) in chunks:
                chunk_list.append((ci, c0, L, si))

                def f(E, c0=c0, L=L):
                    for kc in range(8):
                        ins = E.matmul(ps[2][0:L, 0:32], lhsT=aT[:, kc, c0:c0 + L], rhs=wdt[:, kc, :], start=(kc == 0), stop=(kc == 7))
                    return ins
                S.emit("pe", f, reads=["aT", "wdt"], writes=[PK(2)])
                S.emit("dve", lambda E, L=L, ci=ci: E.tensor_tensor(out=dtb[0:L, ci, :], in0=ps[2][0:L, 0:32], in1=rows[0:L, R_DTB:R_DTB + 32], op=ALU.add),
                       reads=[PK(2), "rows"], writes=["dtb"])
                S.emit("act", lambda E, L=L, ci=ci: E.activation(out=dtb[0:L, ci, :], in_=dtb[0:L, ci, :], func=AF.Exp), reads=["dtb"], writes=["dtb"])
                S.emit("act", lambda E, L=L, ci=ci: E.activation(out=dtb[0:L, ci, :], in_=dtb[0:L, ci, :], func=AF.Ln, bias=1.0, scale=1.0),
                       reads=["dtb"], writes=["dtb"])
                S.emit("dve", lambda E, L=L, ci=ci: E.tensor_tensor(out=dtA[0:L, ci, :], in0=dtb[0:L, ci, :], in1=abc[0:L, :], op=ALU.mult),
                       reads=["dtb", "abc"], writes=["dtA"])
                ci += 1
        pend_silu = []
        for b in range(8):
            def ev(oc, bank, b=b):
                ch = b * 4 + oc
                pb = ch % 2
                P = pre[pb]
                pk, ak = ("pre", pb), ("cacc", pb)
                off = 0
                for (s0, Ls, si, chunks) in segs:
                    base = s0 + 3 * (segs.index((s0, Ls, si, chunks)))
                    S.emit("pool", lambda E, base=base, si=si: E.tensor_copy(out=P[:, base:base + 3], in_=tail[si][:, ch, :]),
                           reads=[("tail", si)], writes=[pk])
                    S.emit("act", lambda E, base=base, s0=s0, Ls=Ls: E.copy(out=P[:, base + 3:base + 3 + Ls], in_=ps[bank][:, s0:s0 + Ls]),
                           reads=[PK(bank)], writes=[pk])
                    S.emit("pool", lambda E, base=base, si=si, Ls=Ls: E.tensor_copy(out=tail[si][:, ch, :], in_=P[:, base + Ls:base + Ls + 3]),
                           reads=[pk], writes=[("tail", si)])
                    cw = cols[:, C_CW + ch * 4:C_CW + ch * 4 + 4]
                    A = cacc[pb]
                    Q2, Q3 = cq2[pb], cq3[pb]
                    qk2, qk3 = ("cq2", pb), ("cq3", pb)
                    S.emit("act", lambda E, s0=s0, Ls=Ls, cw=cw, Q3=Q3: E.activation(
                        out=Q3[:, s0:s0 + Ls], in_=ps[bank][:, s0:s0 + Ls], func=AF.Copy, scale=cw[:, 3:4]),
                        reads=[PK(bank), "cols"], writes=[qk3])
                    S.emit("act", lambda E, base=base, s0=s0, Ls=Ls, cw=cw, Q2=Q2: E.activation(
                        out=Q2[:, s0:s0 + Ls], in_=P[:, base + 2:base + 2 + Ls], func=AF.Copy, scale=cw[:, 2:3]),
                        reads=[pk, "cols"], writes=[qk2])
                    S.emit("pool", lambda E, s0=s0, Ls=Ls, Q2=Q2, Q3=Q3: E.tensor_tensor(
                        out=Q2[:, s0:s0 + Ls], in0=Q2[:, s0:s0 + Ls], in1=Q3[:, s0:s0 + Ls], op=ALU.add),
                        reads=[qk2, qk3], writes=[qk2])
                    S.emit("dve", lambda E, base=base, s0=s0, Ls=Ls, cw=cw, A=A: E.tensor_scalar(
                        out=A[:, s0:s0 + Ls], in0=P[:, base:base + Ls], scalar1=cw[:, 0:1], scalar2=None, op0=ALU.mult),
                        reads=[pk, "cols"], writes=[ak])
                    S.emit("dve", lambda E, base=base, s0=s0, Ls=Ls, cw=cw, A=A: E.scalar_tensor_tensor(
                        out=A[:, s0:s0 + Ls], in0=P[:, base + 1:base + 1 + Ls], scalar=cw[:, 1:2], in1=A[:, s0:s0 + Ls],
                        op0=ALU.mult, op1=ALU.add), reads=[pk, ak, "cols"], writes=[ak])
                    S.emit("dve", lambda E, s0=s0, Ls=Ls, A=A, Q2=Q2: E.tensor_tensor(
                        out=A[:, s0:s0 + Ls], in0=A[:, s0:s0 + Ls], in1=Q2[:, s0:s0 + Ls], op=ALU.add),
                        reads=[ak, qk2], writes=[ak])
                def silu(A=cacc[pb], ch=ch, ak=ak):
                    S.emit("act", lambda E: E.activation(out=xbcT[:, ch, 0:T], in_=A[:, 0:T], func=AF.Silu,
                                                         bias=cols[:, C_CB + ch:C_CB + ch + 1], scale=1.0),
                           reads=[ak, "cols"], writes=["xbcT"])
                if pend_silu:
                    pend_silu.pop()()
                pend_silu.append(silu)
            dense_ws("xbc", 8, 512, 512, T, lambda kc: aT[:, kc, 0:T], ["aT"], ev)
        if pend_silu:
            pend_silu.pop()()
        for b in range(4):
            view, wkey = WS.next("z", 8, 512)
            for (ci, c0, L, si) in chunk_list:
                bank = dbank()

                def f(E, c0=c0, L=L, bank=bank, view=view):
                    for kc in range(8):
                        ins = E.matmul(ps[bank][0:L, 0:512], lhsT=aT[:, kc, c0:c0 + L], rhs=view[:, kc, :], start=(kc == 0), stop=(kc == 7))
                    return ins
                S.emit("pe", f, reads=[wkey, "aT"], writes=[PK(bank)])
                S.emit("act", lambda E, L=L, ci=ci, bank=bank, b=b: E.activation(out=sz[0:L, ci, b * 512:(b + 1) * 512], in_=ps[bank][0:L, 0:512], func=AF.Silu),
                       reads=[PK(bank)], writes=["sz"])
            WS.done()
        ssd_tile(chunk_list)
        for b in range(4):
            def ev(oc, bank, b=b):
                S.emit("act", lambda E: E.copy(out=mT[:, b * 2 + oc, 0:T], in_=ps[bank][:, 0:T]), reads=[PK(bank)], writes=[("mT", b * 2 + oc)])
            dense_ws("wout", 16, 256, 256, T, lambda kc: ynT[:, kc, 0:T], ["ynT"], ev)
        postnorm_residual(T, ln_col(2))
        ffn(0, T, hidT1, next_prenorm)
        S.emit("pool", lambda E, hT_=H[0]: E.dma_start(out=hs[hs_idx].rearrange("p (k t) -> p k t", k=8)[:, :, 0:T], in_=hT_[:, :, 0:T]),
               reads=H[1], writes=[("hs", hs_idx)], dma=True)

    for si in range(2):
        S.emit("pool", lambda E, si=si: E.dma_start(out=Sst[si][:], in_=ssmst_d[si]), writes=SKs(si), dma=True)
        S.emit("pool", lambda E, si=si: E.dma_start(out=tail[si][:].rearrange("p c k -> p (c k)"), in_=convst_d[si]), writes=[("tail", si)], dma=True)
        S.emit("act", lambda E, si=si: E.copy(out=Sbf[si][:], in_=Sst[si][:]), reads=SKs(si), writes=SBKs(si))

    def load1(n):
        buf, keys = HB[n % 2]
        if n == 0:
            S.emit("pool", lambda E: E.dma_start(out=buf[:, :, 0:2 * LS], in_=xsT.rearrange("k p t -> p k t")), writes=keys, dma=True)
        elif n <= NT:
            j = n - 1
            S.emit("pool", lambda E: E.dma_start(out=buf[:, :, :], in_=xT.rearrange("k p t -> p k t")[:, :, j * TT:(j + 1) * TT]),
                   writes=keys, dma=True)
    load1(0)
    use_buf(0)
    def prenorm_next(n, lncol):
        if n > NT:
            return None
        buf, keys = HB[n % 2]
        return lambda: norm_to(buf, keys, 8, D_MODEL, TT, ln_col(lncol), aT, "aT")
    pass1_tile(2 * LS, [(0, LS, 0, [(0, LS)]), (LS, LS, 1, [(LS, LS)])], lambda: load1(1), NT, False, prenorm_next(1, 0))
    for si in range(2):
        S.emit("pool", lambda E, si=si: E.dma_start(out=sssm_o[si], in_=Sst[si][:]), reads=SKs(si), writes=[("o_sssm", si)], dma=True)
        S.emit("pool", lambda E, si=si: E.dma_start(out=sconv_o[si], in_=tail[si][:].rearrange("p c k -> p (c k)")),
               reads=[("tail", si)], writes=[("o_sconv", si)], dma=True)
    S.emit("pool", lambda E: E.memset(Sst[0][:], 0.0), writes=SKs(0))
    S.emit("pool", lambda E: E.memset(Sbf[0][:], 0.0), writes=SBKs(0))
    S.emit("pool", lambda E: E.memset(tail[0][:], 0.0), writes=[("tail", 0)])
    for j in range(NT):
        use_buf(j + 1)
        pass1_tile(TT, [(0, TT, 0, [(0, 128), (128, 128)])], (lambda j=j: load1(j + 2)), j, True, prenorm_next(j + 2, 0))
        if j % 16 == 15 and j != NT - 1:
            S.new_epoch()
    S.emit("pool", lambda E: E.dma_start(out=pssm_o[:, :], in_=Sst[0][:]), reads=SKs(0), writes=["o_pssm"], dma=True)
    S.emit("pool", lambda E: E.dma_start(out=pconv_o[:, :], in_=tail[0][:].rearrange("p c k -> p (c k)")), reads=[("tail", 0)], writes=["o_pconv"], dma=True)

    S.new_epoch()
    replay_all()
    es1.close()

    es2 = ExitStack()

    def sb2(name, shape, dt=F32):
        return es2.enter_context(nc.sbuf_tensor("s2_" + name, list(shape), dt))

    NBLK = SEQ // 128
    latT = sb2("latT", [128, 2, SEQ], BF16)
    krT = sb2("krT", [32, SEQ], BF16)
    lat_tok = sb2("lat_tok", [128, NBLK, 256], BF16)
    if NBLK >= 63:
        flat = lat_tok[:].rearrange("p b c -> p (b c)")
        off = [20 * 256]

        def carve(n):
            v = flat[:, off[0]:off[0] + n]
            off[0] += n
            return v
        s_latT = [carve(2 * (PAST + LS)).rearrange("p (j t) -> p j t", j=2) for i in range(2)]
        s_krT = [carve(PAST + LS)[0:32, :] for i in range(2)]
        s_lat_tok = [carve(9 * 256).rearrange("p (b c) -> p b c", b=9) for i in range(2)]
        assert off[0] <= NBLK * 256
    else:
        s_latT = [sb2("s_latT%d" % i, [128, 2, PAST + LS], BF16) for i in range(2)]
        s_krT = [sb2("s_krT%d" % i, [32, PAST + LS], BF16) for i in range(2)]
        s_lat_tok = [sb2("s_lat_tok%d" % i, [128, 9, 256], BF16) for i in range(2)]
    wuk = sb2("wuk", [128, 8, 256], BF16)
    wuv = sb2("wuv", [128, 2, 16, 128], BF16)
    cqT = mT[:, 0:4, :]
    cqnT = sb2("cqnT", [128, 4, TT], BF16)
    latraw = sb2("latraw", [128, 2, TT])
    latn = sb2("latn", [128, 2, TT])
    krf = sb2("krf", [32, 3, TT])
    cst = sb2("cst", [32, 2, TT])
    qq = sb2("qq", [128, 24 * TT], BF16)
    qnT = qq[:, 0:8 * TT].rearrange("p (k t) -> p k t", k=8)
    qrT = qq[0:32, 8 * TT:24 * TT].rearrange("p (k t) -> p k t", k=16)
    hidT2 = qq[:, 0:22 * TT].rearrange("p (k t) -> p k t", k=22)
    qlT = [sb2("qlT%d" % i, [128, 2, 2, TT], BF16) for i in range(2)]
    pT = [sb2("pT%d" % i, [128, 2 * TT], BF16) for i in range(3)]
    osb = sb2("osb", [128, 3, 2 * TT])
    pacc = [sb2("pacc%d" % i, [128, 2 * TT]) for i in range(2)]
    olT = [sb2("olT%d" % i, [128, 2, 2, TT], BF16) for i in range(2)]
    oT = sb2("oT", [128, 8, TT], BF16)

    S.emit("pool", lambda E: E.dma_start(out=wuk[:].rearrange("p a c -> p (a c)"), in_=wuk_d[:, :]), writes=["wuk"], dma=True)
    S.emit("pool", lambda E: E.dma_start(out=wuv[:].rearrange("p a b c -> p (a b c)"), in_=wuv_d[:, :]), writes=["wuv"], dma=True)
    for si in range(2):
        S.emit("pool", lambda E, si=si: E.dma_start(out=s_lat_tok[si][:, 0:8, :], in_=clat_d[si].rearrange("b p c -> p b c")),
               writes=[("ltok", si)], dma=True)
        S.emit("pool", lambda E, si=si: E.dma_start(out=s_latT[si][:, :, 0:PAST], in_=clatT_d[si].rearrange("j p t -> p j t")),
               writes=[("latT", si)], dma=True)
        S.emit("pool", lambda E, si=si: E.dma_start(out=s_krT[si][:, 0:PAST], in_=ckrT_d[si]), writes=[("krT", si)], dma=True)

    def load2(n):
        if n > NT:
            return
        buf, keys = HB[n % 2]
        idx, T_ = (NT, 2 * LS) if n == 0 else (n - 1, TT)
        S.emit("pool", lambda E: E.dma_start(out=buf[:, :, 0:T_], in_=hs[idx].rearrange("p (k t) -> p k t", k=8)[:, :, 0:T_]),
               reads=[("hs", idx)], writes=keys, dma=True)

    def pass2_tile(T, segs, tile_n, cs_src, out_fn, final_fn, prenormed=False, next_prenorm=None):
        use_buf(tile_n)
        load2(tile_n + 1)
        S.emit("pool", lambda E: E.dma_start(out=cst[:, :, 0:T], in_=cs_src), writes=["cst"], dma=True)
        if not prenormed:
            norm_to(H[0], H[1], 8, D_MODEL, T, ln_col(1), aT, "aT")

        def ev(oc, bank):
            S.emit("act", lambda E: E.copy(out=cqT[:, oc, 0:T], in_=ps[bank][:, 0:T]), reads=[PK(bank)], writes=["cqT", ("mT", oc)])
        dense_ws("wqa", 8, 512, 512, T, lambda kc: aT[:, kc, 0:T], ["aT"], ev)
        norm_to(cqT, ["cqT"] + [("mT", k) for k in range(4)], 4, 512, T, cols[:, C_QN:C_QN + 4], cqnT, "cqnT")
        view, wkey = WS.next("wkva", 8, 320)
        for oc in range(2):
            bank = dbank()

            def f(E, oc=oc, bank=bank, view=view):
                for kc in range(8):
                    ins = E.matmul(ps[bank][:, 0:T], lhsT=view[:, kc, oc * 128:(oc + 1) * 128], rhs=aT[:, kc, 0:T], start=(kc == 0), stop=(kc == 7))
                return ins
            S.emit("pe", f, reads=[wkey, "aT"], writes=[PK(bank)])
            S.emit("act", lambda E, oc=oc, bank=bank: E.copy(out=latraw[:, oc, 0:T], in_=ps[bank][:, 0:T]), reads=[PK(bank)], writes=["latraw"])
        bank = dbank()

        def f(E, bank=bank, view=view):
            for w in range(2):
                for kc in range(8):
                    ins = E.matmul(ps[bank][0:32, w * T:(w + 1) * T], lhsT=view[:, kc, 256 + 32 * w:288 + 32 * w], rhs=aT[:, kc, 0:T],
                                   start=(kc == 0), stop=(kc == 7))
            return ins
        S.emit("pe", f, reads=[wkey, "aT"], writes=[PK(bank)])
        WS.done()
        S.emit("dve", lambda E, bank=bank: E.tensor_tensor(out=krf[:, 0, 0:T], in0=ps[bank][0:32, 0:T], in1=cst[:, 0, 0:T], op=ALU.mult),
               reads=[PK(bank), "cst"], writes=["krf0"])
        S.emit("dve", lambda E, bank=bank: E.tensor_tensor(out=krf[:, 1, 0:T], in0=ps[bank][0:32, T:2 * T], in1=cst[:, 1, 0:T], op=ALU.mult),
               reads=[PK(bank), "cst"], writes=["krf1"])
        S.emit("dve", lambda E: E.tensor_tensor(out=krf[:, 2, 0:T], in0=krf[:, 0, 0:T], in1=krf[:, 1, 0:T], op=ALU.add),
               reads=["krf0", "krf1"], writes=["krf2"])
        norm_to(latraw, ["latraw"], 2, 256, T, cols[:, C_KN:C_KN + 2], latn, "latn")
        out_fn()
        for sg in segs:
            c0, L, ctx, kp = sg["c0"], sg["L"], sg["ctx"], sg["kpos"]
            cl, ck, ct, ci = ctx
            S.emit("act", lambda E, c0=c0, L=L, cl=cl, kp=kp: E.copy(out=cl[:, :, kp:kp + L], in_=latn[:, :, c0:c0 + L]),
                   reads=["latn"], writes=[("latT", ci)])
            S.emit("act", lambda E, c0=c0, L=L, ck=ck, kp=kp: E.copy(out=ck[:, kp:kp + L], in_=krf[:, 2, c0:c0 + L]),
                   reads=["krf2"], writes=[("krT", ci)])
            nch = (L + 127) // 128
            for cc in range(nch):
                Lc = min(128, L - cc * 128)
                blk = (kp + cc * 128) // 128
                bank = dbank()

                def f(E, cl=cl, kp=kp, cc=cc, Lc=Lc, bank=bank):
                    for j in range(2):
                        ins = E.transpose(psb[bank][0:Lc, j * 128:(j + 1) * 128], cl[:, j, kp + cc * 128:kp + cc * 128 + Lc], identb[:, :])
                    return ins
                S.emit("pe", f, reads=[("latT", ci), "identb"], writes=[PK(bank)])
                S.emit("act", lambda E, ct=ct, blk=blk, Lc=Lc, bank=bank: E.copy(out=ct[0:Lc, blk, :], in_=psb[bank][0:Lc, 0:256]),
                       reads=[PK(bank)], writes=[("ltok", ci)])
        view, wkey = WS.next("wqb", 4, 1024)
        for i in range(8):
            bank = dbank()

            def f(E, i=i, bank=bank, view=view):
                for kc in range(4):
                    ins = E.matmul(ps[bank][:, 0:T], lhsT=view[:, kc, i * 128:(i + 1) * 128], rhs=cqnT[:, kc, 0:T], start=(kc == 0), stop=(kc == 3))
                return ins
            S.emit("pe", f, reads=[wkey, "cqnT"], writes=[PK(bank)])
            S.emit("act", lambda E, i=i, bank=bank: E.copy(out=qnT[:, i, 0:T], in_=ps[bank][:, 0:T]), reads=[PK(bank)], writes=["qnT"])
        WS.done()
        view, wkey = WS.next("wqb", 4, 1024)
        for h in range(16):
            bank = dbank()

            def f(E, h=h, bank=bank, view=view):
                for w in range(2):
                    for kc in range(4):
                        ins = E.matmul(ps[bank][0:32, w * T:(w + 1) * T], lhsT=view[:, kc, w * 512 + h * 32:w * 512 + h * 32 + 32],
                                       rhs=cqnT[:, kc, 0:T], start=(kc == 0), stop=(kc == 3))
                return ins
            S.emit("pe", f, reads=[wkey, "cqnT"], writes=[PK(bank)])
            S.emit("dve", lambda E, bank=bank: E.tensor_tensor(out=krf[:, 0, 0:T], in0=ps[bank][0:32, 0:T], in1=cst[:, 0, 0:T], op=ALU.mult),
                   reads=[PK(bank), "cst"], writes=["krf0"])
            S.emit("dve", lambda E, bank=bank: E.tensor_tensor(out=krf[:, 1, 0:T], in0=ps[bank][0:32, T:2 * T], in1=cst[:, 1, 0:T], op=ALU.mult),
                   reads=[PK(bank), "cst"], writes=["krf1"])
            S.emit("dve", lambda E, h=h: E.tensor_tensor(out=qrT[:, h, 0:T], in0=krf[:, 0, 0:T], in1=krf[:, 1, 0:T], op=ALU.add),
                   reads=["krf0", "krf1"], writes=["qrT"])
        WS.done()
        def emit_qlat(i):
            QL = qlT[i % 2]
            qk = ("qlT", i % 2)
            for e in range(2):
                for j in range(2):
                    bank = dbank()
                    S.emit("pe", lambda E, e=e, j=j, bank=bank, i=i: E.matmul(
                        ps[bank][:, 0:T], lhsT=wuk[64 * e:64 * e + 64, i, j * 128:(j + 1) * 128], rhs=qnT[64 * e:64 * e + 64, i, 0:T],
                        start=True, stop=True), reads=["wuk", "qnT"], writes=[PK(bank)])
                    S.emit("act", lambda E, e=e, j=j, bank=bank, QL=QL: E.copy(out=QL[:, j, e, 0:T], in_=ps[bank][:, 0:T]),
                           reads=[PK(bank)], writes=[qk])

        def emit_wuv(i):
            OL = olT[i % 2]
            ok = ("olT", i % 2)
            bank = dbank()

            def f(E, i=i, bank=bank, OL=OL):
                n = 0
                for e in range(2):
                    for j in range(2):
                        ins = E.matmul(ps[bank][:, 0:T], lhsT=wuv[:, j, 2 * i + e, :], rhs=OL[:, j, e, 0:T], start=(n == 0), stop=(n == 3))
                        n += 1
                return ins
            S.emit("pe", f, reads=["wuv", ok], writes=[PK(bank)])
            S.emit("act", lambda E, i=i, bank=bank: E.copy(out=oT[:, i, 0:T], in_=ps[bank][:, 0:T]), reads=[PK(bank)], writes=["oT"])

        pti = [0]
        pai = [0]
        pend_fin = []
        emit_qlat(0)
        for i in range(8):
            QL = qlT[i % 2]
            OL = olT[i % 2]
            qk, ok = ("qlT", i % 2), ("olT", i % 2)
            if i + 1 < 8:
                emit_qlat(i + 1)
            for sgi, sg in enumerate(segs):
                c0, L, ctx = sg["c0"], sg["L"], sg["ctx"]
                cl, ck, ct, ci = ctx
                blocks = sg["blocks"]
                nb = len(blocks)
                slots = []
                PA = pacc[pai[0] % 2]
                pak = ("pacc", pai[0] % 2)
                pai[0] += 1

                def emit_qk(bi):
                    blk, key0, nk, qa, diag = blocks[bi]
                    sbank = 3 + (pti[0] % 2)
                    P = pT[pti[0] % 3]
                    pk = ("pT", pti[0] % 3)
                    pti[0] += 1
                    n = L - qa
                    qs = slice(c0 + qa, c0 + L)
                    sv = ps[sbank][0:nk, 0:2 * n].rearrange("p (e q) -> p e q", e=2)

                    def f(E, cl=cl, ck=ck, QL=QL, i=i):
                        E.matmul(sv, lhsT=cl[:, 0, key0:key0 + nk], rhs=QL[:, 0, :, qs], start=True, stop=False)
                        E.matmul(sv, lhsT=cl[:, 1, key0:key0 + nk], rhs=QL[:, 1, :, qs], start=False, stop=False)
                        return E.matmul(sv, lhsT=ck[:, key0:key0 + nk], rhs=qrT[:, 2 * i:2 * i + 2, qs], start=False, stop=True)
                    S.emit("pe", f, reads=[("latT", ci), ("krT", ci), qk, "qrT"], writes=[PK(sbank)])
                    S.emit("act", lambda E: E.activation(out=P[0:nk, 0:2 * n], in_=ps[sbank][0:nk, 0:2 * n], func=AF.Exp, scale=SCALE),
                           reads=[PK(sbank)], writes=[pk])
                    if diag:
                        S.emit("pool", lambda E: E.memset(P[64:128, 0:2 * n].rearrange("p (e q) -> p e q", e=2)[:, :, 0:64], 0.0),
                               reads=[pk], writes=[pk])
                    slots.append((P, pk, n))

                def emit_pv(bi):
                    blk, key0, nk, qa, diag = blocks[bi]
                    P, pk, n = slots[bi]
                    first, last = (bi == 0), (bi == nb - 1)
                    pv = P[0:nk, 0:2 * n].rearrange("p (e q) -> p e q", e=2)

                    def f(E, ct=ct, L=L):
                        o0 = ps[5][:, 0:2 * L].rearrange("p (e q) -> p e q", e=2)[:, :, qa:qa + n]
                        o1 = ps[6][:, 0:2 * L].rearrange("p (e q) -> p e q", e=2)[:, :, qa:qa + n]
                        E.matmul(o0, lhsT=ct[0:nk, blk, 0:128], rhs=pv, start=first, stop=last, skip_group_check=True)
                        return E.matmul(o1, lhsT=ct[0:nk, blk, 128:256], rhs=pv, start=first, stop=last, skip_group_check=True)
                    S.emit("pe", f, reads=[("ltok", ci), pk], writes=[PK(5), PK(6)])
                    pav = PA[0:nk, 0:2 * L].rearrange("p (e q) -> p e q", e=2)[:, :, qa:qa + n]
                    if first:
                        S.emit("dve", lambda E, pav=pav, pv=pv: E.tensor_copy(out=pav, in_=pv), reads=[pk], writes=[pak])
                    else:
                        S.emit("dve", lambda E, pav=pav, pv=pv: E.tensor_tensor(out=pav, in0=pav, in1=pv, op=ALU.add), reads=[pk, pak], writes=[pak])

                emit_qk(0)
                if nb > 1:
                    emit_qk(1)
                if pend_fin:
                    pend_fin.pop()()
                for bi in range(nb):
                    if bi >= 1 and bi + 1 < nb:
                        emit_qk(bi + 1)
                    emit_pv(bi)
                    if bi == min(2, nb - 1) and sgi == 0 and i > 0:
                        emit_wuv(i - 1)
                S.emit("act", lambda E, L=L: E.copy(out=osb[:, 0, 0:2 * L], in_=ps[5][:, 0:2 * L]), reads=[PK(5)], writes=["osb0"])
                S.emit("dve", lambda E, L=L: E.tensor_copy(out=osb[:, 1, 0:2 * L], in_=ps[6][:, 0:2 * L]), reads=[PK(6)], writes=["osb1"])

                def fin(L=L, c0=c0, OL=OL, PA=PA, pak=pak, ok=ok):
                    S.emit("pe", lambda E: E.matmul(ps[7][:, 0:2 * L], lhsT=onesf[:, :], rhs=PA[:, 0:2 * L], start=True, stop=True),
                           reads=[pak, "consts"], writes=[PK(7)])
                    S.emit("act", lambda E: E.copy(out=osb[:, 2, 0:2 * L], in_=ps[7][:, 0:2 * L]), reads=[PK(7)], writes=["osb2"])
                    S.emit("dve", lambda E: E.reciprocal(out=osb[:, 2, 0:2 * L], in_=osb[:, 2, 0:2 * L]), reads=["osb2"], writes=["osb2"])
                    for j in range(2):
                        S.emit("dve" if j == 0 else "pool", lambda E, j=j: E.tensor_tensor(
                            out=OL[:, j, :, c0:c0 + L], in0=osb[:, j, 0:2 * L].rearrange("p (e q) -> p e q", e=2),
                            in1=osb[:, 2, 0:2 * L].rearrange("p (e q) -> p e q", e=2), op=ALU.mult),
                            reads=["osb%d" % j, "osb2"], writes=[ok])
                pend_fin.append(fin)
        if pend_fin:
            pend_fin.pop()()
        emit_wuv(7)
        for b in range(2):
            def ev(oc, bank, b=b):
                S.emit("act", lambda E: E.copy(out=mT[:, b * 4 + oc, 0:T], in_=ps[bank][:, 0:T]), reads=[PK(bank)], writes=[("mT", b * 4 + oc)])
            dense_ws("wo", 8, 512, 512, T, lambda kc: oT[:, kc, 0:T], ["oT"], ev)
        postnorm_residual(T, ln_col(3))
        ffn(1, T, hidT2, next_prenorm)
        final_fn()

    def out_sample():
        S.emit("pool", lambda E: E.dma_start(out=slatT_o.rearrange("j p t -> p j t"), in_=latn[:, :, 0:2 * LS]), reads=["latn"], writes=["o_slat"], dma=True)
        S.emit("pool", lambda E: E.dma_start(out=skrT_o[:, :], in_=krf[:, 2, 0:2 * LS]), reads=["krf2"], writes=["o_skr"], dma=True)

    def fin_sample():
        S.emit("pool", lambda E, hT_=H[0]: E.dma_start(out=ysT_o.rearrange("k p t -> p k t"), in_=hT_[:, :, 0:2 * LS]), reads=H[1], writes=["o_ys"], dma=True)
    segs = []
    for si in range(2):
        blocks = [(b, b * 128, 128, 0, False) for b in range(8)] + [(8, PAST, LS, 0, False)]
        segs.append(dict(c0=si * LS, L=LS, ctx=(s_latT[si], s_krT[si], s_lat_tok[si], si), kpos=PAST, blocks=blocks))
    load2(0)
    pass2_tile(2 * LS, segs, 0, css_d[:, :, :], out_sample, fin_sample, False, prenorm_next(1, 1))

    for j in range(NT):
        cols_j = slice(j * TT, (j + 1) * TT)

        def out_p(cols_j=cols_j):
            S.emit("pool", lambda E: E.dma_start(out=platT_o.rearrange("j p t -> p j t")[:, :, cols_j], in_=latn[:, :, :]),
                   reads=["latn"], writes=[("o_plat", cols_j.start)], dma=True)
            S.emit("pool", lambda E: E.dma_start(out=pkrT_o[:, cols_j], in_=krf[:, 2, :]), reads=["krf2"], writes=[("o_pkr", cols_j.start)], dma=True)

        def fin_p(cols_j=cols_j):
            S.emit("pool", lambda E, hT_=H[0]: E.dma_start(out=yT_o.rearrange("k p t -> p k t")[:, :, cols_j], in_=hT_[:, :, :]),
                   reads=H[1], writes=[("o_y", cols_j.start)], dma=True)
        blocks = [(b, b * 128, 128, 0, False) for b in range(2 * j)]
        blocks += [(2 * j, 2 * j * 128, 128, 0, True), (2 * j + 1, (2 * j + 1) * 128, 128, 128, True)]
        segs = [dict(c0=0, L=TT, ctx=(latT, krT, lat_tok, 2), kpos=j * TT, blocks=blocks)]
        pass2_tile(TT, segs, j + 1, cs_d[:, :, cols_j], out_p, fin_p, True, prenorm_next(j + 2, 1))
        if j % 4 == 3 and j != NT - 1:
            S.new_epoch()

    S.barrier()
    replay_all()
    es2.close()
    es.close()
    return nc


def _blk(W, KC, c0, NB):
    K, N = W.shape
    nv = min(NB, N - c0)
    out = np.zeros((128, KC, NB), np.float32)
    out[:, :, :nv] = W[:, c0:c0 + nv].reshape(KC, 128, nv).transpose(1, 0, 2)
    res = np.zeros((128, WB), np.float32)
    res[:, :KC * NB] = out.reshape(128, KC * NB)
    return res


def _colvec(v):
    n = v.shape[0] // 128
    return np.ascontiguousarray(v.reshape(n, 128).T)


def _prep_shared(inp, SEQ):
    f = np.float32
    w_in = inp["ssd_w_in"][0]
    perm = (np.arange(32) + 16) % 32
    blocks0 = []
    for b in range(8):
        blocks0.append(_blk(w_in[:, 2048:6144], 8, b * 512, 512))
    for b in range(4):
        blocks0.append(_blk(w_in[:, 0:2048], 8, b * 512, 512))
    for b in range(4):
        blocks0.append(_blk(inp["ssd_w_out"][0], 16, b * 256, 256))

    def ffn_blocks(l):
        r = []
        for b in range(6):
            r.append(_blk(inp["ffn_w_gate"][l], 8, b * 512, 512))
        for b in range(6):
            r.append(_blk(inp["ffn_w_up"][l], 8, b * 512, 512))
        for b in range(8):
            r.append(_blk(inp["ffn_w_down"][l], 22, b * 128, 128))
        return r
    blocks0 += ffn_blocks(0)
    blocks1 = [_blk(inp["mla_wq_a"][0], 8, 0, 512)]
    wkv = inp["mla_wkv_a"][0]
    wkv2 = np.concatenate([wkv[:, :256], wkv[:, 256:288], wkv[:, 256:288][:, perm]], axis=1)
    blocks1.append(_blk(wkv2, 8, 0, 320))
    wqb = inp["mla_wq_b"][0].reshape(512, 16, 96)
    blocks1.append(_blk(np.ascontiguousarray(wqb[:, :, :64]).reshape(512, 1024), 4, 0, 1024))
    rope = wqb[:, :, 64:]
    blocks1.append(_blk(np.concatenate([rope.reshape(512, 512), rope[:, :, perm].reshape(512, 512)], axis=1), 4, 0, 1024))
    for b in range(2):
        blocks1.append(_blk(inp["mla_w_o"][0], 8, b * 512, 512))
    blocks1 += ffn_blocks(1)
    assert len(blocks0) == NB0 and len(blocks1) == NB1
    w0 = np.concatenate(blocks0, axis=0)
    w1 = np.concatenate(blocks1, axis=0)

    cols = np.zeros((128, NCOLS), f)
    lns = [inp["ln_mix_pre"][0], inp["ln_mix_pre"][1], inp["ln_mix_post"][0], inp["ln_mix_post"][1],
           inp["ln_ffn_pre"][0], inp["ln_ffn_pre"][1], inp["ln_ffn_post"][0], inp["ln_ffn_post"][1]]
    for i, v in enumerate(lns):
        cols[:, C_LN + 8 * i:C_LN + 8 * i + 8] = _colvec(v)
    cw = inp["ssd_conv_w"][0]
    cols[:, C_CW:C_CW + 128] = cw.T.reshape(32, 128, 4).transpose(1, 0, 2).reshape(128, 128)
    cols[:, C_CB:C_CB + 32] = _colvec(inp["ssd_conv_b"][0])
    cols[:, C_QN:C_QN + 4] = _colvec(inp["mla_q_norm"][0])
    cols[:, C_KN:C_KN + 2] = _colvec(inp["mla_kv_norm"][0])
    rows = np.zeros((128, NROWS), f)
    rows[:, R_DTB:R_DTB + 32] = inp["ssd_dt_bias"][0][None, :]
    rows[:, R_ALOG:R_ALOG + 32] = inp["ssd_a_log"][0][None, :]
    rows[:, R_D:R_D + 32] = inp["ssd_d"][0][None, :]
    rows[:, R_GN:R_GN + 2048] = inp["ssd_gate_norm"][0][None, :]
    wdt = np.ascontiguousarray(w_in[:, 6144:6176].reshape(8, 128, 32).transpose(1, 0, 2)).reshape(128, 256)
    wuk_src = inp["mla_w_uk"][0]
    wuk = wuk_src.transpose(1, 2, 0).reshape(8, 2, 64, 256).transpose(1, 2, 0, 3).reshape(128, 8 * 256)
    wuv_src = inp["mla_w_uv"][0]
    wuv = np.zeros((128, 2, 16, 128), f)
    for h in range(16):
        for j in range(2):
            wuv[:, j, h, (h % 2) * 64:(h % 2) * 64 + 64] = wuv_src[j * 128:(j + 1) * 128, h, :]
    wuv = wuv.reshape(128, 2 * 16 * 128)

    def rope_tab(pos):
        half = 16
        inv = (np.float32(10000.0) ** (-np.arange(half, dtype=np.float32) / np.float32(half))).astype(np.float32)
        ang = pos.astype(np.float32)[:, None] * inv[None, :]
        c, s_ = np.cos(ang).astype(np.float32), np.sin(ang).astype(np.float32)
        tab = np.zeros((32, 2, pos.shape[0]), np.float32)
        tab[:16, 0] = c.T
        tab[16:, 0] = c.T
        tab[:16, 1] = -s_.T
        tab[16:, 1] = s_.T
        return tab
    cs = rope_tab(np.arange(SEQ))
    c1 = rope_tab(PAST + np.arange(LS))
    css = np.concatenate([c1, c1], axis=2)
    consts = np.zeros((128, 512), f)
    i = np.arange(128)
    consts[:, 0:128] = (i[:, None] == i[None, :])
    consts[:, 128:256] = (i[:, None] <= i[None, :])
    consts[:, 256:384] = (i[:, None] > i[None, :])
    consts[:, 384:512] = 1.0
    return dict(w0=w0, w1=w1, cols=cols, rows=rows, wdt=wdt, wuk=np.ascontiguousarray(wuk), wuv=wuv, cs=cs, css=css, consts=consts)


_NT_OVERRIDE = [None]


def kernel(**inputs):
    inp = {k: np.asarray(v) for k, v in inputs.items()}
    B = inp["x_prompt"].shape[0]
    SEQ = inp["x_prompt"].shape[1]
    NT = SEQ // TT
    shared = _prep_shared(inp, SEQ)
    in_maps = []
    for b in range(B):
        m = dict(shared)
        m["xT"] = np.ascontiguousarray(inp["x_prompt"][b].T).reshape(8, 128, SEQ)
        xs = inp["x_sample"][2 * b:2 * b + 2].reshape(2 * LS, D_MODEL)
        m["xsT"] = np.ascontiguousarray(xs.T).reshape(8, 128, 2 * LS)
        cst = inp["state_ssd_conv"][0, 2 * b:2 * b + 2]
        m["convst"] = np.ascontiguousarray(cst.transpose(0, 2, 1).reshape(2, 32, 128, 3).transpose(0, 2, 1, 3)).reshape(2, 128, 96)
        sst = inp["state_ssd_ssm"][0, 2 * b:2 * b + 2]
        m["ssmst"] = np.ascontiguousarray(sst.transpose(0, 3, 1, 2)).reshape(2, 128, 2048)
        cl = inp["cache_mla_latent"][0, 2 * b:2 * b + 2]
        m["clat"] = np.ascontiguousarray(cl).reshape(2, 8, 128, 256)
        m["clatT"] = np.ascontiguousarray(cl.transpose(0, 2, 1)).reshape(2, 2, 128, PAST)
        ck = inp["cache_mla_krope"][0, 2 * b:2 * b + 2]
        m["ckrT"] = np.ascontiguousarray(ck.transpose(0, 2, 1))
        in_maps.append(m)
    nc = build_program(NT)
    res = run_bass_kernel_spmd(nc, in_maps, core_ids=list(range(B)))
    f = np.float32
    y_prompt = np.zeros((B, SEQ, D_MODEL), f)
    y_sample = np.zeros((2 * B, LS, D_MODEL), f)
    p_conv = np.zeros((1, B, 3, 4096), f)
    p_ssm = np.zeros((1, B, 32, 64, 128), f)
    p_lat = np.zeros((1, B, SEQ, 256), f)
    p_kr = np.zeros((1, B, SEQ, 32), f)
    s_conv = np.zeros((1, 2 * B, 3, 4096), f)
    s_ssm = np.zeros((1, 2 * B, 32, 64, 128), f)
    s_lat = np.zeros((1, 2 * B, LS, 256), f)
    s_kr = np.zeros((1, 2 * B, LS, 32), f)
    for b in range(B):
        r = res.results[b]
        y_prompt[b] = r["yT"].reshape(1024, SEQ).T
        ys = r["ysT"].reshape(1024, 2 * LS).T
        p_conv[0, b] = r["pconv"].reshape(128, 32, 3).transpose(2, 1, 0).reshape(3, 4096)
        p_ssm[0, b] = r["pssm"].T.reshape(32, 64, 128)
        p_lat[0, b] = r["platT"].reshape(256, SEQ).T
        p_kr[0, b] = r["pkrT"].T
        sl = r["slatT"].reshape(256, 2 * LS).T
        sk = r["skrT"].T
        for si in range(2):
            y_sample[2 * b + si] = ys[si * LS:(si + 1) * LS]
            s_conv[0, 2 * b + si] = r["sconv"][si].reshape(128, 32, 3).transpose(2, 1, 0).reshape(3, 4096)
            s_ssm[0, 2 * b + si] = r["sssm"][si].T.reshape(32, 64, 128)
            s_lat[0, 2 * b + si] = sl[si * LS:(si + 1) * LS]
            s_kr[0, 2 * b + si] = sk[si * LS:(si + 1) * LS]
    return (y_prompt, y_sample, p_conv, p_ssm, p_lat, p_kr, s_conv, s_ssm, s_lat, s_kr)
```

```python
import math
import numpy as np
import concourse.bass as bass
import concourse.mybir as mybir
from concourse.bass_utils import run_bass_kernel_spmd

F32 = mybir.dt.float32
BF16 = mybir.dt.bfloat16
AF = mybir.ActivationFunctionType
ALU = mybir.AluOpType
AX = mybir.AxisListType

D_MODEL = 1024
SEQ_FULL = 8192
PAST = 1024
LS = 32
TT = 256
FFN = 2816
EPS = 1e-6
SCALE = 1.0 / math.sqrt(96.0)
NSLOT = 3
WB = 4096

L0_BLOCKS = ([("xbc", 8, 512)] * 8 + [("z", 8, 512)] * 4 + [("wout", 16, 256)] * 4
             + [("gate", 8, 512)] * 6 + [("up", 8, 512)] * 6 + [("down", 22, 128)] * 8)
L1_BLOCKS = ([("wqa", 8, 512)] + [("wkva", 8, 320)] + [("wqb", 4, 1024)] * 2 + [("wo", 8, 512)] * 2
             + [("gate", 8, 512)] * 6 + [("up", 8, 512)] * 6 + [("down", 22, 128)] * 8)
NB0, NB1 = len(L0_BLOCKS), len(L1_BLOCKS)

C_LN = 0
C_CW = 64
C_CB = 192
C_QN = 224
C_KN = 228
NCOLS = 232
R_DTB, R_ALOG, R_D, R_GN = 0, 32, 64, 96
NROWS = 96 + 2048


class Sched:
    ENG = ("pe", "act", "dve", "pool", "sp")

    def __init__(self, nc, ndma=14):
        self.nc = nc
        self.ops = {e: [] for e in self.ENG}
        self.count = {e: 0 for e in self.ENG}
        self.waited = {e: {} for e in self.ENG}
        self.res = {}
        self.ndma = ndma
        self.epoch = 0
        self.nepoch = 12
        self.dma_val = {}
        self.dma_rr = {"sp": 0, "pool": 0, "act": 0}
        for q in ("sp", "pool"):
            for k in range(ndma):
                self.dma_val["d_%s_%d" % (q, k)] = 0

    def sem_names(self):
        return ["%s@%d" % (e, k) for e in self.ENG[:4] for k in range(self.nepoch)] + list(self.dma_val.keys())

    def cname(self, eng):
        return "%s@%d" % (eng, self.epoch)

    def new_epoch(self):
        self.barrier()
        self.epoch += 1
        assert self.epoch < self.nepoch
        for e in self.ENG:
            self.count[e] = 0
            self.waited[e] = {}

    def emit(self, eng, fn, reads=(), writes=(), dma=False):
        deps = []
        for r in reads:
            st = self.res.get(r)
            if st is not None and st[0] is not None:
                deps.append(st[0])
        for w in writes:
            st = self.res.get(w)
            if st is not None:
                if st[0] is not None:
                    deps.append(st[0])
                deps.extend(st[1])
        if dma:
            k = self.dma_rr[eng]
            self.dma_rr[eng] = (k + 1) % self.ndma
            sname = "d_%s_%d" % (eng, k)
            if self.dma_val[sname] > 0:
                deps.append((sname, self.dma_val[sname]))
            self.dma_val[sname] += 16
            token = (sname, self.dma_val[sname])
        else:
            self.count[eng] += 1
            token = (self.cname(eng), self.count[eng])
        waits = {}
        for (s, v) in deps:
            if s == self.cname(eng) and eng in ("pe", "sp"):
                continue
            if self.waited[eng].get(s, 0) >= v:
                continue
            if waits.get(s, 0) < v:
                waits[s] = v
        for s, v in waits.items():
            self.waited[eng][s] = v
        self.ops[eng].append((list(waits.items()), fn, token, dma))
        for w in writes:
            self.res[w] = [token, []]
        for r in reads:
            st = self.res.get(r)
            if st is None:
                self.res[r] = [None, [token]]
            else:
                st[1].append(token)
        return token

    def barrier(self):
        toks = [(self.cname(e), self.count[e]) for e in self.ENG if self.count[e] > 0]
        toks += [(s, v) for s, v in self.dma_val.items() if v > 0]
        for e in self.ENG:
            waits = []
            for (s, v) in toks:
                if s == self.cname(e):
                    continue
                if self.waited[e].get(s, 0) >= v:
                    continue
                self.waited[e][s] = v
                waits.append((s, v))
            if waits:
                self.ops[e].append((waits, None, None, False))
        self.res = {}

    def check(self, semvals):
        pos = {e: 0 for e in self.ENG}
        progress = True
        while progress:
            progress = False
            for e in self.ENG:
                while pos[e] < len(self.ops[e]):
                    waits, fn, token, dma = self.ops[e][pos[e]]
                    if any(semvals.get(s, 0) < v for s, v in waits):
                        break
                    if fn is not None:
                        semvals[token[0]] = semvals.get(token[0], 0) + (16 if dma else 1)
                        assert semvals[token[0]] == token[1], (e, pos[e], token, semvals[token[0]])
                    pos[e] += 1
                    progress = True
        stuck = {e: (pos[e], len(self.ops[e])) for e in self.ENG if pos[e] < len(self.ops[e])}
        for e in stuck:
            waits, fn, token, dma = self.ops[e][pos[e]]
            print("STUCK", e, pos[e], [(s, v, semvals.get(s, 0)) for s, v in waits if semvals.get(s, 0) < v], token)
        return not stuck

    def replay(self, eng, E, sems):
        for waits, fn, token, dma in self.ops[eng]:
            for s, v in waits:
                E.wait_ge(sems[s], v)
            if fn is None:
                continue
            ins = fn(E)
            ins.then_inc(sems[token[0]], 16 if dma else 1)


def build_program(NT):
    SEQ = NT * TT
    nc = bass.Bass("TRN2", target_bir_lowering=False)
    S = Sched(nc)

    def din(name, shape, dt=F32):
        return nc.dram_tensor(name, list(shape), dt, kind="ExternalInput").ap()

    def dout(name, shape, dt=F32):
        return nc.dram_tensor(name, list(shape), dt, kind="ExternalOutput").ap()

    xT = din("xT", [8, 128, SEQ])
    xsT = din("xsT", [8, 128, 2 * LS])
    w0 = din("w0", [NB0 * 128, WB])
    w1 = din("w1", [NB1 * 128, WB])
    cols_d = din("cols", [128, NCOLS])
    rows_d = din("rows", [128, NROWS])
    wdt_d = din("wdt", [128, 8 * 32])
    wuk_d = din("wuk", [128, 8 * 256])
    wuv_d = din("wuv", [128, 2 * 16 * 128])
    cs_d = din("cs", [32, 2, SEQ])
    css_d = din("css", [32, 2, 2 * LS])
    consts_d = din("consts", [128, 512])
    convst_d = din("convst", [2, 128, 96])
    ssmst_d = din("ssmst", [2, 128, 2048])
    clat_d = din("clat", [2, 8, 128, 256])
    clatT_d = din("clatT", [2, 2, 128, PAST])
    ckrT_d = din("ckrT", [2, 32, PAST])

    yT_o = dout("yT", [8, 128, SEQ])
    ysT_o = dout("ysT", [8, 128, 2 * LS])
    pconv_o = dout("pconv", [128, 96])
    pssm_o = dout("pssm", [128, 2048])
    platT_o = dout("platT", [2, 128, SEQ])
    pkrT_o = dout("pkrT", [32, SEQ])
    sconv_o = dout("sconv", [2, 128, 96])
    sssm_o = dout("sssm", [2, 128, 2048])
    slatT_o = dout("slatT", [2, 128, 2 * LS])
    skrT_o = dout("skrT", [32, 2 * LS])

    wb0 = nc.dram_tensor("wb0", [NB0 * 128, WB], BF16).ap()
    wb1 = nc.dram_tensor("wb1", [NB1 * 128, WB], BF16).ap()
    hs = nc.dram_tensor("hs", [NT + 1, 128, 8 * TT], F32).ap()

    from contextlib import ExitStack
    es = ExitStack()

    def sb(name, shape, dt=F32):
        return es.enter_context(nc.sbuf_tensor("sb_" + name, list(shape), dt))

    ps = [es.enter_context(nc.psum_tensor("ps%d" % i, [128, 512], F32)) for i in range(8)]
    psb = [p.bitcast(BF16) for p in ps]
    sems = {}
    for n in S.sem_names():
        sems[n] = es.enter_context(nc.semaphore("s_" + n))

    def PK(i):
        return ("ps", i)

    semvals = {}

    def replay_all():
        import os
        if os.environ.get("KCHECK"):
            print("deadlock check ok:", S.check(semvals))
        with nc.Block() as block:
            @block.tensor
            def _(E):
                S.replay("pe", E, sems)

            @block.scalar
            def _(E):
                S.replay("act", E, sems)

            @block.vector
            def _(E):
                S.replay("dve", E, sems)

            @block.gpsimd
            def _(E):
                S.replay("pool", E, sems)

            @block.sync
            def _(E):
                S.replay("sp", E, sems)
        for e in S.ENG:
            S.ops[e] = []

    consts = sb("consts", [128, 512])
    identb = sb("identb", [128, 128], BF16)
    onesb = sb("onesb", [128, 128], BF16)
    cols = sb("cols", [128, NCOLS])
    epsc = sb("epsc", [128, 1])
    wring = [sb("wring%d" % i, [128, WB], BF16) for i in range(NSLOT)]
    hT = sb("hT", [128, 8, TT])
    hT2 = sb("hT2", [128, 8, TT])
    aT = sb("aT", [128, 8, TT], BF16)
    mT = sb("mT", [128, 8, TT])
    sq = sb("sq", [128, 8, TT], BF16)
    rstd = sb("rstd", [128, TT])
    triu = consts[:, 128:256]
    lstr = consts[:, 256:384]
    onesf = consts[:, 384:512]

    S.emit("pool", lambda E: E.dma_start(out=consts[:], in_=consts_d[:, :]), writes=["consts"], dma=True)
    S.emit("pool", lambda E: E.dma_start(out=identb[:], in_=consts_d[:, 0:128]), writes=["identb"], dma=True)
    S.emit("pool", lambda E: E.dma_start(out=onesb[:], in_=consts_d[:, 384:512]), writes=["onesb"], dma=True)
    S.emit("pool", lambda E: E.dma_start(out=cols[:], in_=cols_d[:, :]), writes=["cols"], dma=True)
    S.emit("pool", lambda E: E.memset(epsc[:], EPS), writes=["epsc"])
    for (src, dst, nb) in ((w0, wb0, NB0), (w1, wb1, NB1)):
        for b0 in range(0, nb, 2):
            b1 = min(nb, b0 + 2)
            S.emit("pool", lambda E, src=src, dst=dst, b0=b0, b1=b1: E.dma_start(
                out=dst[b0 * 128:b1 * 128, :], in_=src[b0 * 128:b1 * 128, :]),
                writes=[("wb", id(dst) % 1000, b) for b in range(b0, b1)], dma=True)

    class WStream:
        def __init__(self):
            self.seq = []
            self.issued = 0
            self.used = 0

        def add(self, dram, tag, nblocks, reps):
            for _ in range(reps):
                for b in range(nblocks):
                    self.seq.append((dram, b, ("wb", tag, b)))

        def issue(self):
            if self.issued >= len(self.seq):
                return
            dram, b, key = self.seq[self.issued]
            slot = self.issued % NSLOT
            S.emit("sp", lambda E, dram=dram, b=b, slot=slot: E.dma_start(
                out=wring[slot][:], in_=dram[b * 128:(b + 1) * 128, :]),
                reads=[key], writes=[("ws", slot)], dma=True)
            self.issued += 1

        def start(self):
            for _ in range(NSLOT):
                self.issue()

        def next(self, name, KC, NB):
            i = self.used
            slot = i % NSLOT
            self.used += 1
            view = wring[slot][:, 0:KC * NB].rearrange("p (k n) -> p k n", k=KC)
            return view, ("ws", slot)

        def done(self):
            self.issue()

    WS = WStream()
    WS.add(wb0, id(wb0) % 1000, NB0, NT + 1)
    WS.add(wb1, id(wb1) % 1000, NB1, NT + 1)

    dense_rr = [0]

    def dbank():
        dense_rr[0] ^= 1
        return dense_rr[0]

    def ln_col(idx):
        return cols[:, C_LN + idx * 8: C_LN + idx * 8 + 8]

    HK = [("hT", k) for k in range(8)]
    HK2 = [("hT2", k) for k in range(8)]
    HB = [(hT, HK), (hT2, HK2)]
    H = [hT, HK]

    def use_buf(n):
        H[0], H[1] = HB[n % 2]
    MK = [("mT", k) for k in range(8)]

    def compute_rstd(src, srckeys, KC, D, T):
        S.emit("act", lambda E: E.activation(out=sq[:, 0:KC, 0:T], in_=src, func=AF.Square),
               reads=list(srckeys), writes=["sq"])

        def f(E):
            for kc in range(KC):
                ins = E.matmul(ps[2][:, 0:T], lhsT=onesb[:, :], rhs=sq[:, kc, 0:T], start=(kc == 0), stop=(kc == KC - 1))
            return ins
        S.emit("pe", f, reads=["sq", "onesb"], writes=[PK(2)])
        S.emit("act", lambda E: E.activation(out=rstd[:, 0:T], in_=ps[2][:, 0:T], func=AF.Sqrt, bias=epsc[:], scale=1.0 / D),
               reads=[PK(2), "epsc"], writes=["rstd"])
        S.emit("dve", lambda E: E.reciprocal(out=rstd[:, 0:T], in_=rstd[:, 0:T]), reads=["rstd"], writes=["rstd"])

    def norm_to(src3, srckeys, KC, D, T, gcols, dst3, dstkey):
        compute_rstd(src3[:, 0:KC, 0:T], srckeys, KC, D, T)
        for kc in range(KC):
            S.emit("dve", lambda E, kc=kc: E.scalar_tensor_tensor(
                out=dst3[:, kc, 0:T], in0=src3[:, kc, 0:T], scalar=gcols[:, kc:kc + 1], in1=rstd[:, 0:T],
                op0=ALU.mult, op1=ALU.mult), reads=list(srckeys) + ["rstd", "cols"], writes=[dstkey])

    def postnorm_residual(T, gcols):
        compute_rstd(mT[:, :, 0:T], MK, 8, D_MODEL, T)
        for kc in range(8):
            S.emit("dve", lambda E, kc=kc: E.scalar_tensor_tensor(
                out=mT[:, kc, 0:T], in0=mT[:, kc, 0:T], scalar=gcols[:, kc:kc + 1], in1=rstd[:, 0:T],
                op0=ALU.mult, op1=ALU.mult), reads=[("mT", kc), "rstd", "cols"], writes=[("mT", kc)])
            S.emit("pool", lambda E, kc=kc, hT_=H[0]: E.tensor_tensor(
                out=hT_[:, kc, 0:T], in0=hT_[:, kc, 0:T], in1=mT[:, kc, 0:T], op=ALU.add),
                reads=[("mT", kc), H[1][kc]], writes=[H[1][kc]])

    def dense_ws(name, KC, NB, nvalid, T, rhs_fn, rhskeys, evac):
        view, wkey = WS.next(name, KC, NB)
        for oc in range(nvalid // 128):
            bank = dbank()

            def f(E, oc=oc, bank=bank):
                for kc in range(KC):
                    ins = E.matmul(ps[bank][:, 0:T], lhsT=view[:, kc, oc * 128:(oc + 1) * 128], rhs=rhs_fn(kc),
                                   start=(kc == 0), stop=(kc == KC - 1))
                return ins
            S.emit("pe", f, reads=[wkey] + rhskeys, writes=[PK(bank)])
            evac(oc, bank)
        WS.done()

    def ffn(layer, T, hidT, mid_fn=None):
        norm_to(H[0], H[1], 8, D_MODEL, T, ln_col(4 + layer), aT, "aT")
        for b in range(6):
            nv = 512 if b < 5 else 256

            def ev(oc, bank, b=b):
                j = b * 4 + oc
                S.emit("act", lambda E: E.activation(out=hidT[:, j, 0:T], in_=ps[bank][:, 0:T], func=AF.Silu),
                       reads=[PK(bank)], writes=[("hid", j)])
            dense_ws("gate", 8, 512, nv, T, lambda kc: aT[:, kc, 0:T], ["aT"], ev)
        for b in range(6):
            nv = 512 if b < 5 else 256

            def ev(oc, bank, b=b):
                j = b * 4 + oc
                S.emit("dve", lambda E: E.tensor_tensor(out=hidT[:, j, 0:T], in0=ps[bank][:, 0:T], in1=hidT[:, j, 0:T], op=ALU.mult),
                       reads=[PK(bank), ("hid", j)], writes=[("hid", j)])
            dense_ws("up", 8, 512, nv, T, lambda kc: aT[:, kc, 0:T], ["aT"], ev)
        if mid_fn is not None:
            mid_fn()
        for b in range(8):
            def ev(oc, bank, b=b):
                S.emit("act", lambda E: E.copy(out=mT[:, b, 0:T], in_=ps[bank][:, 0:T]), reads=[PK(bank)], writes=[("mT", b)])
            dense_ws("down", 22, 128, 128, T, lambda kc: hidT[:, kc, 0:T], [("hid", j) for j in range(22)], ev)
        postnorm_residual(T, ln_col(6 + layer))

    es1 = ExitStack()

    def sb1(name, shape, dt=F32):
        return es1.enter_context(nc.sbuf_tensor("s1_" + name, list(shape), dt))

    rows = sb1("rows", [128, NROWS])
    abc = sb1("abc", [128, 32])
    wdt = sb1("wdt", [128, 8, 32], BF16)
    Sst = [sb1("Sst%d" % i, [128, 2048]) for i in range(2)]
    Sbf = [sb1("Sbf%d" % i, [128, 2048], BF16) for i in range(2)]
    tail = [sb1("tail%d" % i, [128, 32, 3]) for i in range(2)]
    xbcT = sb1("xbcT", [128, 32, TT], BF16)
    pre = [sb1("pre%d" % i, [128, 2 * (LS + 3) if False else TT + 8]) for i in range(2)]
    cacc = [sb1("cacc%d" % i, [128, TT]) for i in range(2)]
    cq2 = [sb1("cq2%d" % i, [128, TT]) for i in range(2)]
    cq3 = [sb1("cq3%d" % i, [128, TT]) for i in range(2)]
    sz = sb1("sz", [128, 2, 2048], BF16)
    dtb = sb1("dtb", [128, 2, 32])
    dtA = sb1("dtA", [128, 2, 32])
    x_tok = [sb1("x_tok%d" % i, [128, 2048], BF16) for i in range(2)]
    B_tok = [sb1("B_tok%d" % i, [128, 1024], BF16) for i in range(2)]
    Rm = [sb1("Rm%d" % i, [128, 512]) for i in range(2)]
    decay = [sb1("decay%d" % i, [128, 512], BF16) for i in range(2)]
    MT = [sb1("MT%d" % i, [128, 512], BF16) for i in range(2)]
    cbm = [sb1("cbm%d" % i, [128, 128], BF16) for i in range(2)]
    xdt = [sb1("xdt%d" % i, [128, 256], BF16) for i in range(2)]
    xw = [sb1("xw%d" % i, [128, 256], BF16) for i in range(2)]
    xD = [sb1("xD%d" % i, [128, 256], BF16) for i in range(2)]
    tmp = [sb1("tmp%d" % i, [128, 256]) for i in range(2)]
    yall = [sb1("yall%d" % i, [128, 2048]) for i in range(2)]
    ysq = sb1("ysq", [128, 2048], BF16)
    yn = sb1("yn", [128, 2048], BF16)
    ynT = sb1("ynT", [128, 16, TT], BF16)
    eacum = [sb1("eacum%d" % i, [128, 32]) for i in range(2)]
    Elast = [sb1("Elast%d" % i, [128, 32]) for i in range(2)]
    ss8 = sb1("ss8", [128, 8])
    hidT1 = sb1("hidT1", [128, 22, TT], BF16)

    S.emit("pool", lambda E: E.dma_start(out=rows[:], in_=rows_d[:, :]), writes=["rows"], dma=True)
    S.emit("pool", lambda E: E.dma_start(out=wdt[:].rearrange("p k n -> p (k n)"), in_=wdt_d[:, :]), writes=["wdt"], dma=True)
    S.emit("act", lambda E: E.activation(out=abc[:], in_=rows[:, R_ALOG:R_ALOG + 32], func=AF.Exp), reads=["rows"], writes=["abc"])
    S.emit("dve", lambda E: E.tensor_scalar(out=abc[:], in0=abc[:], scalar1=-1.0, scalar2=None, op0=ALU.mult), reads=["abc"], writes=["abc"])
    WS.start()

    def SKs(si):
        return [(("S", si), g) for g in range(8)]

    def SBKs(si):
        return [(("Sbf", si), g) for g in range(8)]

    def ssd_chunk_gen(ci, c0, L, si):
        tok = slice(c0, c0 + L)
        cp = ci % 2
        SK, SBK = ("S", si), ("Sbf", si)
        XT, BT, EA, EL, YA = x_tok[cp], B_tok[cp], eacum[cp], Elast[cp], yall[cp]
        kXT, kBT, kEA, kEL = ("x_tok", cp), ("B_tok", cp), ("eacum", cp), ("Elast", cp)
        YK = [("yall", cp, g) for g in range(8)]
        for half in range(2):
            bank = dbank()

            def f(E, half=half, bank=bank):
                for k in range(8):
                    ins = E.transpose(psb[bank][0:L, k * 128:(k + 1) * 128], xbcT[:, half * 8 + k, tok], identb[:, :])
                return ins
            S.emit("pe", f, reads=["xbcT", "identb"], writes=[PK(bank)])
            S.emit("act", lambda E, half=half, bank=bank: E.copy(out=XT[0:L, half * 1024:(half + 1) * 1024], in_=psb[bank][0:L, 0:1024]),
                   reads=[PK(bank)], writes=[kXT])
        bank = dbank()

        def f(E, bank=bank):
            for k in range(8):
                ins = E.transpose(psb[bank][0:L, k * 128:(k + 1) * 128], xbcT[:, 16 + k, tok], identb[:, :])
            return ins
        S.emit("pe", f, reads=["xbcT", "identb"], writes=[PK(bank)])
        S.emit("act", lambda E, bank=bank: E.copy(out=BT[0:L, :], in_=psb[bank][0:L, 0:1024]), reads=[PK(bank)], writes=[kBT])

        def f(E):
            E.matmul(ps[2][0:L, 0:32], lhsT=triu[0:L, 0:L], rhs=dtA[0:L, ci, :], start=True, stop=True)
            return E.matmul(ps[2][:, 32:64], lhsT=onesf[0:L, :], rhs=dtA[0:L, ci, :], start=True, stop=True)
        S.emit("pe", f, reads=["dtA", "consts"], writes=[PK(2)])
        S.emit("act", lambda E: E.activation(out=EA[0:L, :], in_=ps[2][0:L, 0:32], func=AF.Exp), reads=[PK(2)], writes=[kEA])
        S.emit("act", lambda E: E.activation(out=EL[:, :], in_=ps[2][:, 32:64], func=AF.Exp), reads=[PK(2)], writes=[kEL])
        yield "pro"

        def unit_gen(g):
            u = g % 2
            hs4 = slice(4 * g, 4 * g + 4)
            gcols = slice(256 * g, 256 * g + 256)
            W4 = 4 * L
            Rm_, dec_, MT_, cbm_, xdt_, xw_, xD_, tmp_ = Rm[u], decay[u], MT[u], cbm[u], xdt[u], xw[u], xD[u], tmp[u]
            kR, kD, kM, kC, kX, kW, kXD, kT = ("Rm", u), ("decay", u), ("MT", u), ("cbm", u), ("xdt", u), ("xw", u), ("xD", u), ("tmp", u)
            segb = 3 + u
            cbv = ps[5][0:L, 0:L] if u == 0 else ps[2][0:L, 256:256 + L]
            dbk = dbank()
            dsv, kds_ = ps[dbk][:, 0:256], PK(dbk)
            yv = ps[6][0:L, u * 256:(u + 1) * 256]
            ysv = ps[7][0:L, u * 256:(u + 1) * 256]
            kcb, kds, ky, kys = (PK(5) if u == 0 else PK(2)), kds_, ("ps6", u), ("ps7", u)
            xv = XT[0:L, gcols].rearrange("p (h d) -> p h d", h=4)
            S.emit("pool", lambda E: E.tensor_tensor(
                out=Rm_[0:L, 0:W4].rearrange("p (h q) -> p h q", h=4),
                in0=dtA[0:L, ci, hs4].unsqueeze(2).to_broadcast([L, 4, L]),
                in1=triu[0:L, 0:L].unsqueeze(1).to_broadcast([L, 4, L]), op=ALU.mult),
                reads=["dtA", "consts"], writes=[kR])
            S.emit("pe", lambda E: E.matmul(ps[segb][0:L, 0:W4], lhsT=lstr[0:L, 0:L], rhs=Rm_[0:L, 0:W4], start=True, stop=True),
                   reads=[kR, "consts"], writes=[PK(segb)])
            S.emit("act", lambda E: E.activation(out=dec_[0:L, 0:W4], in_=ps[segb][0:L, 0:W4], func=AF.Exp),
                   reads=[PK(segb)], writes=[kD])
            S.emit("pe", lambda E: E.matmul(cbv, lhsT=xbcT[:, 16 + g, tok], rhs=xbcT[:, 24 + g, tok], start=True, stop=True),
                   reads=["xbcT"], writes=[kcb])
            S.emit("dve", lambda E: E.tensor_tensor(out=cbm_[0:L, 0:L], in0=cbv, in1=triu[0:L, 0:L], op=ALU.mult),
                   reads=[kcb, "consts"], writes=[kC])
            S.emit("pool", lambda E: E.tensor_tensor(
                out=xdt_[0:L, :].rearrange("p (h d) -> p h d", h=4), in0=xv,
                in1=dtb[0:L, ci, hs4].unsqueeze(2).to_broadcast([L, 4, 64]), op=ALU.mult),
                reads=[kXT, "dtb"], writes=[kX])
            S.emit("pool", lambda E: E.tensor_tensor(
                out=xD_[0:L, :].rearrange("p (h d) -> p h d", h=4), in0=xv,
                in1=rows[0:L, R_D + 4 * g:R_D + 4 * g + 4].unsqueeze(2).to_broadcast([L, 4, 64]), op=ALU.mult),
                reads=[kXT, "rows"], writes=[kXD])
            yield "A"
            S.emit("dve", lambda E: E.tensor_tensor(
                out=MT_[0:L, 0:W4].rearrange("p (r q) -> p r q", r=4),
                in0=dec_[0:L, 0:W4].rearrange("p (r q) -> p r q", r=4),
                in1=cbm_[0:L, 0:L].unsqueeze(1).to_broadcast([L, 4, L]), op=ALU.mult),
                reads=[kD, kC], writes=[kM])
            S.emit("dve", lambda E: E.tensor_tensor(
                out=xw_[0:L, :].rearrange("p (h d) -> p h d", h=4), in0=xdt_[0:L, :].rearrange("p (h d) -> p h d", h=4),
                in1=dec_[0:L, 0:W4].rearrange("p (h q) -> p h q", h=4)[:, :, L - 1:L].to_broadcast([L, 4, 64]), op=ALU.mult),
                reads=[kX, kD], writes=[kW])

            def f(E):
                E.matmul(yv, lhsT=identb[0:L, 0:L], rhs=xD_[0:L, :], start=True, stop=False)
                for hh in range(4):
                    ins = E.matmul(yv[:, hh * 64:(hh + 1) * 64], lhsT=MT_[0:L, hh * L:(hh + 1) * L], rhs=xdt_[0:L, hh * 64:(hh + 1) * 64],
                                   start=False, stop=True, skip_group_check=True)
                return ins
            S.emit("pe", f, reads=[kM, kX, kXD, "identb"], writes=[ky])
            S.emit("pe", lambda E: E.matmul(ysv, lhsT=xbcT[:, 24 + g, tok], rhs=Sbf[si][:, g * 256:(g + 1) * 256], start=True, stop=True),
                   reads=["xbcT", (SBK, g)], writes=[kys])
            S.emit("pe", lambda E: E.matmul(dsv, lhsT=BT[0:L, g * 128:(g + 1) * 128], rhs=xw_[0:L, :], start=True, stop=True),
                   reads=[kBT, kW], writes=[kds])
            yield "B"
            S.emit("dve", lambda E: E.tensor_tensor(
                out=tmp_[0:L, :].rearrange("p (h d) -> p h d", h=4), in0=ysv.rearrange("p (h d) -> p h d", h=4),
                in1=EA[0:L, hs4].unsqueeze(2).to_broadcast([L, 4, 64]), op=ALU.mult),
                reads=[kys, kEA], writes=[kT])
            S.emit("dve", lambda E: E.tensor_tensor(out=tmp_[0:L, :], in0=yv, in1=tmp_[0:L, :], op=ALU.add),
                   reads=[ky, kT], writes=[kT])
            S.emit("dve", lambda E: E.tensor_tensor(out=YA[0:L, gcols], in0=tmp_[0:L, :], in1=sz[0:L, ci, gcols], op=ALU.mult),
                   reads=[kT, "sz"], writes=[("yall", cp, g)])
            S.emit("dve", lambda E: E.tensor_tensor(
                out=Sst[si][:, gcols].rearrange("p (h d) -> p h d", h=4), in0=Sst[si][:, gcols].rearrange("p (h d) -> p h d", h=4),
                in1=EL[:, hs4].unsqueeze(2).to_broadcast([128, 4, 64]), op=ALU.mult),
                reads=[(SK, g), kEL], writes=[(SK, g)])
            S.emit("dve", lambda E: E.tensor_tensor(out=Sst[si][:, gcols], in0=dsv, in1=Sst[si][:, gcols], op=ALU.add),
                   reads=[kds, (SK, g)], writes=[(SK, g)])
            S.emit("act", lambda E: E.copy(out=Sbf[si][:, gcols], in_=Sst[si][:, gcols]), reads=[(SK, g)], writes=[(SBK, g)])

        gens = [unit_gen(g) for g in range(8)]
        next(gens[0])
        for g in range(8):
            if g + 1 < 8:
                next(gens[g + 1])
            next(gens[g])
            next(gens[g], None)
            if g == 3:
                yield "mid"
        yield "units"
        S.emit("act", lambda E: E.activation(out=ysq[0:L, :], in_=YA[0:L, :], func=AF.Square), reads=YK, writes=["ysq"])
        S.emit("dve", lambda E: E.tensor_reduce(out=ss8[0:L, :], in_=ysq[0:L, :].rearrange("p (g c) -> p g c", g=8), axis=AX.X, op=ALU.add),
               reads=["ysq"], writes=["ss8"])
        S.emit("act", lambda E: E.activation(out=ss8[0:L, :], in_=ss8[0:L, :], func=AF.Sqrt, bias=epsc[0:L, :], scale=1.0 / 256.0),
               reads=["ss8", "epsc"], writes=["ss8"])
        S.emit("dve", lambda E: E.reciprocal(out=ss8[0:L, :], in_=ss8[0:L, :]), reads=["ss8"], writes=["ss8"])
        S.emit("dve", lambda E: E.tensor_tensor(
            out=YA[0:L, :].rearrange("p (g c) -> p g c", g=8), in0=YA[0:L, :].rearrange("p (g c) -> p g c", g=8),
            in1=ss8[0:L, :].unsqueeze(2).to_broadcast([L, 8, 256]), op=ALU.mult), reads=YK + ["ss8"], writes=YK)
        S.emit("pool", lambda E: E.tensor_tensor(out=yn[0:L, :], in0=YA[0:L, :], in1=rows[0:L, R_GN:R_GN + 2048], op=ALU.mult),
               reads=YK + ["rows"], writes=["yn"])
        for half in range(2):
            bank = dbank()

            def f(E, half=half, bank=bank):
                for k in range(8):
                    c = half * 8 + k
                    ins = E.transpose(psb[bank][:, k * L:(k + 1) * L], yn[0:L, c * 128:(c + 1) * 128], identb[0:L, 0:L])
                return ins
            S.emit("pe", f, reads=["yn", "identb"], writes=[PK(bank)])
            S.emit("act", lambda E, half=half, bank=bank: E.copy(
                out=ynT[:, half * 8:(half + 1) * 8, tok], in_=psb[bank][:, 0:8 * L].rearrange("p (k q) -> p k q", k=8)),
                reads=[PK(bank)], writes=["ynT"])
        yield "epi"

    def ssd_tile(chunk_list):
        gens = [ssd_chunk_gen(*c) for c in chunk_list]
        for gch in gens:
            next(gch)
        prev = None
        for gch in gens:
            next(gch)
            if prev is not None:
                next(prev)
            next(gch)
            prev = gch
        next(prev)

    def pass1_tile(T, segs, load_fn, hs_idx, prenormed=False, next_prenorm=None):
        load_fn()
        if not prenormed:
            norm_to(H[0], H[1], 8, D_MODEL, T, ln_col(0), aT, "aT")
        ci = 0
        chunk_list = []
        for (s0, Ls, si, chunks) in segs:
            for (c0, L) in chunks:
                chunk_list.append((ci, c0, L, si))

                def f(E, c0=c0, L=L):
                    for kc in range(8):
                        ins = E.matmul(ps[2][0:L, 0:32], lhsT=aT[:, kc, c0:c0 + L], rhs=wdt[:, kc, :], start=(kc == 0), stop=(kc == 7))
                    return ins
                S.emit("pe", f, reads=["aT", "wdt"], writes=[PK(2)])
                S.emit("dve", lambda E, L=L, ci=ci: E.tensor_tensor(out=dtb[0:L, ci, :], in0=ps[2][0:L, 0:32], in1=rows[0:L, R_DTB:R_DTB + 32], op=ALU.add),
                       reads=[PK(2), "rows"], writes=["dtb"])
                S.emit("act", lambda E, L=L, ci=ci: E.activation(out=dtb[0:L, ci, :], in_=dtb[0:L, ci, :], func=AF.Exp), reads=["dtb"], writes=["dtb"])
                S.emit("act", lambda E, L=L, ci=ci: E.activation(out=dtb[0:L, ci, :], in_=dtb[0:L, ci, :], func=AF.Ln, bias=1.0, scale=1.0),
                       reads=["dtb"], writes=["dtb"])
                S.emit("dve", lambda E, L=L, ci=ci: E.tensor_tensor(out=dtA[0:L, ci, :], in0=dtb[0:L, ci, :], in1=abc[0:L, :], op=ALU.mult),
                       reads=["dtb", "abc"], writes=["dtA"])
                ci += 1
        pend_silu = []
        for b in range(8):
            def ev(oc, bank, b=b):
                ch = b * 4 + oc
                pb = ch % 2
                P = pre[pb]
                pk, ak = ("pre", pb), ("cacc", pb)
                off = 0
                for (s0, Ls, si, chunks) in segs:
                    base = s0 + 3 * (segs.index((s0, Ls, si, chunks)))
                    S.emit("pool", lambda E, base=base, si=si: E.tensor_copy(out=P[:, base:base + 3], in_=tail[si][:, ch, :]),
                           reads=[("tail", si)], writes=[pk])
                    S.emit("act", lambda E, base=base, s0=s0, Ls=Ls: E.copy(out=P[:, base + 3:base + 3 + Ls], in_=ps[bank][:, s0:s0 + Ls]),
                           reads=[PK(bank)], writes=[pk])
                    S.emit("pool", lambda E, base=base, si=si, Ls=Ls: E.tensor_copy(out=tail[si][:, ch, :], in_=P[:, base + Ls:base + Ls + 3]),
                           reads=[pk], writes=[("tail", si)])
                    cw = cols[:, C_CW + ch * 4:C_CW + ch * 4 + 4]
                    A = cacc[pb]
                    Q2, Q3 = cq2[pb], cq3[pb]
                    qk2, qk3 = ("cq2", pb), ("cq3", pb)
                    S.emit("act", lambda E, s0=s0, Ls=Ls, cw=cw, Q3=Q3: E.activation(
                        out=Q3[:, s0:s0 + Ls], in_=ps[bank][:, s0:s0 + Ls], func=AF.Copy, scale=cw[:, 3:4]),
                        reads=[PK(bank), "cols"], writes=[qk3])
                    S.emit("act", lambda E, base=base, s0=s0, Ls=Ls, cw=cw, Q2=Q2: E.activation(
                        out=Q2[:, s0:s0 + Ls], in_=P[:, base + 2:base + 2 + Ls], func=AF.Copy, scale=cw[:, 2:3]),
                        reads=[pk, "cols"], writes=[qk2])
                    S.emit("pool", lambda E, s0=s0, Ls=Ls, Q2=Q2, Q3=Q3: E.tensor_tensor(
                        out=Q2[:, s0:s0 + Ls], in0=Q2[:, s0:s0 + Ls], in1=Q3[:, s0:s0 + Ls], op=ALU.add),
                        reads=[qk2, qk3], writes=[qk2])
                    S.emit("dve", lambda E, base=base, s0=s0, Ls=Ls, cw=cw, A=A: E.tensor_scalar(
                        out=A[:, s0:s0 + Ls], in0=P[:, base:base + Ls], scalar1=cw[:, 0:1], scalar2=None, op0=ALU.mult),
                        reads=[pk, "cols"], writes=[ak])
                    S.emit("dve", lambda E, base=base, s0=s0, Ls=Ls, cw=cw, A=A: E.scalar_tensor_tensor(
                        out=A[:, s0:s0 + Ls], in0=P[:, base + 1:base + 1 + Ls], scalar=cw[:, 1:2], in1=A[:, s0:s0 + Ls],
                        op0=ALU.mult, op1=ALU.add), reads=[pk, ak, "cols"], writes=[ak])
                    S.emit("dve", lambda E, s0=s0, Ls=Ls, A=A, Q2=Q2: E.tensor_tensor(
                        out=A[:, s0:s0 + Ls], in0=A[:, s0:s0 + Ls], in1=Q2[:, s0:s0 + Ls], op=ALU.add),
                        reads=[ak, qk2], writes=[ak])
                def silu(A=cacc[pb], ch=ch, ak=ak):
                    S.emit("act", lambda E: E.activation(out=xbcT[:, ch, 0:T], in_=A[:, 0:T], func=AF.Silu,
                                                         bias=cols[:, C_CB + ch:C_CB + ch + 1], scale=1.0),
                           reads=[ak, "cols"], writes=["xbcT"])
                if pend_silu:
                    pend_silu.pop()()
                pend_silu.append(silu)
            dense_ws("xbc", 8, 512, 512, T, lambda kc: aT[:, kc, 0:T], ["aT"], ev)
        if pend_silu:
            pend_silu.pop()()
        for b in range(4):
            view, wkey = WS.next("z", 8, 512)
            for (ci, c0, L, si) in chunk_list:
                bank = dbank()

                def f(E, c0=c0, L=L, bank=bank, view=view):
                    for kc in range(8):
                        ins = E.matmul(ps[bank][0:L, 0:512], lhsT=aT[:, kc, c0:c0 + L], rhs=view[:, kc, :], start=(kc == 0), stop=(kc == 7))
                    return ins
                S.emit("pe", f, reads=[wkey, "aT"], writes=[PK(bank)])
                S.emit("act", lambda E, L=L, ci=ci, bank=bank, b=b: E.activation(out=sz[0:L, ci, b * 512:(b + 1) * 512], in_=ps[bank][0:L, 0:512], func=AF.Silu),
                       reads=[PK(bank)], writes=["sz"])
            WS.done()
        ssd_tile(chunk_list)
        for b in range(4):
            def ev(oc, bank, b=b):
                S.emit("act", lambda E: E.copy(out=mT[:, b * 2 + oc, 0:T], in_=ps[bank][:, 0:T]), reads=[PK(bank)], writes=[("mT", b * 2 + oc)])
            dense_ws("wout", 16, 256, 256, T, lambda kc: ynT[:, kc, 0:T], ["ynT"], ev)
        postnorm_residual(T, ln_col(2))
        ffn(0, T, hidT1, next_prenorm)
        S.emit("pool", lambda E, hT_=H[0]: E.dma_start(out=hs[hs_idx].rearrange("p (k t) -> p k t", k=8)[:, :, 0:T], in_=hT_[:, :, 0:T]),
               reads=H[1], writes=[("hs", hs_idx)], dma=True)

    for si in range(2):
        S.emit("pool", lambda E, si=si: E.dma_start(out=Sst[si][:], in_=ssmst_d[si]), writes=SKs(si), dma=True)
        S.emit("pool", lambda E, si=si: E.dma_start(out=tail[si][:].rearrange("p c k -> p (c k)"), in_=convst_d[si]), writes=[("tail", si)], dma=True)
        S.emit("act", lambda E, si=si: E.copy(out=Sbf[si][:], in_=Sst[si][:]), reads=SKs(si), writes=SBKs(si))

    def load1(n):
        buf, keys = HB[n % 2]
        if n == 0:
            S.emit("pool", lambda E: E.dma_start(out=buf[:, :, 0:2 * LS], in_=xsT.rearrange("k p t -> p k t")), writes=keys, dma=True)
        elif n <= NT:
            j = n - 1
            S.emit("pool", lambda E: E.dma_start(out=buf[:, :, :], in_=xT.rearrange("k p t -> p k t")[:, :, j * TT:(j + 1) * TT]),
                   writes=keys, dma=True)
    load1(0)
    use_buf(0)
    def prenorm_next(n, lncol):
        if n > NT:
            return None
        buf, keys = HB[n % 2]
        return lambda: norm_to(buf, keys, 8, D_MODEL, TT, ln_col(lncol), aT, "aT")
    pass1_tile(2 * LS, [(0, LS, 0, [(0, LS)]), (LS, LS, 1, [(LS, LS)])], lambda: load1(1), NT, False, prenorm_next(1, 0))
    for si in range(2):
        S.emit("pool", lambda E, si=si: E.dma_start(out=sssm_o[si], in_=Sst[si][:]), reads=SKs(si), writes=[("o_sssm", si)], dma=True)
        S.emit("pool", lambda E, si=si: E.dma_start(out=sconv_o[si], in_=tail[si][:].rearrange("p c k -> p (c k)")),
               reads=[("tail", si)], writes=[("o_sconv", si)], dma=True)
    S.emit("pool", lambda E: E.memset(Sst[0][:], 0.0), writes=SKs(0))
    S.emit("pool", lambda E: E.memset(Sbf[0][:], 0.0), writes=SBKs(0))
    S.emit("pool", lambda E: E.memset(tail[0][:], 0.0), writes=[("tail", 0)])
    for j in range(NT):
        use_buf(j + 1)
        pass1_tile(TT, [(0, TT, 0, [(0, 128), (128, 128)])], (lambda j=j: load1(j + 2)), j, True, prenorm_next(j + 2, 0))
        if j % 16 == 15 and j != NT - 1:
            S.new_epoch()
    S.emit("pool", lambda E: E.dma_start(out=pssm_o[:, :], in_=Sst[0][:]), reads=SKs(0), writes=["o_pssm"], dma=True)
    S.emit("pool", lambda E: E.dma_start(out=pconv_o[:, :], in_=tail[0][:].rearrange("p c k -> p (c k)")), reads=[("tail", 0)], writes=["o_pconv"], dma=True)

    S.new_epoch()
    replay_all()
    es1.close()

    es2 = ExitStack()

    def sb2(name, shape, dt=F32):
        return es2.enter_context(nc.sbuf_tensor("s2_" + name, list(shape), dt))

    NBLK = SEQ // 128
    latT = sb2("latT", [128, 2, SEQ], BF16)
    krT = sb2("krT", [32, SEQ], BF16)
    lat_tok = sb2("lat_tok", [128, NBLK, 256], BF16)
    if NBLK >= 63:
        flat = lat_tok[:].rearrange("p b c -> p (b c)")
        off = [20 * 256]

        def carve(n):
            v = flat[:, off[0]:off[0] + n]
            off[0] += n
            return v
        s_latT = [carve(2 * (PAST + LS)).rearrange("p (j t) -> p j t", j=2) for i in range(2)]
        s_krT = [carve(PAST + LS)[0:32, :] for i in range(2)]
        s_lat_tok = [carve(9 * 256).rearrange("p (b c) -> p b c", b=9) for i in range(2)]
        assert off[0] <= NBLK * 256
    else:
        s_latT = [sb2("s_latT%d" % i, [128, 2, PAST + LS], BF16) for i in range(2)]
        s_krT = [sb2("s_krT%d" % i, [32, PAST + LS], BF16) for i in range(2)]
        s_lat_tok = [sb2("s_lat_tok%d" % i, [128, 9, 256], BF16) for i in range(2)]
    wuk = sb2("wuk", [128, 8, 256], BF16)
    wuv = sb2("wuv", [128, 2, 16, 128], BF16)
    cqT = mT[:, 0:4, :]
    cqnT = sb2("cqnT", [128, 4, TT], BF16)
    latraw = sb2("latraw", [128, 2, TT])
    latn = sb2("latn", [128, 2, TT])
    krf = sb2("krf", [32, 3, TT])
    cst = sb2("cst", [32, 2, TT])
    qq = sb2("qq", [128, 24 * TT], BF16)
    qnT = qq[:, 0:8 * TT].rearrange("p (k t) -> p k t", k=8)
    qrT = qq[0:32, 8 * TT:24 * TT].rearrange("p (k t) -> p k t", k=16)
    hidT2 = qq[:, 0:22 * TT].rearrange("p (k t) -> p k t", k=22)
    qlT = [sb2("qlT%d" % i, [128, 2, 2, TT], BF16) for i in range(2)]
    pT = [sb2("pT%d" % i, [128, 2 * TT], BF16) for i in range(3)]
    osb = sb2("osb", [128, 3, 2 * TT])
    pacc = [sb2("pacc%d" % i, [128, 2 * TT]) for i in range(2)]
    olT = [sb2("olT%d" % i, [128, 2, 2, TT], BF16) for i in range(2)]
    oT = sb2("oT", [128, 8, TT], BF16)

    S.emit("pool", lambda E: E.dma_start(out=wuk[:].rearrange("p a c -> p (a c)"), in_=wuk_d[:, :]), writes=["wuk"], dma=True)
    S.emit("pool", lambda E: E.dma_start(out=wuv[:].rearrange("p a b c -> p (a b c)"), in_=wuv_d[:, :]), writes=["wuv"], dma=True)
    for si in range(2):
        S.emit("pool", lambda E, si=si: E.dma_start(out=s_lat_tok[si][:, 0:8, :], in_=clat_d[si].rearrange("b p c -> p b c")),
               writes=[("ltok", si)], dma=True)
        S.emit("pool", lambda E, si=si: E.dma_start(out=s_latT[si][:, :, 0:PAST], in_=clatT_d[si].rearrange("j p t -> p j t")),
               writes=[("latT", si)], dma=True)
        S.emit("pool", lambda E, si=si: E.dma_start(out=s_krT[si][:, 0:PAST], in_=ckrT_d[si]), writes=[("krT", si)], dma=True)

    def load2(n):
        if n > NT:
            return
        buf, keys = HB[n % 2]
        idx, T_ = (NT, 2 * LS) if n == 0 else (n - 1, TT)
        S.emit("pool", lambda E: E.dma_start(out=buf[:, :, 0:T_], in_=hs[idx].rearrange("p (k t) -> p k t", k=8)[:, :, 0:T_]),
               reads=[("hs", idx)], writes=keys, dma=True)

    def pass2_tile(T, segs, tile_n, cs_src, out_fn, final_fn, prenormed=False, next_prenorm=None):
        use_buf(tile_n)
        load2(tile_n + 1)
        S.emit("pool", lambda E: E.dma_start(out=cst[:, :, 0:T], in_=cs_src), writes=["cst"], dma=True)
        if not prenormed:
            norm_to(H[0], H[1], 8, D_MODEL, T, ln_col(1), aT, "aT")

        def ev(oc, bank):
            S.emit("act", lambda E: E.copy(out=cqT[:, oc, 0:T], in_=ps[bank][:, 0:T]), reads=[PK(bank)], writes=["cqT", ("mT", oc)])
        dense_ws("wqa", 8, 512, 512, T, lambda kc: aT[:, kc, 0:T], ["aT"], ev)
        view, wkey = WS.next("wkva", 8, 320)
        for oc in range(2):
            bank = dbank()

            def f(E, oc=oc, bank=bank, view=view):
                for kc in range(8):
                    ins = E.matmul(ps[bank][:, 0:T], lhsT=view[:, kc, oc * 128:(oc + 1) * 128], rhs=aT[:, kc, 0:T], start=(kc == 0), stop=(kc == 7))
                return ins
            S.emit("pe", f, reads=[wkey, "aT"], writes=[PK(bank)])
            S.emit("act", lambda E, oc=oc, bank=bank: E.copy(out=latraw[:, oc, 0:T], in_=ps[bank][:, 0:T]), reads=[PK(bank)], writes=["latraw"])
        bank = dbank()

        def f(E, bank=bank, view=view):
            for w in range(2):
                for kc in range(8):
                    ins = E.matmul(ps[bank][0:32, w * T:(w + 1) * T], lhsT=view[:, kc, 256 + 32 * w:288 + 32 * w], rhs=aT[:, kc, 0:T],
                                   start=(kc == 0), stop=(kc == 7))
            return ins
        S.emit("pe", f, reads=[wkey, "aT"], writes=[PK(bank)])
        WS.done()
        norm_to(cqT, ["cqT"] + [("mT", k) for k in range(4)], 4, 512, T, cols[:, C_QN:C_QN + 4], cqnT, "cqnT")
        S.emit("dve", lambda E, bank=bank: E.tensor_tensor(out=krf[:, 0, 0:T], in0=ps[bank][0:32, 0:T], in1=cst[:, 0, 0:T], op=ALU.mult),
               reads=[PK(bank), "cst"], writes=["krf0"])
        S.emit("dve", lambda E, bank=bank: E.tensor_tensor(out=krf[:, 1, 0:T], in0=ps[bank][0:32, T:2 * T], in1=cst[:, 1, 0:T], op=ALU.mult),
               reads=[PK(bank), "cst"], writes=["krf1"])
        S.emit("dve", lambda E: E.tensor_tensor(out=krf[:, 2, 0:T], in0=krf[:, 0, 0:T], in1=krf[:, 1, 0:T], op=ALU.add),
               reads=["krf0", "krf1"], writes=["krf2"])
        norm_to(latraw, ["latraw"], 2, 256, T, cols[:, C_KN:C_KN + 2], latn, "latn")
        out_fn()
        for sg in segs:
            c0, L, ctx, kp = sg["c0"], sg["L"], sg["ctx"], sg["kpos"]
            cl, ck, ct, ci = ctx
            S.emit("act", lambda E, c0=c0, L=L, cl=cl, kp=kp: E.copy(out=cl[:, :, kp:kp + L], in_=latn[:, :, c0:c0 + L]),
                   reads=["latn"], writes=[("latT", ci)])
            S.emit("act", lambda E, c0=c0, L=L, ck=ck, kp=kp: E.copy(out=ck[:, kp:kp + L], in_=krf[:, 2, c0:c0 + L]),
                   reads=["krf2"], writes=[("krT", ci)])
            nch = (L + 127) // 128
            for cc in range(nch):
                Lc = min(128, L - cc * 128)
                blk = (kp + cc * 128) // 128
                bank = dbank()

                def f(E, cl=cl, kp=kp, cc=cc, Lc=Lc, bank=bank):
                    for j in range(2):
                        ins = E.transpose(psb[bank][0:Lc, j * 128:(j + 1) * 128], cl[:, j, kp + cc * 128:kp + cc * 128 + Lc], identb[:, :])
                    return ins
                S.emit("pe", f, reads=[("latT", ci), "identb"], writes=[PK(bank)])
                S.emit("act", lambda E, ct=ct, blk=blk, Lc=Lc, bank=bank: E.copy(out=ct[0:Lc, blk, :], in_=psb[bank][0:Lc, 0:256]),
                       reads=[PK(bank)], writes=[("ltok", ci)])
        view, wkey = WS.next("wqb", 4, 1024)
        for i in range(8):
            bank = dbank()

            def f(E, i=i, bank=bank, view=view):
                for kc in range(4):
                    ins = E.matmul(ps[bank][:, 0:T], lhsT=view[:, kc, i * 128:(i + 1) * 128], rhs=cqnT[:, kc, 0:T], start=(kc == 0), stop=(kc == 3))
                return ins
            S.emit("pe", f, reads=[wkey, "cqnT"], writes=[PK(bank)])
            S.emit("act", lambda E, i=i, bank=bank: E.copy(out=qnT[:, i, 0:T], in_=ps[bank][:, 0:T]), reads=[PK(bank)], writes=["qnT"])
        WS.done()
        view, wkey = WS.next("wqb", 4, 1024)
        for h in range(16):
            bank = dbank()

            def f(E, h=h, bank=bank, view=view):
                for w in range(2):
                    for kc in range(4):
                        ins = E.matmul(ps[bank][0:32, w * T:(w + 1) * T], lhsT=view[:, kc, w * 512 + h * 32:w * 512 + h * 32 + 32],
                                       rhs=cqnT[:, kc, 0:T], start=(kc == 0), stop=(kc == 3))
                return ins
            S.emit("pe", f, reads=[wkey, "cqnT"], writes=[PK(bank)])
            S.emit("dve", lambda E, bank=bank: E.tensor_tensor(out=krf[:, 0, 0:T], in0=ps[bank][0:32, 0:T], in1=cst[:, 0, 0:T], op=ALU.mult),
                   reads=[PK(bank), "cst"], writes=["krf0"])
            S.emit("dve", lambda E, bank=bank: E.tensor_tensor(out=krf[:, 1, 0:T], in0=ps[bank][0:32, T:2 * T], in1=cst[:, 1, 0:T], op=ALU.mult),
                   reads=[PK(bank), "cst"], writes=["krf1"])
            S.emit("dve", lambda E, h=h: E.tensor_tensor(out=qrT[:, h, 0:T], in0=krf[:, 0, 0:T], in1=krf[:, 1, 0:T], op=ALU.add),
                   reads=["krf0", "krf1"], writes=["qrT"])
        WS.done()
        def emit_qlat(i):
            QL = qlT[i % 2]
            qk = ("qlT", i % 2)
            for e in range(2):
                for j in range(2):
                    bank = dbank()
                    S.emit("pe", lambda E, e=e, j=j, bank=bank, i=i: E.matmul(
                        ps[bank][:, 0:T], lhsT=wuk[64 * e:64 * e + 64, i, j * 128:(j + 1) * 128], rhs=qnT[64 * e:64 * e + 64, i, 0:T],
                        start=True, stop=True), reads=["wuk", "qnT"], writes=[PK(bank)])
                    S.emit("act", lambda E, e=e, j=j, bank=bank, QL=QL: E.copy(out=QL[:, j, e, 0:T], in_=ps[bank][:, 0:T]),
                           reads=[PK(bank)], writes=[qk])

        def emit_wuv(i):
            OL = olT[i % 2]
            ok = ("olT", i % 2)
            bank = dbank()

            def f(E, i=i, bank=bank, OL=OL):
                n = 0
                for e in range(2):
                    for j in range(2):
                        ins = E.matmul(ps[bank][:, 0:T], lhsT=wuv[:, j, 2 * i + e, :], rhs=OL[:, j, e, 0:T], start=(n == 0), stop=(n == 3))
                        n += 1
                return ins
            S.emit("pe", f, reads=["wuv", ok], writes=[PK(bank)])
            S.emit("act", lambda E, i=i, bank=bank: E.copy(out=oT[:, i, 0:T], in_=ps[bank][:, 0:T]), reads=[PK(bank)], writes=["oT"])

        pti = [0]
        pai = [0]
        pend_fin = []
        emit_qlat(0)
        for i in range(8):
            QL = qlT[i % 2]
            OL = olT[i % 2]
            qk, ok = ("qlT", i % 2), ("olT", i % 2)
            if i + 1 < 8:
                emit_qlat(i + 1)
            for sgi, sg in enumerate(segs):
                c0, L, ctx = sg["c0"], sg["L"], sg["ctx"]
                cl, ck, ct, ci = ctx
                blocks = sg["blocks"]
                nb = len(blocks)
                slots = []
                PA = pacc[pai[0] % 2]
                pak = ("pacc", pai[0] % 2)
                pai[0] += 1

                def emit_qk(bi):
                    blk, key0, nk, qa, diag = blocks[bi]
                    sbank = 3 + (pti[0] % 2)
                    P = pT[pti[0] % 3]
                    pk = ("pT", pti[0] % 3)
                    pti[0] += 1
                    n = L - qa
                    qs = slice(c0 + qa, c0 + L)
                    sv = ps[sbank][0:nk, 0:2 * n].rearrange("p (e q) -> p e q", e=2)

                    def f(E, cl=cl, ck=ck, QL=QL, i=i):
                        E.matmul(sv, lhsT=cl[:, 0, key0:key0 + nk], rhs=QL[:, 0, :, qs], start=True, stop=False)
                        E.matmul(sv, lhsT=cl[:, 1, key0:key0 + nk], rhs=QL[:, 1, :, qs], start=False, stop=False)
                        return E.matmul(sv, lhsT=ck[:, key0:key0 + nk], rhs=qrT[:, 2 * i:2 * i + 2, qs], start=False, stop=True)
                    S.emit("pe", f, reads=[("latT", ci), ("krT", ci), qk, "qrT"], writes=[PK(sbank)])
                    S.emit("act", lambda E: E.activation(out=P[0:nk, 0:2 * n], in_=ps[sbank][0:nk, 0:2 * n], func=AF.Exp, scale=SCALE),
                           reads=[PK(sbank)], writes=[pk])
                    if diag:
                        S.emit("pool", lambda E: E.memset(P[64:128, 0:2 * n].rearrange("p (e q) -> p e q", e=2)[:, :, 0:64], 0.0),
                               reads=[pk], writes=[pk])
                    slots.append((P, pk, n))

                def emit_pv(bi):
                    blk, key0, nk, qa, diag = blocks[bi]
                    P, pk, n = slots[bi]
                    first, last = (bi == 0), (bi == nb - 1)
                    pv = P[0:nk, 0:2 * n].rearrange("p (e q) -> p e q", e=2)

                    def f(E, ct=ct, L=L):
                        o0 = ps[5][:, 0:2 * L].rearrange("p (e q) -> p e q", e=2)[:, :, qa:qa + n]
                        o1 = ps[6][:, 0:2 * L].rearrange("p (e q) -> p e q", e=2)[:, :, qa:qa + n]
                        E.matmul(o0, lhsT=ct[0:nk, blk, 0:128], rhs=pv, start=first, stop=last, skip_group_check=True)
                        return E.matmul(o1, lhsT=ct[0:nk, blk, 128:256], rhs=pv, start=first, stop=last, skip_group_check=True)
                    S.emit("pe", f, reads=[("ltok", ci), pk], writes=[PK(5), PK(6)])
                    pav = PA[0:nk, 0:2 * L].rearrange("p (e q) -> p e q", e=2)[:, :, qa:qa + n]
                    if first:
                        S.emit("dve", lambda E, pav=pav, pv=pv: E.tensor_copy(out=pav, in_=pv), reads=[pk], writes=[pak])
                    else:
                        S.emit("dve", lambda E, pav=pav, pv=pv: E.tensor_tensor(out=pav, in0=pav, in1=pv, op=ALU.add), reads=[pk, pak], writes=[pak])

                emit_qk(0)
                if nb > 1:
                    emit_qk(1)
                if pend_fin:
                    pend_fin.pop()()
                for bi in range(nb):
                    if bi >= 1 and bi + 1 < nb:
                        emit_qk(bi + 1)
                    emit_pv(bi)
                    if bi == min(2, nb - 1) and sgi == 0 and i > 0:
                        emit_wuv(i - 1)
                S.emit("act", lambda E, L=L: E.copy(out=osb[:, 0, 0:2 * L], in_=ps[5][:, 0:2 * L]), reads=[PK(5)], writes=["osb0"])
                S.emit("dve", lambda E, L=L: E.tensor_copy(out=osb[:, 1, 0:2 * L], in_=ps[6][:, 0:2 * L]), reads=[PK(6)], writes=["osb1"])

                def fin(L=L, c0=c0, OL=OL, PA=PA, pak=pak, ok=ok):
                    S.emit("pe", lambda E: E.matmul(ps[7][:, 0:2 * L], lhsT=onesf[:, :], rhs=PA[:, 0:2 * L], start=True, stop=True),
                           reads=[pak, "consts"], writes=[PK(7)])
                    S.emit("act", lambda E: E.copy(out=osb[:, 2, 0:2 * L], in_=ps[7][:, 0:2 * L]), reads=[PK(7)], writes=["osb2"])
                    S.emit("dve", lambda E: E.reciprocal(out=osb[:, 2, 0:2 * L], in_=osb[:, 2, 0:2 * L]), reads=["osb2"], writes=["osb2"])
                    for j in range(2):
                        S.emit("dve" if j == 0 else "pool", lambda E, j=j: E.tensor_tensor(
                            out=OL[:, j, :, c0:c0 + L], in0=osb[:, j, 0:2 * L].rearrange("p (e q) -> p e q", e=2),
                            in1=osb[:, 2, 0:2 * L].rearrange("p (e q) -> p e q", e=2), op=ALU.mult),
                            reads=["osb%d" % j, "osb2"], writes=[ok])
                pend_fin.append(fin)
        if pend_fin:
            pend_fin.pop()()
        emit_wuv(7)
        for b in range(2):
            def ev(oc, bank, b=b):
                S.emit("act", lambda E: E.copy(out=mT[:, b * 4 + oc, 0:T], in_=ps[bank][:, 0:T]), reads=[PK(bank)], writes=[("mT", b * 4 + oc)])
            dense_ws("wo", 8, 512, 512, T, lambda kc: oT[:, kc, 0:T], ["oT"], ev)
        postnorm_residual(T, ln_col(3))
        ffn(1, T, hidT2, next_prenorm)
        final_fn()

    def out_sample():
        S.emit("pool", lambda E: E.dma_start(out=slatT_o.rearrange("j p t -> p j t"), in_=latn[:, :, 0:2 * LS]), reads=["latn"], writes=["o_slat"], dma=True)
        S.emit("pool", lambda E: E.dma_start(out=skrT_o[:, :], in_=krf[:, 2, 0:2 * LS]), reads=["krf2"], writes=["o_skr"], dma=True)

    def fin_sample():
        S.emit("pool", lambda E, hT_=H[0]: E.dma_start(out=ysT_o.rearrange("k p t -> p k t"), in_=hT_[:, :, 0:2 * LS]), reads=H[1], writes=["o_ys"], dma=True)
    segs = []
    for si in range(2):
        blocks = [(b, b * 128, 128, 0, False) for b in range(8)] + [(8, PAST, LS, 0, False)]
        segs.append(dict(c0=si * LS, L=LS, ctx=(s_latT[si], s_krT[si], s_lat_tok[si], si), kpos=PAST, blocks=blocks))
    load2(0)
    pass2_tile(2 * LS, segs, 0, css_d[:, :, :], out_sample, fin_sample, False, prenorm_next(1, 1))

    for j in range(NT):
        cols_j = slice(j * TT, (j + 1) * TT)

        def out_p(cols_j=cols_j):
            S.emit("pool", lambda E: E.dma_start(out=platT_o.rearrange("j p t -> p j t")[:, :, cols_j], in_=latn[:, :, :]),
                   reads=["latn"], writes=[("o_plat", cols_j.start)], dma=True)
            S.emit("pool", lambda E: E.dma_start(out=pkrT_o[:, cols_j], in_=krf[:, 2, :]), reads=["krf2"], writes=[("o_pkr", cols_j.start)], dma=True)

        def fin_p(cols_j=cols_j):
            S.emit("pool", lambda E, hT_=H[0]: E.dma_start(out=yT_o.rearrange("k p t -> p k t")[:, :, cols_j], in_=hT_[:, :, :]),
                   reads=H[1], writes=[("o_y", cols_j.start)], dma=True)
        blocks = [(b, b * 128, 128, 0, False) for b in range(2 * j)]
        blocks += [(2 * j, 2 * j * 128, 128, 0, True), (2 * j + 1, (2 * j + 1) * 128, 128, 128, True)]
        segs = [dict(c0=0, L=TT, ctx=(latT, krT, lat_tok, 2), kpos=j * TT, blocks=blocks)]
        pass2_tile(TT, segs, j + 1, cs_d[:, :, cols_j], out_p, fin_p, True, prenorm_next(j + 2, 1))
        if j % 4 == 3 and j != NT - 1:
            S.new_epoch()

    S.barrier()
    replay_all()
    es2.close()
    es.close()
    return nc


def _blk(W, KC, c0, NB):
    K, N = W.shape
    nv = min(NB, N - c0)
    out = np.zeros((128, KC, NB), np.float32)
    out[:, :, :nv] = W[:, c0:c0 + nv].reshape(KC, 128, nv).transpose(1, 0, 2)
    res = np.zeros((128, WB), np.float32)
    res[:, :KC * NB] = out.reshape(128, KC * NB)
    return res


def _colvec(v):
    n = v.shape[0] // 128
    return np.ascontiguousarray(v.reshape(n, 128).T)


def _prep_shared(inp, SEQ):
    f = np.float32
    w_in = inp["ssd_w_in"][0]
    perm = (np.arange(32) + 16) % 32
    blocks0 = []
    for b in range(8):
        blocks0.append(_blk(w_in[:, 2048:6144], 8, b * 512, 512))
    for b in range(4):
        blocks0.append(_blk(w_in[:, 0:2048], 8, b * 512, 512))
    for b in range(4):
        blocks0.append(_blk(inp["ssd_w_out"][0], 16, b * 256, 256))

    def ffn_blocks(l):
        r = []
        for b in range(6):
            r.append(_blk(inp["ffn_w_gate"][l], 8, b * 512, 512))
        for b in range(6):
            r.append(_blk(inp["ffn_w_up"][l], 8, b * 512, 512))
        for b in range(8):
            r.append(_blk(inp["ffn_w_down"][l], 22, b * 128, 128))
        return r
    blocks0 += ffn_blocks(0)
    blocks1 = [_blk(inp["mla_wq_a"][0], 8, 0, 512)]
    wkv = inp["mla_wkv_a"][0]
    wkv2 = np.concatenate([wkv[:, :256], wkv[:, 256:288], wkv[:, 256:288][:, perm]], axis=1)
    blocks1.append(_blk(wkv2, 8, 0, 320))
    wqb = inp["mla_wq_b"][0].reshape(512, 16, 96)
    blocks1.append(_blk(np.ascontiguousarray(wqb[:, :, :64]).reshape(512, 1024), 4, 0, 1024))
    rope = wqb[:, :, 64:]
    blocks1.append(_blk(np.concatenate([rope.reshape(512, 512), rope[:, :, perm].reshape(512, 512)], axis=1), 4, 0, 1024))
    for b in range(2):
        blocks1.append(_blk(inp["mla_w_o"][0], 8, b * 512, 512))
    blocks1 += ffn_blocks(1)
    assert len(blocks0) == NB0 and len(blocks1) == NB1
    w0 = np.concatenate(blocks0, axis=0)
    w1 = np.concatenate(blocks1, axis=0)

    cols = np.zeros((128, NCOLS), f)
    lns = [inp["ln_mix_pre"][0], inp["ln_mix_pre"][1], inp["ln_mix_post"][0], inp["ln_mix_post"][1],
           inp["ln_ffn_pre"][0], inp["ln_ffn_pre"][1], inp["ln_ffn_post"][0], inp["ln_ffn_post"][1]]
    for i, v in enumerate(lns):
        cols[:, C_LN + 8 * i:C_LN + 8 * i + 8] = _colvec(v)
    cw = inp["ssd_conv_w"][0]
    cols[:, C_CW:C_CW + 128] = cw.T.reshape(32, 128, 4).transpose(1, 0, 2).reshape(128, 128)
    cols[:, C_CB:C_CB + 32] = _colvec(inp["ssd_conv_b"][0])
    cols[:, C_QN:C_QN + 4] = _colvec(inp["mla_q_norm"][0])
    cols[:, C_KN:C_KN + 2] = _colvec(inp["mla_kv_norm"][0])
    rows = np.zeros((128, NROWS), f)
    rows[:, R_DTB:R_DTB + 32] = inp["ssd_dt_bias"][0][None, :]
    rows[:, R_ALOG:R_ALOG + 32] = inp["ssd_a_log"][0][None, :]
    rows[:, R_D:R_D + 32] = inp["ssd_d"][0][None, :]
    rows[:, R_GN:R_GN + 2048] = inp["ssd_gate_norm"][0][None, :]
    wdt = np.ascontiguousarray(w_in[:, 6144:6176].reshape(8, 128, 32).transpose(1, 0, 2)).reshape(128, 256)
    wuk_src = inp["mla_w_uk"][0]
    wuk = wuk_src.transpose(1, 2, 0).reshape(8, 2, 64, 256).transpose(1, 2, 0, 3).reshape(128, 8 * 256)
    wuv_src = inp["mla_w_uv"][0]
    wuv = np.zeros((128, 2, 16, 128), f)
    for h in range(16):
        for j in range(2):
            wuv[:, j, h, (h % 2) * 64:(h % 2) * 64 + 64] = wuv_src[j * 128:(j + 1) * 128, h, :]
    wuv = wuv.reshape(128, 2 * 16 * 128)

    def rope_tab(pos):
        half = 16
        inv = (np.float32(10000.0) ** (-np.arange(half, dtype=np.float32) / np.float32(half))).astype(np.float32)
        ang = pos.astype(np.float32)[:, None] * inv[None, :]
        c, s_ = np.cos(ang).astype(np.float32), np.sin(ang).astype(np.float32)
        tab = np.zeros((32, 2, pos.shape[0]), np.float32)
        tab[:16, 0] = c.T
        tab[16:, 0] = c.T
        tab[:16, 1] = -s_.T
        tab[16:, 1] = s_.T
        return tab
    cs = rope_tab(np.arange(SEQ))
    c1 = rope_tab(PAST + np.arange(LS))
    css = np.concatenate([c1, c1], axis=2)
    consts = np.zeros((128, 512), f)
    i = np.arange(128)
    consts[:, 0:128] = (i[:, None] == i[None, :])
    consts[:, 128:256] = (i[:, None] <= i[None, :])
    consts[:, 256:384] = (i[:, None] > i[None, :])
    consts[:, 384:512] = 1.0
    return dict(w0=w0, w1=w1, cols=cols, rows=rows, wdt=wdt, wuk=np.ascontiguousarray(wuk), wuv=wuv, cs=cs, css=css, consts=consts)


_NT_OVERRIDE = [None]


def kernel(**inputs):
    inp = {k: np.asarray(v) for k, v in inputs.items()}
    B = inp["x_prompt"].shape[0]
    SEQ = inp["x_prompt"].shape[1]
    NT = SEQ // TT
    shared = _prep_shared(inp, SEQ)
    in_maps = []
    for b in range(B):
        m = dict(shared)
        m["xT"] = np.ascontiguousarray(inp["x_prompt"][b].T).reshape(8, 128, SEQ)
        xs = inp["x_sample"][2 * b:2 * b + 2].reshape(2 * LS, D_MODEL)
        m["xsT"] = np.ascontiguousarray(xs.T).reshape(8, 128, 2 * LS)
        cst = inp["state_ssd_conv"][0, 2 * b:2 * b + 2]
        m["convst"] = np.ascontiguousarray(cst.transpose(0, 2, 1).reshape(2, 32, 128, 3).transpose(0, 2, 1, 3)).reshape(2, 128, 96)
        sst = inp["state_ssd_ssm"][0, 2 * b:2 * b + 2]
        m["ssmst"] = np.ascontiguousarray(sst.transpose(0, 3, 1, 2)).reshape(2, 128, 2048)
        cl = inp["cache_mla_latent"][0, 2 * b:2 * b + 2]
        m["clat"] = np.ascontiguousarray(cl).reshape(2, 8, 128, 256)
        m["clatT"] = np.ascontiguousarray(cl.transpose(0, 2, 1)).reshape(2, 2, 128, PAST)
        ck = inp["cache_mla_krope"][0, 2 * b:2 * b + 2]
        m["ckrT"] = np.ascontiguousarray(ck.transpose(0, 2, 1))
        in_maps.append(m)
    nc = build_program(NT)
    res = run_bass_kernel_spmd(nc, in_maps, core_ids=list(range(B)))
    f = np.float32
    y_prompt = np.zeros((B, SEQ, D_MODEL), f)
    y_sample = np.zeros((2 * B, LS, D_MODEL), f)
    p_conv = np.zeros((1, B, 3, 4096), f)
    p_ssm = np.zeros((1, B, 32, 64, 128), f)
    p_lat = np.zeros((1, B, SEQ, 256), f)
    p_kr = np.zeros((1, B, SEQ, 32), f)
    s_conv = np.zeros((1, 2 * B, 3, 4096), f)
    s_ssm = np.zeros((1, 2 * B, 32, 64, 128), f)
    s_lat = np.zeros((1, 2 * B, LS, 256), f)
    s_kr = np.zeros((1, 2 * B, LS, 32), f)
    for b in range(B):
        r = res.results[b]
        y_prompt[b] = r["yT"].reshape(1024, SEQ).T
        ys = r["ysT"].reshape(1024, 2 * LS).T
        p_conv[0, b] = r["pconv"].reshape(128, 32, 3).transpose(2, 1, 0).reshape(3, 4096)
        p_ssm[0, b] = r["pssm"].T.reshape(32, 64, 128)
        p_lat[0, b] = r["platT"].reshape(256, SEQ).T
        p_kr[0, b] = r["pkrT"].T
        sl = r["slatT"].reshape(256, 2 * LS).T
        sk = r["skrT"].T
        for si in range(2):
            y_sample[2 * b + si] = ys[si * LS:(si + 1) * LS]
            s_conv[0, 2 * b + si] = r["sconv"][si].reshape(128, 32, 3).transpose(2, 1, 0).reshape(3, 4096)
            s_ssm[0, 2 * b + si] = r["sssm"][si].T.reshape(32, 64, 128)
            s_lat[0, 2 * b + si] = sl[si * LS:(si + 1) * LS]
            s_kr[0, 2 * b + si] = sk[si * LS:(si + 1) * LS]
    return (y_prompt, y_sample, p_conv, p_ssm, p_lat, p_kr, s_conv, s_ssm, s_lat, s_kr)
```
